# Optimizing a Trainium2 kernel written in Bass

```python
import jax
import jax.numpy as jnp
from jax import lax

D_MODEL = 1024
BATCH = 32
SEQ = 2048
DEPTH = 1
DEC_BATCH = 32
DEC_SEQ = 64
PAST_LEN = 1024

CHUNK = 64
N_HEADS = 16
HEAD_DIM = 64
D_ATTN = N_HEADS * HEAD_DIM
D_RNN = D_MODEL
RNN_BLOCKS = 16
RNN_BLOCK = D_RNN // RNN_BLOCKS
CONV_W = 4
RG_C = 8.0
D_FF = 4 * D_MODEL
Q_BLOCK = 128
EPS = 1e-6
_SPLITS = (D_ATTN, 2 * D_ATTN, 3 * D_ATTN, 3 * D_ATTN + N_HEADS,
           3 * D_ATTN + N_HEADS + D_RNN, 3 * D_ATTN + N_HEADS + 2 * D_RNN,
           3 * D_ATTN + N_HEADS + 2 * D_RNN + D_MODEL)
D_IN = 3 * D_ATTN + N_HEADS + 2 * D_RNN + 2 * D_MODEL

kernel_name = 'fox_rglru_gated_parallel_streaming_step'


def _rmsnorm(x, g):
    xf = x.astype(jnp.float32)
    xf = xf * lax.rsqrt(jnp.mean(xf * xf, axis=-1, keepdims=True) + EPS)
    return (xf * g.astype(jnp.float32)).astype(x.dtype)


def _fox_attention(q, k, v, c_q, c_k, q_pos, k_pos):
    b, t, h, dh = q.shape
    qb = min(t, Q_BLOCK)
    nb = t // qb
    scale = dh ** -0.5
    ck = jnp.transpose(c_k, (0, 2, 1))[:, :, None, :]

    def block(args):
        q_b, cq_b, pos_b = args
        s = jnp.einsum('bqhd,bkhd->bhqk', q_b, k, preferred_element_type=jnp.float32) * scale
        s = s + jnp.transpose(cq_b, (0, 2, 1))[..., None] - ck
        mask = k_pos[None, :] <= pos_b[:, None]
        s = jnp.where(mask[None, None], s, -jnp.inf)
        p = jax.nn.softmax(s, axis=-1)
        return jnp.einsum('bhqk,bkhd->bqhd', p.astype(v.dtype), v)

    q_blocks = jnp.transpose(q.reshape(b, nb, qb, h, dh), (1, 0, 2, 3, 4))
    cq_blocks = jnp.transpose(c_q.reshape(b, nb, qb, h), (1, 0, 2, 3))
    pos_blocks = q_pos.reshape(nb, qb)
    o = lax.map(block, (q_blocks, cq_blocks, pos_blocks))
    return jnp.transpose(o, (1, 0, 2, 3, 4)).reshape(b, t, h * dh)


def _linear_scan(a, bx, h0):
    def step(hc, ab):
        a_t, b_t = ab
        hc = a_t * hc + b_t
        return hc, hc
    h_last, hs = lax.scan(step, h0, (jnp.transpose(a, (1, 0, 2)), jnp.transpose(bx, (1, 0, 2))))
    return jnp.transpose(hs, (1, 0, 2)), h_last


def _layer(x, k_hist, v_hist, logf_hist, conv_hist, h0,
           norm_mix_g, w_in, b_f, conv_w, conv_b, w_rg_a, b_rg_a, w_rg_x, b_rg_x, rg_lambda,
           w_proj_attn, w_proj_rnn, w_out, norm_mlp_g, w_up, w_down):
    bsz, t, _ = x.shape
    past = k_hist.shape[1]
    hn = _rmsnorm(x, norm_mix_g)
    z = hn @ w_in
    q, k, v, f_logit, xr, yr, ga, gb = jnp.split(z, _SPLITS, axis=-1)

    q = q.reshape(bsz, t, N_HEADS, HEAD_DIM)
    k = k.reshape(bsz, t, N_HEADS, HEAD_DIM)
    v = v.reshape(bsz, t, N_HEADS, HEAD_DIM)
    logf = jax.nn.log_sigmoid(f_logit.astype(jnp.float32) + b_f.astype(jnp.float32))
    k_all = jnp.concatenate([k_hist.astype(k.dtype), k], axis=1)
    v_all = jnp.concatenate([v_hist.astype(v.dtype), v], axis=1)
    logf_all = jnp.concatenate([logf_hist.astype(jnp.float32), logf], axis=1)
    c_all = jnp.cumsum(logf_all, axis=1)
    k_pos = jnp.arange(past + t)
    q_pos = past + jnp.arange(t)
    o_attn = _fox_attention(q, k_all, v_all, c_all[:, past:], c_all, q_pos, k_pos)

    conv_in = jnp.concatenate([conv_hist.astype(xr.dtype), xr], axis=1)
    xc = conv_b + sum(conv_in[:, w:w + t] * conv_w[w] for w in range(CONV_W))
    new_conv = conv_in[:, t:]
    xb = xc.reshape(bsz, t, RNN_BLOCKS, RNN_BLOCK)
    r = jax.nn.sigmoid(jnp.einsum('btnd,nde->btne', xb, w_rg_a).reshape(bsz, t, D_RNN) + b_rg_a)
    i = jax.nn.sigmoid(jnp.einsum('btnd,nde->btne', xb, w_rg_x).reshape(bsz, t, D_RNN) + b_rg_x)
    log_a = -RG_C * r.astype(jnp.float32) * jax.nn.softplus(-rg_lambda.astype(jnp.float32))
    a = jnp.exp(log_a)
    bx = jnp.sqrt(-jnp.expm1(2.0 * log_a)) * (i * xc).astype(jnp.float32)
    hs, h_last = _linear_scan(a, bx, h0.astype(jnp.float32))
    o_rnn = hs.astype(x.dtype) * jax.nn.gelu(yr)

    y_a = o_attn @ w_proj_attn
    y_b = o_rnn @ w_proj_rnn
    merged = jax.nn.sigmoid(ga) * y_a + jax.nn.sigmoid(gb) * y_b
    x = x + merged @ w_out

    h2 = _rmsnorm(x, norm_mlp_g)
    x = x + jnp.square(jax.nn.relu(h2 @ w_up)) @ w_down
    return x, k, v, logf, new_conv, h_last


def setup_inputs(seed: int = 0) -> dict:
    key = jax.random.key(seed)
    ks = jax.random.split(key, 26)
    f32 = jnp.float32

    def nrm(k, shape, scale):
        return jax.random.normal(k, shape, f32) * scale

    u = jax.random.uniform(ks[16], (DEPTH, D_RNN), f32, 0.9, 0.999)
    a0 = u ** (1.0 / RG_C)
    return {
        'x_prompt': nrm(ks[0], (BATCH, SEQ, D_MODEL), 1.0),
        'x_sample': nrm(ks[1], (DEC_BATCH, DEC_SEQ, D_MODEL), 1.0),
        'cache_k': nrm(ks[2], (DEPTH, DEC_BATCH, PAST_LEN, N_HEADS, HEAD_DIM), 1.0),
        'cache_v': nrm(ks[3], (DEPTH, DEC_BATCH, PAST_LEN, N_HEADS, HEAD_DIM), 1.0),
        'cache_logf': jax.nn.log_sigmoid(2.5 + nrm(ks[4], (DEPTH, DEC_BATCH, PAST_LEN, N_HEADS), 1.0)),
        'state_conv': nrm(ks[5], (DEPTH, DEC_BATCH, CONV_W - 1, D_RNN), 1.0),
        'state_rglru': nrm(ks[6], (DEPTH, DEC_BATCH, D_RNN), 0.5),
        'norm_mix_g': 1.0 + nrm(ks[7], (DEPTH, D_MODEL), 0.05),
        'w_in': nrm(ks[8], (DEPTH, D_MODEL, D_IN), D_MODEL ** -0.5),
        'b_f': jax.random.uniform(ks[9], (DEPTH, N_HEADS), f32, 1.0, 4.0),
        'conv_w': nrm(ks[10], (DEPTH, CONV_W, D_RNN), CONV_W ** -0.5),
        'conv_b': nrm(ks[11], (DEPTH, D_RNN), 0.01),
        'w_rg_a': nrm(ks[12], (DEPTH, RNN_BLOCKS, RNN_BLOCK, RNN_BLOCK), RNN_BLOCK ** -0.5),
        'b_rg_a': nrm(ks[13], (DEPTH, D_RNN), 0.01),
        'w_rg_x': nrm(ks[14], (DEPTH, RNN_BLOCKS, RNN_BLOCK, RNN_BLOCK), RNN_BLOCK ** -0.5),
        'b_rg_x': nrm(ks[15], (DEPTH, D_RNN), 0.01),
        'rg_lambda': jnp.log(a0) - jnp.log1p(-a0),
        'w_proj_attn': nrm(ks[17], (DEPTH, D_ATTN, D_MODEL), D_ATTN ** -0.5),
        'w_proj_rnn': nrm(ks[18], (DEPTH, D_RNN, D_MODEL), D_RNN ** -0.5),
        'w_out': nrm(ks[19], (DEPTH, D_MODEL, D_MODEL), D_MODEL ** -0.5),
        'norm_mlp_g': 1.0 + nrm(ks[20], (DEPTH, D_MODEL), 0.05),
        'w_up': nrm(ks[21], (DEPTH, D_MODEL, D_FF), D_MODEL ** -0.5),
        'w_down': nrm(ks[22], (DEPTH, D_FF, D_MODEL), D_FF ** -0.5),
        'norm_final_g': 1.0 + nrm(ks[23], (D_MODEL,), 0.05),
    }


def reference(x_prompt, x_sample, cache_k, cache_v, cache_logf, state_conv, state_rglru,
              norm_mix_g, w_in, b_f, conv_w, conv_b, w_rg_a, b_rg_a, w_rg_x, b_rg_x, rg_lambda,
              w_proj_attn, w_proj_rnn, w_out, norm_mlp_g, w_up, w_down, norm_final_g):
    bp = x_prompt.shape[0]
    xp = x_prompt
    xs = x_sample
    kp, vp, lp, cp, hp = [], [], [], [], []
    ksm, vsm, lsm, csm, hsm = [], [], [], [], []
    for l in range(DEPTH):
        weights = (norm_mix_g[l], w_in[l], b_f[l], conv_w[l], conv_b[l], w_rg_a[l], b_rg_a[l],
                   w_rg_x[l], b_rg_x[l], rg_lambda[l], w_proj_attn[l], w_proj_rnn[l], w_out[l],
                   norm_mlp_g[l], w_up[l], w_down[l])
        xp, k1, v1, f1, c1, h1 = _layer(
            xp,
            jnp.zeros((bp, 0, N_HEADS, HEAD_DIM), xp.dtype),
            jnp.zeros((bp, 0, N_HEADS, HEAD_DIM), xp.dtype),
            jnp.zeros((bp, 0, N_HEADS), jnp.float32),
            jnp.zeros((bp, CONV_W - 1, D_RNN), xp.dtype),
            jnp.zeros((bp, D_RNN), jnp.float32),
            *weights)
        xs, k2, v2, f2, c2, h2 = _layer(
            xs, cache_k[l], cache_v[l], cache_logf[l], state_conv[l], state_rglru[l], *weights)
        kp.append(k1); vp.append(v1); lp.append(f1); cp.append(c1); hp.append(h1)
        ksm.append(k2); vsm.append(v2); lsm.append(f2); csm.append(c2); hsm.append(h2)
    y_prompt = _rmsnorm(xp, norm_final_g)
    y_sample = _rmsnorm(xs, norm_final_g)
    return (y_prompt, y_sample,
            jnp.stack(kp), jnp.stack(vp), jnp.stack(lp), jnp.stack(cp), jnp.stack(hp),
            jnp.stack(ksm), jnp.stack(vsm), jnp.stack(lsm), jnp.stack(csm), jnp.stack(hsm))
```

```python
import numpy as np
from contextlib import ExitStack
import concourse.bass as bass
import concourse.mybir as mybir
from concourse.bass_utils import run_bass_kernel_spmd

F32 = mybir.dt.float32
BF16 = mybir.dt.bfloat16
ALU = mybir.AluOpType
AF = mybir.ActivationFunctionType

D = 1024
NH = 16
DH = 64
NCH = 8
DFF = 4096
NFF = 32
DIN = 7184
C_Q, C_K, C_V, C_F, C_XR, C_YR, C_GA, C_GB = 0, 1024, 2048, 3072, 3088, 4112, 5136, 6160
EPS = 1e-6
GELU_C = 0.7978845608028654


class Tok:
    __slots__ = ("sem", "val", "eng")

    def __init__(self, sem, val, eng):
        self.sem, self.val, self.eng = sem, val, eng


class Buf:
    __slots__ = ("name", "w", "r", "excl")

    def __init__(self, name, excl=False):
        self.name, self.w, self.r, self.excl = name, None, [], excl


class Eng:
    def __init__(self, name, e, sem):
        self.name, self.e, self.sem, self.cnt, self.seen = name, e, sem, 0, {}


class K:
    def __init__(self, nc, es, ndma=40):
        self.nc = nc
        self.engs = {}
        for name, e in (("pe", nc.tensor), ("act", nc.scalar), ("dve", nc.vector), ("pool", nc.gpsimd), ("sp", nc.sync)):
            self.engs[name] = Eng(name, e, es.enter_context(nc.semaphore("sem_" + name)))
        self.dpools = {"sp": [[es.enter_context(nc.semaphore("dsp%d" % i)), 0] for i in range(28)],
                       "pool": [[es.enter_context(nc.semaphore("dpl%d" % i)), 0] for i in range(12)]}
        self.dsems = self.dpools["sp"] + self.dpools["pool"]
        self.dnext = {"sp": 0, "pool": 0}
        self.dma_toks = []

    def _wait(self, E, t, raw, is_dma=False):
        if t is None:
            return
        if t.eng is E and not is_dma:
            if E.name in ("pe", "sp"):
                return
            if not raw or t.val < E.cnt - 2:
                return
        key = id(t.sem)
        if E.seen.get(key, 0) >= t.val:
            return
        E.e.wait_ge(t.sem, t.val)
        E.seen[key] = t.val

    def _deps(self, E, reads, writes, is_dma=False):
        for b in reads:
            self._wait(E, b.w, True, is_dma)
            if b.excl:
                for t in b.r:
                    self._wait(E, t, False, is_dma)
        for b in writes:
            self._wait(E, b.w, False, is_dma)
            for t in b.r:
                self._wait(E, t, False, is_dma)

    def _commit(self, tok, reads, writes):
        for b in reads:
            b.r = [t for t in b.r if t.eng is not tok.eng or tok.eng is None] + [tok]
        for b in writes:
            b.w = tok
            b.r = []

    def op(self, en, fn, reads=(), writes=()):
        E = self.engs[en]
        self._deps(E, reads, writes)
        ins = fn(E.e)
        E.cnt += 1
        ins.then_inc(E.sem, 1)
        tok = Tok(E.sem, E.cnt, E)
        self._commit(tok, reads, writes)
        return tok

    def dma(self, en, out, in_, reads=(), writes=()):
        E = self.engs[en]
        self._deps(E, reads, writes, True)
        pl = self.dpools[en]
        slot = pl[self.dnext[en]]
        self.dnext[en] = (self.dnext[en] + 1) % len(pl)
        if slot[1] > 0:
            key = id(slot[0])
            if E.seen.get(key, 0) < slot[1]:
                E.e.wait_ge(slot[0], slot[1])
                E.seen[key] = slot[1]
        E.e.dma_start(out=out, in_=in_).then_inc(slot[0], 16)
        slot[1] += 16
        tok = Tok(slot[0], slot[1], None)
        self._commit(tok, reads, writes)
        self.dma_toks.append(tok)
        return tok

    def barrier(self):
        cur = [(E.sem, E.cnt) for E in self.engs.values() if E.cnt > 0]
        dm = [(s[0], s[1]) for s in self.dsems if s[1] > 0]
        for E in self.engs.values():
            for sem, val in cur + dm:
                if sem is E.sem:
                    continue
                key = id(sem)
                if E.seen.get(key, 0) < val:
                    E.e.wait_ge(sem, val)
                    E.seen[key] = val

    def finish(self):
        E = self.engs["sp"]
        for s in self.dsems:
            if s[1] > 0 and E.seen.get(id(s[0]), 0) < s[1]:
                E.e.wait_ge(s[0], s[1])
        for F in self.engs.values():
            if F is not E and F.cnt > 0:
                E.e.wait_ge(F.sem, F.cnt)


class Job:
    pass


class _Stop(Exception):
    pass


import os as _os
_STAGE = float(_os.environ.get('KSTAGE', '99'))


def ckpt(n):
    if n > _STAGE:
        raise _Stop()


def build(NPS, T, NSS, TS, HIST):
    nc = bass.Bass("TRN2", target_bir_lowering=False)
    dt = nc.dram_tensor

    def din(name, shape):
        return dt(name, shape, F32, kind="ExternalInput").ap()

    def dout(name, shape):
        return dt(name, shape, F32, kind="ExternalOutput").ap()

    xp = din("xp", [NPS, T, D]); xs = din("xs", [NSS, TS, D])
    ck = din("ck", [NSS, HIST, D]); cv = din("cv", [NSS, HIST, D]); clf = din("clf", [NSS, HIST, NH])
    sconv = din("sconv", [NSS, 3, D]); sh = din("sh", [NSS, D])
    g_mix = din("g_mix", [D]); w_in = din("w_in", [D, DIN]); b_f = din("b_f", [NH])
    conv_w = din("conv_w", [4, D]); conv_b = din("conv_b", [D])
    w_rg_a = din("w_rg_a", [16, 64, 64]); b_rg_a = din("b_rg_a", [D])
    w_rg_x = din("w_rg_x", [16, 64, 64]); b_rg_x = din("b_rg_x", [D])
    rg_lambda = din("rg_lambda", [D])
    w_pa = din("w_pa", [D, D]); w_pr = din("w_pr", [D, D]); w_out = din("w_out", [D, D])
    g_mlp = din("g_mlp", [D]); w_up = din("w_up", [D, DFF]); w_down = din("w_down", [DFF, D]); g_fin = din("g_fin", [D])

    yp = dout("yp", [NPS, T, D]); ys = dout("ys", [NSS, TS, D])
    kp = dout("kp", [NPS, T, D]); vp = dout("vp", [NPS, T, D]); lp = dout("lp", [NPS, T, NH])
    cp = dout("cp", [NPS, 3, D]); hp = dout("hp", [NPS, D])
    ks = dout("ks", [NSS, TS, D]); vs = dout("vs", [NSS, TS, D]); ls = dout("ls", [NSS, TS, NH])
    cs = dout("cs", [NSS, 3, D]); hs_o = dout("hs", [NSS, D])

    GR = {"q": C_Q, "k": C_K, "v": C_V, "xr": C_XR, "yr": C_YR, "ga": C_GA, "gb": C_GB}
    wst = {g: dt("wst_" + g, [NCH, 128, 8, 128], BF16, kind="Internal").ap() for g in GR}
    wst["pa"] = dt("wst_pa", [NCH, 128, 8, 128], BF16, kind="Internal").ap()
    wst["pr"] = dt("wst_pr", [NCH, 128, 8, 128], BF16, kind="Internal").ap()
    wst["up"] = dt("wst_up", [NFF, 128, 8, 128], BF16, kind="Internal").ap()
    wst["out"] = dt("wst_out", [8, 128, 1024], BF16, kind="Internal").ap()
    wst["down"] = dt("wst_down", [NFF, 128, 1024], BF16, kind="Internal").ap()

    jobs = []
    for s in range(NPS):
        j = Job(); j.T = T; j.HIST = 0; j.x = xp[s]; j.y = yp[s]; j.ko = kp[s]; j.vo = vp[s]; j.lo = lp[s]
        j.co = cp[s]; j.ho = hp[s]; jobs.append(j)
    for s in range(NSS):
        j = Job(); j.T = TS; j.HIST = HIST; j.x = xs[s]; j.y = ys[s]; j.ko = ks[s]; j.vo = vs[s]; j.lo = ls[s]
        j.co = cs[s]; j.ho = hs_o[s]; j.ck = ck[s]; j.cv = cv[s]; j.clf = clf[s]; j.sconv = sconv[s]; j.sh = sh[s]
        jobs.append(j)

    TMAX = max(T, TS + HIST)
    NKBMAX = max(T // 128, HIST // 128 + 1)
    NS = 11

    with ExitStack() as es:
        es.enter_context(nc.allow_non_contiguous_dma(reason="small strided parameter / state vectors"))
        sb = lambda name, shape, d=F32: es.enter_context(nc.sbuf_tensor(name, shape, d))
        ident_bf = sb("ident_bf", [128, 128], BF16); ident_f = sb("ident_f", [128, 128])
        mask_bf = sb("mask_bf", [128, 128], BF16); tri_f = sb("tri_f", [128, 128]); ones_f = sb("ones_f", [128, 128])
        gmix = sb("gmix", [128, 8]); gmlp = sb("gmlp", [128, 8]); convw = sb("convw", [128, 4, 8]); convb = sb("convb", [128, 8])
        scr = sb("scr", [128, 8]); hcl2 = sb("hcl2", [128, 8])
        hba = sb("hba", [128, 8]); hbx = sb("hbx", [128, 8]); hcl = sb("hcl", [128, 8]); lam = sb("lam", [128, 8])
        gf_bc = sb("gf_bc", [128, D]); bf_bc = sb("bf_bc", [128, NH])
        bd_a = sb("bd_a", [128, 8, 128], BF16); bd_x = sb("bd_x", [128, 8, 128], BF16); wf = sb("wf", [128, 8, NH], BF16)
        oattnT = sb("oattnT", [128, 8, T], BF16)
        wring = sb("wring", [128, NS, 1024], BF16)
        NSTG = 5
        stg = sb("stg", [128, NSTG, D])
        xn = sb("xn", [128, 2, D], BF16)
        junk = sb("junk", [128, D], BF16)
        ss = sb("ss", [128, 8]); rstd = sb("rstd", [128, 8])
        hstate = sb("hstate", [128, 8]); convhist = sb("convhist", [128, 8, 3])
        lf = sb("lf", [128, NKBMAX, NH]); zf = sb("zf", [128, NKBMAX, NH])
        csb = sb("csb", [128, NKBMAX, NH]); rsb = sb("rsb", [128, NKBMAX + 1, NH])
        ones_bf = sb("ones_bf", [128, 128], BF16); rsbb = sb("rsbb", [128, NKBMAX + 1, NH], BF16)
        rden = sb("rden", [128, 512]); rden_b = sb("rden_b", [128, 512])
        A_HN = 8 * T; A_VBF = NKBMAX * 8 * 192; A_QK = 2 * T + 2 * TMAX; A_PT = 4 * 512; A_KVF = 2 * 512 * 2
        att_elems = A_HN + A_VBF + A_QK + A_PT + A_KVF
        TT0 = min(512, T)
        USED_T = (0, 1, 2, 3, 5, 6, 8, 10)
        tail_elems = 8 * TT0 + 2 * 4 * D + 2 * len(USED_T) * 520 + 8 * TT0 + 8 * TT0 + 8 * TT0 + 32 * TT0
        arena = sb("arena", [128, max(att_elems, tail_elems)], BF16)
        off = [0]

        def carve(n, d=BF16):
            a = arena[:, off[0]:off[0] + n]
            off[0] += n
            return a if d == BF16 else a.bitcast(F32)

        hnT = carve(A_HN).rearrange("p (c t) -> p c t", c=8)
        vbf = carve(A_VBF).rearrange("p (b j s) -> p b j s", j=8, s=192)
        qT = [carve(T), carve(T)]; kT = [carve(TMAX), carve(TMAX)]
        pT = [carve(512) for _ in range(4)]
        kvf = [carve(1024, F32), carve(1024, F32)]
        off[0] = 0
        hnt = carve(8 * TT0).rearrange("p (c t) -> p c t", c=8)
        x1 = carve(2 * 4 * D, F32).rearrange("p (b n) -> p b n", b=4)
        rt = [carve(2 * 520, F32) if i in USED_T else None for i in range(11)]
        h2T = carve(8 * TT0).rearrange("p (c t) -> p c t", c=8)
        mergedT = carve(8 * TT0).rearrange("p (c t) -> p c t", c=8)
        ornnT = carve(8 * TT0).rearrange("p (c t) -> p c t", c=8)
        aT_off = off[0]
        aT = carve(32 * TT0).rearrange("p (f t) -> p f t", f=32)
        off[0] = aT_off
        rt1 = [carve(2 * 520, F32) if i in USED_T else None for i in range(11)] if 32 * TT0 >= 11 * 1040 else None
        PB = [es.enter_context(nc.psum_tensor("pb%d" % i, [128, 512], F32)) for i in range(8)]
        PBb = [Buf("pb%d" % i, excl=True) for i in range(8)]

        if _os.environ.get('KDEBUG'):
            print('SBUF bytes remaining', nc.sbuf_bytes_remaining, 'att_elems', att_elems, 'tail_elems', tail_elems)
        k = K(nc, es)
        castsems = [es.enter_context(nc.semaphore("castsem%d" % i)) for i in range(5)]
        cast_cnt = [0, 0, 0, 0, 0]
        _GRP = {"xr": 2, "yr": 2, "pa": 2, "pr": 2, "ga": 2, "gb": 2, "out": 3, "up": 3, "down": 4}
        grp_of = lambda key: (0 if key[1] < 2 else 1) if key[0] in ("q", "k", "v") else _GRP[key[0]]
        block = es.enter_context(nc.Block())
        pe, act, dve, pool, sp = nc.tensor, nc.scalar, nc.vector, nc.gpsimd, nc.sync

        ncast = 0
        cast_done = {}

        def cast(key, out, in_):
            g_ = grp_of(key)
            pool.dma_start(out=out, in_=in_).then_inc(castsems[g_], 16)
            cast_cnt[g_] += 16

        def cast_in(g):
            for c in range(NCH):
                cast((g, c), wst[g][c], w_in[:, GR[g] + c * 128:GR[g] + (c + 1) * 128].rearrange("(kc p) m -> p kc m", p=128))

        cB = {n: Buf(n) for n in ("ident", "vec", "bd", "wf")}
        k.op("pool", lambda e: e.memset(ident_bf[:], 1.0), writes=[cB["ident"]])
        k.op("pool", lambda e: e.affine_select(out=ident_bf[:], in_=ident_bf[:], pattern=[[-1, 128]], compare_op=ALU.is_equal, fill=0.0, base=0, channel_multiplier=1), reads=[cB["ident"]], writes=[cB["ident"]])
        k.op("pool", lambda e: e.memset(ident_f[:], 1.0), writes=[cB["ident"]])
        k.op("pool", lambda e: e.affine_select(out=ident_f[:], in_=ident_f[:], pattern=[[-1, 128]], compare_op=ALU.is_equal, fill=0.0, base=0, channel_multiplier=1), reads=[cB["ident"]], writes=[cB["ident"]])
        k.op("pool", lambda e: e.memset(mask_bf[:], 1.0), writes=[cB["ident"]])
        k.op("pool", lambda e: e.affine_select(out=mask_bf[:], in_=mask_bf[:], pattern=[[1, 128]], compare_op=ALU.is_ge, fill=0.0, base=0, channel_multiplier=-1), reads=[cB["ident"]], writes=[cB["ident"]])
        k.op("pool", lambda e: e.memset(tri_f[:], 1.0), writes=[cB["ident"]])
        k.op("pool", lambda e: e.affine_select(out=tri_f[:], in_=tri_f[:], pattern=[[1, 128]], compare_op=ALU.is_ge, fill=0.0, base=0, channel_multiplier=-1), reads=[cB["ident"]], writes=[cB["ident"]])
        k.op("pool", lambda e: e.memset(ones_f[:], 1.0), writes=[cB["ident"]])
        k.op("pool", lambda e: e.memset(ones_bf[:], 1.0), writes=[cB["ident"]])
        k.op("pool", lambda e: e.memset(bd_a[:], 0.0), writes=[cB["bd"]])
        k.op("pool", lambda e: e.memset(bd_x[:], 0.0), writes=[cB["bd"]])
        for (bd, wsrc) in ((bd_a, w_rg_a), (bd_x, w_rg_x)):
            wv = wsrc.rearrange("(j two) d e -> two d j e", two=2)
            k.dma("pool", bd[0:64, :, 0:64], wv[0], writes=[cB["bd"]])
            k.dma("pool", bd[64:128, :, 64:128], wv[1], writes=[cB["bd"]])
        k.dma("pool", wf[:], w_in[:, C_F:C_F + NH].rearrange("(kc p) n -> p kc n", p=128), writes=[cB["wf"]])
        for c in range(NCH):
            for g in ("q", "k", "v"):
                cast((g, c), wst[g][c], w_in[:, GR[g] + c * 128:GR[g] + (c + 1) * 128].rearrange("(kc p) m -> p kc m", p=128))
        for g in ("xr", "yr", "pa", "pr", "ga", "gb"):
            if g in GR:
                cast_in(g)
            else:
                src = {"pa": w_pa, "pr": w_pr}[g]
                for c in range(NCH):
                    cast((g, c), wst[g][c], src[:, c * 128:(c + 1) * 128].rearrange("(kc p) m -> p kc m", p=128))
        for kc in range(8):
            cast(("out", kc), wst["out"][kc], w_out[kc * 128:(kc + 1) * 128, :])
        for f in range(NFF):
            cast(("up", f), wst["up"][f], w_up[:, f * 128:(f + 1) * 128].rearrange("(kc p) m -> p kc m", p=128))
        for f in range(NFF):
            cast(("down", f), wst["down"][f], w_down[f * 128:(f + 1) * 128, :])

        fm = lambda v: v.rearrange("(c p) -> p c", p=128)
        k.dma("sp", gmix[:], fm(g_mix), writes=[cB["vec"]])
        k.dma("sp", gmlp[:], fm(g_mlp), writes=[cB["vec"]])
        k.dma("sp", convw[:], conv_w.rearrange("w (c p) -> p w c", p=128), writes=[cB["vec"]])
        k.dma("sp", convb[:], fm(conv_b), writes=[cB["vec"]])
        k.dma("sp", hba[:], fm(b_rg_a), writes=[cB["vec"]])
        k.dma("sp", hbx[:], fm(b_rg_x), writes=[cB["vec"]])
        k.dma("sp", lam[:], fm(rg_lambda), writes=[cB["vec"]])
        k.dma("sp", gf_bc[:], g_fin.partition_broadcast(128), writes=[cB["vec"]])
        k.dma("sp", bf_bc[:], b_f.partition_broadcast(128), writes=[cB["vec"]])
        k.op("act", lambda e: e.activation(out=lam[:], in_=lam[:], func=AF.Exp, scale=-1.0), reads=[cB["vec"]], writes=[cB["vec"]])
        k.op("act", lambda e: e.activation(out=lam[:], in_=lam[:], func=AF.Ln, bias=1.0), reads=[cB["vec"]], writes=[cB["vec"]])
        k.op("dve", lambda e: e.tensor_scalar(out=hcl[:], in0=lam[:], scalar1=-4.0, scalar2=None, op0=ALU.mult), reads=[cB["vec"]], writes=[cB["vec"]])
        k.op("dve", lambda e: e.tensor_scalar(out=hba[:], in0=hba[:], scalar1=0.5, scalar2=None, op0=ALU.mult), reads=[cB["vec"]], writes=[cB["vec"]])
        k.op("dve", lambda e: e.tensor_scalar(out=hbx[:], in0=hbx[:], scalar1=0.5, scalar2=None, op0=ALU.mult), reads=[cB["vec"]], writes=[cB["vec"]])
        k.op("dve", lambda e: e.tensor_scalar(out=hcl2[:], in0=hcl[:], scalar1=2.0, scalar2=None, op0=ALU.mult), reads=[cB["vec"]], writes=[cB["vec"]])
        allc = list(cB.values())

        def mlp_events(BPT, has_next):
            steps = [("up", f) for f in range(NFF)]
            for p_ in range(2):
                steps += [("down", p_, f) for f in range(NFF)]
            ins = {}
            if has_next:
                for b in range(BPT):
                    ins.setdefault(3 * b, []).append(("prepA", b))
                    ins.setdefault(3 * b + 2, []).append(("prepB", b))
                for c in range(NCH):
                    ins.setdefault(13 + 10 * c, []).append(("rnnA", c))
                    ins.setdefault(18 + 10 * c, []).append(("rnnG", c))
            ev = []
            for i, st in enumerate(steps):
                ev += ins.get(i, [])
                ev.append(st)
            return ev

        order = []
        for jb in jobs:
            for j in range(NCH):
                order += [("q", j), ("k", j), ("v", j)]
            TTj = min(512, jb.T); NTj = jb.T // TTj; BPTj = TTj // min(128, jb.T)
            for c in range(NCH):
                order += [("xr", c), ("yr", c)]
            for tt in range(NTj):
                for c in range(NCH):
                    order += [("pa", c), ("ga", c), ("pr", c), ("gb", c)]
                order += [("out", kc) for kc in range(8)]
                for ev in mlp_events(BPTj, tt + 1 < NTj):
                    if ev[0] == "up":
                        order.append(("up", ev[1]))
                    elif ev[0] == "down":
                        order.append(("down", ev[2], ev[1]))
                    elif ev[0] == "rnnA":
                        order += [("xr", ev[1]), ("yr", ev[1])]
        ringB = [Buf("ring%d" % i) for i in range(NS)]
        rs = {"issued": 0, "next": 0, "unrel": 0}
        sp_seen_cast = [False] * 5

        def ring_issue(upto):
            while rs["issued"] < min(upto, len(order)):
                i = rs["issued"]
                flush_stores(lambda ent: ent[2] + DEFER <= i)
                key = order[i]
                g_ = grp_of(key)
                if not sp_seen_cast[g_]:
                    sp.wait_ge(castsems[g_], cast_cnt[g_])
                    sp_seen_cast[g_] = True
                slot = i % NS
                if key[0] == "down":
                    k.dma("sp", wring[:, slot, 0:512], wst["down"][key[1]][:, key[2] * 512:(key[2] + 1) * 512], writes=[ringB[slot]])
                elif key[0] == "out":
                    k.dma("sp", wring[:, slot, :], wst["out"][key[1]], writes=[ringB[slot]])
                else:
                    k.dma("sp", wring[:, slot, :], wst[key[0]][key[1]].rearrange("p kc m -> p (kc m)"), writes=[ringB[slot]])
                rs["issued"] += 1

        def getw(key, hold=True):
            i = rs["next"]
            assert order[i] == key, (order[i], key, i)
            rs["next"] += 1
            if not hold:
                rs["unrel"] = rs["next"] - 1
            ring_issue(rs["unrel"] + NS)
            assert rs["issued"] > i
            slot = i % NS
            return wring[:, slot, :], ringB[slot]

        def release_all():
            rs["unrel"] = rs["next"]

        pTB = [Buf("pT%d" % i) for i in range(4)]
        rdenB = Buf("rden")
        rden2 = [rden, rden_b]; rdenB2 = [Buf("rden0"), Buf("rden1")]
        stgB = [Buf("stg%d" % i) for i in range(NSTG)]
        stg_i = [0]
        pend_st = []
        DEFER = 5

        def flush_stores(pred=None):
            keep = []
            for ent in pend_st:
                if pred is None or pred(ent):
                    ent[1]()
                else:
                    keep.append(ent)
            pend_st[:] = keep

        def alloc_stg():
            si = stg_i[0] % NSTG
            stg_i[0] += 1
            flush_stores(lambda ent: ent[0] == si)
            return si

        def defer_store(si, fn):
            pend_st.append((si, fn, rs["issued"]))
        xnB = [Buf("xn0"), Buf("xn1")]
        xn_i = [0]
        junkB = Buf("junk"); ssB = Buf("ss")
        ss_i = [0]

        def norm_rstd(src_ap, srcB, np_):
            col = ss_i[0] % 8
            ss_i[0] += 1
            k.op("act", lambda e: e.activation(out=junk[0:np_, :], in_=src_ap, func=AF.Square, accum_out=ss[0:np_, col:col + 1]), reads=[srcB], writes=[junkB, ssB])
            k.op("act", lambda e: e.activation(out=rstd[0:np_, col:col + 1], in_=ss[0:np_, col:col + 1], func=AF.Ln, scale=1.0 / D, bias=EPS), reads=[ssB], writes=[ssB])
            k.op("act", lambda e: e.activation(out=rstd[0:np_, col:col + 1], in_=rstd[0:np_, col:col + 1], func=AF.Exp, scale=-0.5), reads=[ssB], writes=[ssB])
            return rstd[0:np_, col:col + 1]

        def norm_transpose(src_ap, srcB, np_, gvec, dst, dstB, c0, bank, split=False):
            r = norm_rstd(src_ap, srcB, np_)
            xi = xn_i[0] % 2
            xn_i[0] += 1
            k.op("dve", lambda e: e.tensor_scalar(out=xn[0:np_, xi, :], in0=src_ap, scalar1=r, scalar2=None, op0=ALU.mult), reads=[srcB, ssB], writes=[xnB[xi]])
            pbv = PB[bank][:].bitcast(BF16).rearrange("p (c t) -> p c t", c=8)

            def tr(e):
                ins = None
                for c in range(8):
                    ins = e.transpose(out=pbv[:, c, 0:np_], in_=xn[0:np_, xi, c * 128:(c + 1) * 128], identity=ident_bf[0:np_, 0:np_])
                return ins
            def part2():
                k.op("pe", tr, reads=[xnB[xi]] + allc, writes=[PBb[bank]])
                k.op("dve", lambda e: e.tensor_tensor(out=dst[:, :, c0:c0 + np_], in0=pbv[:, :, 0:np_], in1=gvec[:, :].unsqueeze(2).broadcast_to([128, 8, np_]), op=ALU.mult), reads=[PBb[bank]] + allc, writes=[dstB])
            if split:
                return part2
            part2()

        def mm_group(bank_ap, wslot, rhs_fn, n):
            def f(e):
                ins = None
                for kc in range(8):
                    ins = e.matmul(bank_ap, lhsT=wslot[:, kc * 128:(kc + 1) * 128], rhs=rhs_fn(kc), start=(kc == 0), stop=(kc == 7))
                return ins
            return f

        rtB = [Buf("rt%d" % i) for i in range(11)]
        rtf = [r[:, 0:520] if r is not None else None for r in rt]
        if rt1 is None:
            rt1 = rt; rtB1 = rtB
        else:
            rtB1 = [Buf("rtb%d" % i) for i in range(11)]
        rtf1 = [r[:, 0:520] if r is not None else None for r in rt1]
        RT = [(rt, rtf, rtB), (rt1, rtf1, rtB1)]

        def tail(jb, oaB):
            Tn = jb.T; H = jb.HIST
            PBK = min(128, Tn); TT = min(512, Tn); NT = Tn // TT; BPT = TT // PBK
            stB = Buf("state")
            inB = [Buf("stin%d" % i) for i in range(4)]
            if H:
                k.dma("sp", hstate[:], jb.sh.rearrange("(c p) -> p c", p=128), writes=[inB[0]])
                for w_ in range(3):
                    k.dma("sp", convhist[:, :, w_], jb.sconv[w_].rearrange("(c p) -> p c", p=128), writes=[inB[1 + w_]])
            else:
                k.op("pool", lambda e: e.memset(hstate[:], 0.0), writes=[inB[0]])
                k.op("pool", lambda e: e.memset(convhist[:], 0.0), writes=[inB[1]])
            k.op("dve", lambda e: e.memset(scr[0:1, 2:3], 0.0), reads=inB, writes=[stB])
            cvB = [Buf("cv%d" % c) for c in range(NCH)]; hsB = [Buf("hs%d" % c) for c in range(NCH)]
            for b_ in cvB + hsB:
                b_.w = stB.w
            x1B = [Buf("x1_%d" % b) for b in range(BPT)]
            hntB = Buf("hnt"); h2B = Buf("h2T"); orB = Buf("ornnT"); mgB = Buf("mergedT"); aTB = Buf("aT")
            relu_t = [mergedT[:, 2 * i:2 * i + 2, 0:TT0].rearrange("p a t -> p (a t)").bitcast(F32) for i in range(4)]
            reluB = [Buf("relu%d" % i) for i in range(4)]
            rhs_h = lambda kc: hnt[:, kc, 0:TT]

            def prep_hn(tt, b, split=False):
                t0 = (tt * BPT + b) * PBK
                si = alloc_stg()
                k.dma("sp", stg[0:PBK, si, :], jb.x[t0:t0 + PBK, :], writes=[stgB[si]])
                return norm_transpose(stg[0:PBK, si, :], stgB[si], PBK, gmix, hnt, hntB, b * PBK, 4 + b % 2 if split else b % 2, split=split)

            def rnn_A(c, bk):
                wxr, wxrB = getw(("xr", c)); wyr, wyrB = getw(("yr", c))
                k.op("pe", mm_group(PB[bk[0]][:, 0:TT], wxr, rhs_h, TT), reads=[wxrB, hntB], writes=[PBb[bk[0]]])
                k.op("pe", mm_group(PB[bk[1]][:, 0:TT], wyr, rhs_h, TT), reads=[wyrB, hntB], writes=[PBb[bk[1]]])
                release_all()

            def rnn_S1a(c, bk, ts):
                bx_, by_, ba_, bi_ = bk
                rs_, rf_, rb_ = RT[ts]
                xrp, xc, y2_ = rf_[0], rf_[1], rf_[10]
                xcb = rs_[2].bitcast(BF16)[:, 0:TT]
                k.op("pool", lambda e: e.tensor_copy(out=xrp[:, 0:3], in_=convhist[:, c, :]), reads=[cvB[c]], writes=[rb_[0]])
                k.op("act", lambda e: e.copy(out=xrp[:, 3:3 + TT], in_=PB[bx_][:, 0:TT]), reads=[PBb[bx_]], writes=[rb_[0]])
                k.op("pool", lambda e: e.tensor_copy(out=convhist[:, c, :], in_=xrp[:, TT:TT + 3]), reads=[rb_[0]], writes=[cvB[c]])
                k.op("act", lambda e: e.activation(out=xc[:, 0:TT], in_=PB[bx_][:, 0:TT], func=AF.Identity, bias=convb[:, c:c + 1], scale=convw[:, 3, c:c + 1]), reads=[PBb[bx_]] + allc, writes=[rb_[1]])
                k.op("act", lambda e: e.activation(out=y2_[:, 0:TT], in_=PB[by_][:, 0:TT], func=AF.Square), reads=[PBb[by_]], writes=[rb_[10]])
                for w in (2, 1, 0):
                    k.op("dve", lambda e, w=w: e.scalar_tensor_tensor(out=xc[:, 0:TT], in0=xrp[:, w:w + TT], scalar=convw[:, w, c:c + 1], in1=xc[:, 0:TT], op0=ALU.mult, op1=ALU.add), reads=[rb_[0], rb_[1]], writes=[rb_[1]])
                k.op("pool", lambda e: e.tensor_copy(out=xcb, in_=xc[:, 0:TT]), reads=[rb_[1]], writes=[rb_[2]])

            def rnn_G(c, bk, ts):
                bx_, by_, ba_, bi_ = bk
                rs_, rf_, rb_ = RT[ts]
                y2_ = rf_[10]
                xcb = rs_[2].bitcast(BF16)[:, 0:TT]
                k.op("pe", lambda e: e.matmul(PB[ba_][:, 0:TT], lhsT=bd_a[:, c, :], rhs=xcb, start=True, stop=True), reads=[rb_[2]] + allc, writes=[PBb[ba_]])
                k.op("pe", lambda e: e.matmul(PB[bi_][:, 0:TT], lhsT=bd_x[:, c, :], rhs=xcb, start=True, stop=True), reads=[rb_[2]] + allc, writes=[PBb[bi_]])
                k.op("dve", lambda e: e.tensor_scalar(out=y2_[:, 0:TT], in0=y2_[:, 0:TT], scalar1=0.044715, scalar2=1.0, op0=ALU.mult, op1=ALU.add), reads=[rb_[10]], writes=[rb_[10]])
                k.op("dve", lambda e: e.tensor_tensor(out=y2_[:, 0:TT], in0=y2_[:, 0:TT], in1=PB[by_][:, 0:TT], op=ALU.mult), reads=[rb_[10], PBb[by_]], writes=[rb_[10]])

            def rnn_S2(c, bk, ts):
                bx_, by_, ba_, bi_ = bk
                rs_, rf_, rb_ = RT[ts]
                xc, a_, ti_, u_, hs_, y2_ = rf_[1], rf_[3], rf_[5], rf_[6], rf_[8], rf_[10]
                k.op("act", lambda e: e.activation(out=a_[:, 0:TT], in_=PB[ba_][:, 0:TT], func=AF.Tanh, bias=hba[:, c:c + 1], scale=0.5), reads=[PBb[ba_]] + allc, writes=[rb_[3]])
                k.op("act", lambda e: e.activation(out=ti_[:, 0:TT], in_=PB[bi_][:, 0:TT], func=AF.Tanh, bias=hbx[:, c:c + 1], scale=0.5), reads=[PBb[bi_]] + allc, writes=[rb_[5]])
                k.op("act", lambda e: e.activation(out=y2_[:, 0:TT], in_=y2_[:, 0:TT], func=AF.Tanh, scale=GELU_C), reads=[rb_[10]], writes=[rb_[10]])
                k.op("act", lambda e: e.activation(out=a_[:, 0:TT], in_=a_[:, 0:TT], func=AF.Exp, bias=hcl[:, c:c + 1], scale=hcl[:, c:c + 1]), reads=[rb_[3]] + allc, writes=[rb_[3]])
                k.op("pool", lambda e: e.tensor_tensor(out=u_[:, 0:TT], in0=a_[:, 0:TT], in1=a_[:, 0:TT], op=ALU.mult), reads=[rb_[3]], writes=[rb_[6]])
                k.op("dve", lambda e: e.scalar_tensor_tensor(out=ti_[:, 0:TT], in0=ti_[:, 0:TT], scalar=1.0, in1=xc[:, 0:TT], op0=ALU.add, op1=ALU.mult), reads=[rb_[5], rb_[1]], writes=[rb_[5]])
                k.op("act", lambda e: e.activation(out=u_[:, 0:TT], in_=u_[:, 0:TT], func=AF.Sqrt, bias=1.0, scale=-1.0), reads=[rb_[6]], writes=[rb_[6]])
                k.op("dve", lambda e: e.scalar_tensor_tensor(out=y2_[:, 0:TT], in0=y2_[:, 0:TT], scalar=1.0, in1=PB[by_][:, 0:TT], op0=ALU.add, op1=ALU.mult), reads=[rb_[10], PBb[by_]], writes=[rb_[10]])
                k.op("dve", lambda e: e.scalar_tensor_tensor(out=ti_[:, 0:TT], in0=ti_[:, 0:TT], scalar=0.5, in1=u_[:, 0:TT], op0=ALU.mult, op1=ALU.mult), reads=[rb_[5], rb_[6]], writes=[rb_[5]])
                k.op("dve", lambda e: e.tensor_tensor_scan(out=hs_[:, 0:TT], data0=a_[:, 0:TT], data1=ti_[:, 0:TT], initial=hstate[:, c:c + 1], op0=ALU.mult, op1=ALU.add), reads=[rb_[3], rb_[5], hsB[c]], writes=[rb_[8]])
                k.op("pool", lambda e: e.tensor_copy(out=hstate[:, c:c + 1], in_=hs_[:, TT - 1:TT]), reads=[rb_[8]], writes=[hsB[c]])
                k.op("dve", lambda e: e.scalar_tensor_tensor(out=ornnT[:, c, 0:TT], in0=y2_[:, 0:TT], scalar=0.5, in1=hs_[:, 0:TT], op0=ALU.mult, op1=ALU.mult), reads=[rb_[10], rb_[8]], writes=[orB] + reluB)

            bk2 = lambda c: (c % 2, 2 + c % 2, 4 + c % 2, 6 + c % 2)
            BK1 = (4, 5, 6, 7)

            for b in range(BPT):
                prep_hn(0, b)
            if rtB1 is not rtB:
                k.op("dve", lambda e: e.memset(scr[0:1, 0:1], 0.0), writes=[aTB] + rtB1)
            rnn_A(0, bk2(0)); rnn_A(1, bk2(1))
            rnn_S1a(0, bk2(0), 0); rnn_G(0, bk2(0), 0)
            for c in range(NCH):
                if c + 1 < NCH:
                    rnn_S1a(c + 1, bk2(c + 1), (c + 1) % 2); rnn_G(c + 1, bk2(c + 1), (c + 1) % 2)
                rnn_S2(c, bk2(c), c % 2)
                if c + 2 < NCH:
                    rnn_A(c + 2, bk2(c + 2))

            for tt in range(NT):
                has_next = tt + 1 < NT
                for b in range(BPT):
                    t0 = (tt * BPT + b) * PBK
                    k.dma("sp", x1[0:PBK, b, :], jb.x[t0:t0 + PBK, :], writes=[x1B[b]])
                if rtB1 is not rtB:
                    k.op("dve", lambda e: e.memset(scr[0:1, 3:4], 0.0), writes=[aTB] + rtB1)
                for c in range(NCH):
                    wpa, wpaB = getw(("pa", c)); wga, wgaB = getw(("ga", c)); wpr, wprB = getw(("pr", c)); wgb, wgbB = getw(("gb", c))
                    base = 4 * (c % 2)
                    rhs_a = lambda kc: oattnT[:, kc, tt * TT:(tt + 1) * TT]
                    rhs_r = lambda kc: ornnT[:, kc, 0:TT]
                    k.op("pe", mm_group(PB[base][:, 0:TT], wpa, rhs_a, TT), reads=[wpaB] + [oaB[(kc, tt)] for kc in range(8)], writes=[PBb[base]])
                    k.op("pe", mm_group(PB[base + 1][:, 0:TT], wga, rhs_h, TT), reads=[wgaB, hntB], writes=[PBb[base + 1]])
                    k.op("pe", mm_group(PB[base + 2][:, 0:TT], wpr, rhs_r, TT), reads=[wprB, orB], writes=[PBb[base + 2]])
                    k.op("pe", mm_group(PB[base + 3][:, 0:TT], wgb, rhs_h, TT), reads=[wgbB, hntB], writes=[PBb[base + 3]])
                    release_all()
                    _, rf_, rb_ = RT[c % 2]
                    tga, tgb, m1, m2 = rf_[0], rf_[1], rf_[3], rf_[5]
                    bga, bgb, bm1, bm2 = rb_[0], rb_[1], rb_[3], rb_[5]
                    k.op("act", lambda e: e.activation(out=tga[:, 0:TT], in_=PB[base + 1][:, 0:TT], func=AF.Tanh, scale=0.5), reads=[PBb[base + 1]], writes=[bga])
                    k.op("act", lambda e: e.activation(out=tgb[:, 0:TT], in_=PB[base + 3][:, 0:TT], func=AF.Tanh, scale=0.5), reads=[PBb[base + 3]], writes=[bgb])
                    k.op("dve", lambda e: e.scalar_tensor_tensor(out=m1[:, 0:TT], in0=tga[:, 0:TT], scalar=1.0, in1=PB[base][:, 0:TT], op0=ALU.add, op1=ALU.mult), reads=[bga, PBb[base]], writes=[bm1])
                    k.op("dve", lambda e: e.scalar_tensor_tensor(out=m2[:, 0:TT], in0=tgb[:, 0:TT], scalar=1.0, in1=PB[base + 2][:, 0:TT], op0=ALU.add, op1=ALU.mult), reads=[bgb, PBb[base + 2]], writes=[bm2])
                    k.op("pool", lambda e: e.tensor_tensor(out=mergedT[:, c, 0:TT], in0=m1[:, 0:TT], in1=m2[:, 0:TT], op=ALU.add), reads=[bm1, bm2], writes=[mgB] + reluB)
                wo = [getw(("out", kc), hold=True) for kc in range(8)]
                for b in range(BPT):
                    for half in range(2):
                        bank = (b * 2 + half) % 8

                        def f(e, b=b, half=half, bank=bank):
                            ins = None
                            for kc in range(8):
                                ins = e.matmul(PB[bank][0:PBK, :], lhsT=mergedT[:, kc, b * PBK:(b + 1) * PBK], rhs=wo[kc][0][:, half * 512:(half + 1) * 512], start=(kc == 0), stop=(kc == 7))
                            return ins
                        k.op("pe", f, reads=[mgB] + [w_[1] for w_ in wo], writes=[PBb[bank]])
                        xs_ = x1[0:PBK, b, half * 512:(half + 1) * 512]
                        k.op("dve", lambda e, bank=bank, xs_=xs_: e.scalar_tensor_tensor(out=xs_, in0=PB[bank][0:PBK, :], scalar=0.5, in1=xs_, op0=ALU.mult, op1=ALU.add), reads=[PBb[bank], x1B[b]], writes=[x1B[b]])
                    if b >= 1:
                        norm_transpose(x1[0:PBK, b - 1, :], x1B[b - 1], PBK, gmlp, h2T, h2B, (b - 1) * PBK, 6 + (b - 1) % 2)
                norm_transpose(x1[0:PBK, BPT - 1, :], x1B[BPT - 1], PBK, gmlp, h2T, h2B, (BPT - 1) * PBK, 6 + (BPT - 1) % 2)
                release_all()
                rhs_2 = lambda kc: h2T[:, kc, 0:TT]
                if rtB1 is not rtB:
                    k.op("dve", lambda e: e.memset(scr[0:1, 1:2], 0.0), writes=[aTB] + rtB1)
                pend = {}
                for ev in mlp_events(BPT, has_next):
                    if ev[0] == "up":
                        f_ = ev[1]
                        wu, wuB = getw(("up", f_))
                        bank = f_ % 4
                        k.op("pe", mm_group(PB[bank][:, 0:TT], wu, rhs_2, TT), reads=[wuB, h2B], writes=[PBb[bank]])
                        release_all()
                        rl = relu_t[f_ % 4]; rlB = reluB[f_ % 4]
                        k.op("act", lambda e, bank=bank, rl=rl: e.activation(out=rl[:, 0:TT], in_=PB[bank][:, 0:TT], func=AF.Relu), reads=[PBb[bank]], writes=[rlB, mgB])
                        k.op("pool" if f_ % 2 else "dve", lambda e, rl=rl, f_=f_: e.tensor_tensor(out=aT[:, f_, 0:TT], in0=rl[:, 0:TT], in1=rl[:, 0:TT], op=ALU.mult), reads=[rlB], writes=[aTB])
                    elif ev[0] == "down":
                        h_, f_ = ev[1], ev[2]
                        wd, wdB = getw(("down", f_, h_))

                        def f(e, f_=f_, wd=wd):
                            ins = None
                            for b in range(BPT):
                                ins = e.matmul(PB[b][0:PBK, :], lhsT=aT[:, f_, b * PBK:(b + 1) * PBK], rhs=wd[:, 0:512], start=(f_ == 0), stop=(f_ == NFF - 1))
                            return ins
                        k.op("pe", f, reads=[wdB, aTB], writes=[PBb[i] for i in range(BPT)])
                        release_all()
                        if f_ == NFF - 1:
                            for b in range(BPT):
                                xs_ = x1[0:PBK, b, h_ * 512:(h_ + 1) * 512]
                                k.op("dve", lambda e, b=b, xs_=xs_: e.tensor_tensor(out=xs_, in0=PB[b][0:PBK, :], in1=xs_, op=ALU.add), reads=[PBb[b], x1B[b]], writes=[x1B[b]])
                    elif ev[0] == "prepA":
                        pend[ev[1]] = prep_hn(tt + 1, ev[1], split=True)
                    elif ev[0] == "prepB":
                        pend.pop(ev[1])()
                    elif ev[0] == "rnnA":
                        rnn_A(ev[1], BK1); rnn_S1a(ev[1], BK1, 0)
                    elif ev[0] == "rnnG":
                        rnn_G(ev[1], BK1, 0); rnn_S2(ev[1], BK1, 0)
                for b in range(BPT):
                    r = norm_rstd(x1[0:PBK, b, :], x1B[b], PBK)
                    si = alloc_stg()
                    k.op("dve", lambda e, b=b, si=si, r=r: e.scalar_tensor_tensor(out=stg[0:PBK, si, :], in0=x1[0:PBK, b, :], scalar=r, in1=gf_bc[0:PBK, :], op0=ALU.mult, op1=ALU.mult), reads=[x1B[b], ssB] + allc, writes=[stgB[si]])
                    t0 = (tt * BPT + b) * PBK
                    defer_store(si, lambda t0=t0, si=si: k.dma("sp", jb.y[t0:t0 + PBK, :], stg[0:PBK, si, :], reads=[stgB[si]]))
            for w_ in range(3):
                k.dma("sp", jb.co[w_].rearrange("(c p) -> p c", p=128), convhist[:, :, w_], reads=cvB)
            k.dma("sp", jb.ho.rearrange("(c p) -> p c", p=128), hstate[:], reads=hsB)

        def run_job(ji, jb):
            ckpt(1)
            Tn = jb.T; H = jb.HIST
            PBK = min(128, Tn)
            NNB = Tn // PBK; NHB = H // 128; NKB = NHB + NNB
            TT = min(512, Tn); NT = Tn // TT; BPT = TT // PBK
            QS = min(256, Tn); NQS = TT // QS
            k.barrier()
            oaB = {}
            hnB = [Buf("hn%d" % b) for b in range(NNB)]
            vbfB = [Buf("vbf%d" % b) for b in range(NKB)]
            vbfB2 = [Buf("vbfo%d" % b) for b in range(NKB)]
            lfB = Buf("lf"); cB2 = Buf("c"); rrB = Buf("rr")
            k.op("pool", lambda e: e.memset(vbf[:, 0:NKB, :, 64:128], 1.0), writes=vbfB)
            for b in range(NNB):
                si = alloc_stg()
                k.dma("sp", stg[0:PBK, si, :], jb.x[b * PBK:(b + 1) * PBK, :], writes=[stgB[si]])
                norm_transpose(stg[0:PBK, si, :], stgB[si], PBK, gmix, hnT, hnB[b], b * PBK, b % 2)
            ckpt(2)
            if H:
                k.dma("sp", lf[:, 0:NHB, :], jb.clf.rearrange("(b p) h -> p b h", p=128), writes=[lfB])
            zfv = PB[2][:, 0:NKBMAX * NH].rearrange("p (b h) -> p b h", h=NH)
            for b in range(NNB):
                def f(e, b=b):
                    ins = None
                    for kc in range(8):
                        ins = e.matmul(zfv[0:PBK, b, :], lhsT=hnT[:, kc, b * PBK:(b + 1) * PBK], rhs=wf[:, kc, :], start=(kc == 0), stop=(kc == 7))
                    return ins
                k.op("pe", f, reads=[hnB[b]] + allc, writes=[PBb[2]])
            ckpt(2.2)
            lfn = lf[0:PBK, NHB:NKB, :]
            k.op("dve", lambda e: e.tensor_tensor(out=zf[0:PBK, 0:NNB, :], in0=zfv[0:PBK, 0:NNB, :], in1=bf_bc[0:PBK, :].unsqueeze(1).broadcast_to([PBK, NNB, NH]), op=ALU.add), reads=[PBb[2]] + allc, writes=[cB2])
            k.op("act", lambda e: e.activation(out=zf[0:PBK, 0:NNB, :], in_=zf[0:PBK, 0:NNB, :], func=AF.Exp, scale=-1.0), reads=[cB2], writes=[cB2])
            k.op("act", lambda e: e.activation(out=zf[0:PBK, 0:NNB, :], in_=zf[0:PBK, 0:NNB, :], func=AF.Ln, bias=1.0), reads=[cB2], writes=[cB2])
            k.op("dve", lambda e: e.tensor_scalar(out=lfn, in0=zf[0:PBK, 0:NNB, :], scalar1=-1.0, scalar2=None, op0=ALU.mult), reads=[cB2], writes=[lfB])
            ckpt(2.4)
            k.dma("sp", jb.lo.rearrange("(b p) h -> p b h", p=PBK), lfn, reads=[lfB])
            ckpt(2.5)
            kn_of = lambda i: 128 if i < NHB else PBK
            cv_ = PB[3][:, 0:NKBMAX * NH].rearrange("p (b h) -> p b h", h=NH)
            rv_ = PB[5][:, 0:(NKBMAX + 1) * NH].rearrange("p (b h) -> p b h", h=NH)

            def fc(e):
                ins = None
                for i in range(NKB):
                    kn = kn_of(i)
                    ins = e.matmul(cv_[0:kn, i, :], lhsT=tri_f[0:kn, 0:kn], rhs=lf[0:kn, i, :], start=True, stop=(i == 0))
                    for i2 in range(i):
                        k2 = kn_of(i2)
                        ins = e.matmul(cv_[0:kn, i, :], lhsT=ones_f[0:k2, 0:kn], rhs=lf[0:k2, i2, :], start=False, stop=(i2 == i - 1))
                for m in range(1, NKB + 1):
                    for i2 in range(m):
                        k2 = kn_of(i2)
                        ins = e.matmul(rv_[:, m, :], lhsT=ones_f[0:k2, :], rhs=lf[0:k2, i2, :], start=(i2 == 0), stop=(i2 == m - 1))
                ins = e.matmul(PB[3][0:8, 504:512], lhsT=ident_bf[:, 0:8], rhs=ident_bf[:, 0:8], start=True, stop=True)
                return ins
            k.op("pe", fc, reads=[lfB] + allc, writes=[PBb[3], PBb[5]])
            ckpt(2.6)
            k.op("dve", lambda e: e.tensor_scalar(out=csb[:, 0:NKB, :], in0=cv_[:, 0:NKB, :], scalar1=-1.0, scalar2=None, op0=ALU.mult), reads=[PBb[3]], writes=[cB2])
            ckpt(2.7)
            k.op("dve", lambda e: e.memset(rsb[:, 0:1, :], 0.0), writes=[cB2])
            ckpt(2.8)
            k.op("act", lambda e: e.copy(out=rsb[:, 1:NKB + 1, :], in_=rv_[:, 1:NKB + 1, :]), reads=[PBb[5]], writes=[cB2])
            k.op("dve", lambda e: e.tensor_copy(out=rsbb[:, 0:NKB + 1, :], in_=rsb[:, 0:NKB + 1, :]), reads=[cB2], writes=[rrB])
            def ref_index(qb0):
                return NHB + qb0 + 1 if QS == 256 else NHB + qb0
            NQ = Tn // QS

            ckpt(3)
            pst = {}

            def produce(j):
                bi = j % 2
                qTj, kTj = qT[bi], kT[bi]
                qB = Buf("qT"); kBs = [Buf("kT%d" % t) for t in range((H + Tn + 511) // 512 + 1)]
                kb_of = lambda col: kBs[col // 512]
                pst[j] = (qTj, kTj, qB, kb_of)
                ckpt(3.05)
                if H:
                    xi = xn_i[0] % 2; xn_i[0] += 1
                    kst = xn[:, xi, :].rearrange("p (b m) -> p b m", m=128)
                    k.dma("pool", kst[:, 0:NHB, :], jb.ck[:, j * 128:(j + 1) * 128].rearrange("(b p) m -> p b m", p=128), writes=[xnB[xi]])
                    bank = 4 + (j % 2)
                    pbh = PB[bank][:].bitcast(BF16)
                    pbv3 = pbh.rearrange("p (b t) -> p b t", t=128)

                    def trh(e):
                        ins = None
                        for hb in range(NHB):
                            ins = e.transpose(out=pbv3[:, hb, :], in_=kst[:, hb, :], identity=ident_bf[:])
                        return ins
                    k.op("pe", trh, reads=[xnB[xi]] + allc, writes=[PBb[bank]])
                    k.op("act", lambda e: e.copy(out=kTj[:, 0:H], in_=pbh[:, 0:H]), reads=[PBb[bank]], writes=list({id(kb_of(c_)): kb_of(c_) for c_ in range(0, H, 512)}.values()))
                    cvv = jb.cv[:, j * 128:(j + 1) * 128].rearrange("(b p) m -> p b m", p=128)
                    k.dma("pool", vbf[:, 0:NHB, j, 0:64], cvv[:, :, 0:64], writes=vbfB[0:NHB])
                    k.dma("pool", vbf[:, 0:NHB, j, 128:192], cvv[:, :, 64:128], writes=vbfB2[0:NHB])
                wq, wqB = getw(("q", j)); wk, wkB = getw(("k", j)); wv, wvB = getw(("v", j))
                ckpt(3.1)
                for tt in range(NT):
                    cols = slice(tt * TT, (tt + 1) * TT)
                    hb_ = hnB[tt * BPT:(tt + 1) * BPT]
                    rhs = lambda kc, cols=cols: hnT[:, kc, cols]
                    k.op("pe", mm_group(PB[0][:, 0:TT], wq, rhs, TT), reads=[wqB] + hb_, writes=[PBb[0]])
                    k.op("dve", lambda e, cols=cols: e.tensor_scalar(out=qTj[:, cols], in0=PB[0][:, 0:TT], scalar1=0.125, scalar2=None, op0=ALU.mult), reads=[PBb[0]], writes=[qB])
                    ckpt(3.15)
                    k.op("pe", mm_group(PB[1][:, 0:TT], wk, rhs, TT), reads=[wkB] + hb_, writes=[PBb[1]])
                    kvB = [Buf("kvf0"), Buf("kvf1")]
                    k.op("dve", lambda e: e.tensor_copy(out=kTj[:, H + tt * TT:H + (tt + 1) * TT], in_=PB[1][:, 0:TT]), reads=[PBb[1]], writes=[kb_of(H + tt * TT)])
                    k.op("dve", lambda e: e.tensor_copy(out=kvf[0][:, 0:TT], in_=PB[1][:, 0:TT]), reads=[PBb[1]], writes=[kvB[0]])
                    ckpt(3.2)
                    k.op("pe", mm_group(PB[2][:, 0:TT], wv, rhs, TT), reads=[wvB] + hb_, writes=[PBb[2]])
                    k.op("dve", lambda e: e.tensor_copy(out=kvf[1][:, 0:TT], in_=PB[2][:, 0:TT]), reads=[PBb[2]], writes=[kvB[1]])
                    ckpt(3.3)
                    ktv = PB[3][:, :].rearrange("p (b m) -> p b m", m=128)
                    vtv = PB[4][:, :].rearrange("p (b m) -> p b m", m=128)

                    def ftr(e, src, dstv):
                        ins = None
                        for b in range(BPT):
                            ins = e.transpose(out=dstv[0:PBK, b, :], in_=src[:, b * PBK:(b + 1) * PBK], identity=ident_f[:])
                        return ins
                    k.op("pe", lambda e: ftr(e, kvf[0], ktv), reads=[kvB[0]] + allc, writes=[PBb[3]])
                    k.op("pe", lambda e: ftr(e, kvf[1], vtv), reads=[kvB[1]] + allc, writes=[PBb[4]])
                    ckpt(3.4)
                    sk = alloc_stg()
                    sv = alloc_stg()
                    skv = stg[:, sk, 0:512].rearrange("p (b m) -> p b m", m=128)
                    svv = stg[:, sv, 0:512].rearrange("p (b m) -> p b m", m=128)
                    k.op("dve", lambda e: e.tensor_copy(out=skv[0:PBK, 0:BPT, :], in_=ktv[0:PBK, 0:BPT, :]), reads=[PBb[3]], writes=[stgB[sk]])
                    k.op("dve", lambda e: e.tensor_copy(out=svv[0:PBK, 0:BPT, :], in_=vtv[0:PBK, 0:BPT, :]), reads=[PBb[4]], writes=[stgB[sv]])
                    nb0 = tt * BPT
                    vdst = vbf[0:PBK, NHB + nb0:NHB + nb0 + BPT, j, :]
                    k.op("dve", lambda e: e.tensor_copy(out=vdst[:, :, 0:64], in_=vtv[0:PBK, 0:BPT, 0:64]), reads=[PBb[4]], writes=vbfB[NHB + nb0:NHB + nb0 + BPT])
                    k.op("dve", lambda e: e.tensor_copy(out=vdst[:, :, 128:192], in_=vtv[0:PBK, 0:BPT, 64:128]), reads=[PBb[4]], writes=vbfB[NHB + nb0:NHB + nb0 + BPT])
                    ckpt(3.5)
                    kdst = jb.ko[tt * TT:(tt + 1) * TT, j * 128:(j + 1) * 128].rearrange("(b p) m -> p b m", p=PBK)
                    vdsto = jb.vo[tt * TT:(tt + 1) * TT, j * 128:(j + 1) * 128].rearrange("(b p) m -> p b m", p=PBK)
                    defer_store(sk, lambda kdst=kdst, skv=skv, sk=sk: k.dma("sp", kdst, skv[0:PBK, 0:BPT, :], reads=[stgB[sk]]))
                    defer_store(sv, lambda vdsto=vdsto, svv=svv, sv=sv: k.dma("sp", vdsto, svv[0:PBK, 0:BPT, :], reads=[stgB[sv]]))
                release_all()

            def attend(j):
                qTj, kTj, qB, kb_of = pst[j]
                items = []
                for tt in range(NT):
                    oaB[(j, tt)] = Buf("oa")
                    for hh in range(2):
                        last_i = NHB + (tt + 1) * BPT - 1
                        for i in range(last_i + 1):
                            items.append((tt, hh, i, last_i))
                LA = 2

                def geom(n):
                    tt, hh, i, last_i = items[n]
                    kn = kn_of(i)
                    kcol = i * 128 if i < NHB else H + (i - NHB) * PBK
                    nbi = i - NHB
                    dq = max(0, nbi - tt * BPT)
                    return tt, hh, i, last_i, kn, kcol, nbi, dq * PBK, slice(hh * 64, (hh + 1) * 64), tt * TT

                def emit_qk(n):
                    tt, hh, i, last_i, kn, kcol, nbi, qlo, prow, q0 = geom(n)
                    sbank = 3 + (n % 3)
                    h_ = 2 * j + hh
                    nb0 = (q0 + qlo) // PBK; nblk = (TT - qlo) // PBK
                    rr = rsbb[0:1, NHB + nb0 + 1:NHB + nb0 + 1 + nblk, h_:h_ + 1].broadcast_to([1, nblk, PBK])
                    out3 = PB[sbank][0:kn, qlo:TT].rearrange("p (b t) -> p b t", t=PBK)

                    def fqk(e):
                        e.matmul(PB[sbank][0:kn, qlo:TT], lhsT=kTj[prow, kcol:kcol + kn], rhs=qTj[prow, q0 + qlo:q0 + TT], start=True, stop=False)
                        return e.matmul(out3, lhsT=ones_bf[0:1, 0:kn], rhs=rr, start=False, stop=True)
                    k.op("pe", fqk, reads=[kb_of(kcol), qB, rrB] + allc, writes=[PBb[sbank]])

                def emit_rest(n):
                    tt, hh, i, last_i, kn, kcol, nbi, qlo, prow, q0 = geom(n)
                    sbank = 3 + (n % 3)
                    obank = 6 + hh
                    pt = pT[n % 4]; ptb = pTB[n % 4]
                    h_ = 2 * j + hh
                    k.op("act", lambda e: e.activation(out=pt[0:kn, qlo:TT], in_=PB[sbank][0:kn, qlo:TT], func=AF.Exp, bias=csb[0:kn, i, h_:h_ + 1]), reads=[PBb[sbank], cB2], writes=[ptb])
                    if nbi >= tt * BPT:
                        k.op("pool", lambda e: e.tensor_tensor(out=pt[0:kn, qlo:qlo + PBK], in0=pt[0:kn, qlo:qlo + PBK], in1=mask_bf[0:kn, 0:PBK], op=ALU.mult), reads=[ptb] + allc, writes=[ptb])
                    k.op("pe", lambda e: e.matmul(PB[obank][:, qlo:TT], lhsT=vbf[0:kn, i, j, hh * 64:hh * 64 + 128], rhs=pt[0:kn, qlo:TT], start=(i == 0), stop=(i == last_i)), reads=[ptb, vbfB[i], vbfB2[i]], writes=[PBb[obank]])
                    if i == last_i:
                        drow = slice((1 - hh) * 64, (2 - hh) * 64)
                        rd = rden2[hh]; rdB = rdenB2[hh]
                        k.op("dve", lambda e: e.reciprocal(out=rd[drow, 0:TT], in_=PB[obank][drow, 0:TT]), reads=[PBb[obank]], writes=[rdB])
                        k.op("dve", lambda e: e.tensor_tensor(out=oattnT[prow, j, q0:q0 + TT], in0=PB[obank][prow, 0:TT], in1=rd[drow, 0:TT], op=ALU.mult), reads=[PBb[obank], rdB], writes=[oaB[(j, tt)]])

                for n in range(len(items) + LA):
                    if n < len(items):
                        emit_qk(n)
                    if n >= LA:
                        emit_rest(n - LA)
            produce(0)
            for j in range(NCH):
                if j + 1 < NCH:
                    produce(j + 1)
                attend(j)
            flush_stores()
            k.barrier()
            ckpt(5)
            tail(jb, oaB)
            flush_stores()

        try:
            ckpt(0)
            for ji, jb in enumerate(jobs):
                run_job(ji, jb)
        except _Stop:
            pass
        for g_ in range(5):
            if cast_cnt[g_] > 0:
                sp.wait_ge(castsems[g_], cast_cnt[g_])
        k.finish()
    return nc


_NC_CACHE = {}


def _run(inputs, n_cores, NPS, T, NSS, TS, HIST, trace=False):
    key = (NPS, T, NSS, TS, HIST)
    if key not in _NC_CACHE:
        _NC_CACHE[key] = build(NPS, T, NSS, TS, HIST)
    nc = _NC_CACHE[key]
    f = lambda a: np.ascontiguousarray(np.asarray(a, dtype=np.float32))
    i = inputs
    shared = {
        "g_mix": f(i["norm_mix_g"][0]), "w_in": f(i["w_in"][0]), "b_f": f(i["b_f"][0]), "conv_w": f(i["conv_w"][0]),
        "conv_b": f(i["conv_b"][0]), "w_rg_a": f(i["w_rg_a"][0]), "b_rg_a": f(i["b_rg_a"][0]), "w_rg_x": f(i["w_rg_x"][0]),
        "b_rg_x": f(i["b_rg_x"][0]), "rg_lambda": f(i["rg_lambda"][0]), "w_pa": f(i["w_proj_attn"][0]),
        "w_pr": f(i["w_proj_rnn"][0]), "w_out": f(i["w_out"][0]), "g_mlp": f(i["norm_mlp_g"][0]), "w_up": f(i["w_up"][0]),
        "w_down": f(i["w_down"][0]), "g_fin": f(i["norm_final_g"]),
    }
    in_maps = []
    for c in range(n_cores):
        m = dict(shared)
        ps = slice(c * NPS, (c + 1) * NPS); ss = slice(c * NSS, (c + 1) * NSS)
        m["xp"] = f(i["x_prompt"][ps]); m["xs"] = f(i["x_sample"][ss])
        m["ck"] = f(i["cache_k"][0, ss]).reshape(NSS, HIST, D); m["cv"] = f(i["cache_v"][0, ss]).reshape(NSS, HIST, D)
        m["clf"] = f(i["cache_logf"][0, ss]); m["sconv"] = f(i["state_conv"][0, ss]); m["sh"] = f(i["state_rglru"][0, ss])
        in_maps.append(m)
    res = run_bass_kernel_spmd(nc, in_maps, core_ids=list(range(n_cores)), trace=trace)
    R = res.results
    cat = lambda name: np.concatenate([np.asarray(r[name]) for r in R], axis=0)
    BP = n_cores * NPS; BS = n_cores * NSS
    out = (
        cat("yp"), cat("ys"),
        cat("kp").reshape(1, BP, T, NH, DH), cat("vp").reshape(1, BP, T, NH, DH), cat("lp").reshape(1, BP, T, NH),
        cat("cp").reshape(1, BP, 3, D), cat("hp").reshape(1, BP, D),
        cat("ks").reshape(1, BS, TS, NH, DH), cat("vs").reshape(1, BS, TS, NH, DH), cat("ls").reshape(1, BS, TS, NH),
        cat("cs").reshape(1, BS, 3, D), cat("hs").reshape(1, BS, D),
    )
    return tuple(np.ascontiguousarray(o, dtype=np.float32) for o in out), res


def kernel(**inputs):
    out, _ = _run(inputs, 8, 4, 2048, 4, 64, 1024)
    return out
```

```python
import numpy as np
from contextlib import ExitStack
import concourse.bass as bass
import concourse.mybir as mybir
from concourse.bass_utils import run_bass_kernel_spmd

F32 = mybir.dt.float32
BF16 = mybir.dt.bfloat16
ALU = mybir.AluOpType
AF = mybir.ActivationFunctionType

D = 1024
NH = 16
DH = 64
NCH = 8
DFF = 4096
NFF = 32
DIN = 7184
C_Q, C_K, C_V, C_F, C_XR, C_YR, C_GA, C_GB = 0, 1024, 2048, 3072, 3088, 4112, 5136, 6160
EPS = 1e-6
GELU_C = 0.7978845608028654


class Tok:
    __slots__ = ("sem", "val", "eng")

    def __init__(self, sem, val, eng):
        self.sem, self.val, self.eng = sem, val, eng


class Buf:
    __slots__ = ("name", "w", "r", "excl")

    def __init__(self, name, excl=False):
        self.name, self.w, self.r, self.excl = name, None, [], excl


class Eng:
    def __init__(self, name, e, sem):
        self.name, self.e, self.sem, self.cnt, self.seen = name, e, sem, 0, {}


class K:
    def __init__(self, nc, es, ndma=40):
        self.nc = nc
        self.engs = {}
        for name, e in (("pe", nc.tensor), ("act", nc.scalar), ("dve", nc.vector), ("pool", nc.gpsimd), ("sp", nc.sync)):
            self.engs[name] = Eng(name, e, es.enter_context(nc.semaphore("sem_" + name)))
        self.dpools = {"sp": [[es.enter_context(nc.semaphore("dsp%d" % i)), 0] for i in range(28)],
                       "pool": [[es.enter_context(nc.semaphore("dpl%d" % i)), 0] for i in range(12)]}
        self.dsems = self.dpools["sp"] + self.dpools["pool"]
        self.dnext = {"sp": 0, "pool": 0}
        self.dma_toks = []

    def _wait(self, E, t, raw, is_dma=False):
        if t is None:
            return
        if t.eng is E and not is_dma:
            if E.name in ("pe", "sp"):
                return
            if not raw or t.val < E.cnt - 2:
                return
        key = id(t.sem)
        if E.seen.get(key, 0) >= t.val:
            return
        E.seen[key] = t.val
        self._pw.append((t.sem, t.val))

    def _deps(self, E, reads, writes, is_dma=False):
        self._pw = []
        for b in reads:
            self._wait(E, b.w, True, is_dma)
            if b.excl:
                for t in b.r:
                    self._wait(E, t, False, is_dma)
        for b in writes:
            self._wait(E, b.w, False, is_dma)
            for t in b.r:
                self._wait(E, t, False, is_dma)

    def _commit(self, tok, reads, writes):
        for b in reads:
            b.r = [t for t in b.r if t.eng is not tok.eng or tok.eng is None] + [tok]
        for b in writes:
            b.w = tok
            b.r = []

    def op(self, en, fn, reads=(), writes=(), multi=False):
        E = self.engs[en]
        self._deps(E, reads, writes)
        pw = {}
        for sem_, val_ in self._pw:
            pw[id(sem_)] = (sem_, max(val_, pw.get(id(sem_), (None, 0))[1]))
        pw = list(pw.values())
        att = pw.pop() if (pw and not multi and ATTACH) else None
        for sem_, val_ in pw:
            E.e.wait_ge(sem_, val_)
        ins = fn(E.e)
        if att is not None:
            ins._wait_ge(att[0], att[1])
        E.cnt += 1
        ins.then_inc(E.sem, 1)
        tok = Tok(E.sem, E.cnt, E)
        self._commit(tok, reads, writes)
        return tok

    def dma(self, en, out, in_, reads=(), writes=()):
        E = self.engs[en]
        self._deps(E, reads, writes, True)
        for sem_, val_ in self._pw:
            E.e.wait_ge(sem_, val_)
        pl = self.dpools[en]
        slot = pl[self.dnext[en]]
        self.dnext[en] = (self.dnext[en] + 1) % len(pl)
        if slot[1] > 0:
            key = id(slot[0])
            if E.seen.get(key, 0) < slot[1]:
                E.e.wait_ge(slot[0], slot[1])
                E.seen[key] = slot[1]
        E.e.dma_start(out=out, in_=in_).then_inc(slot[0], 16)
        slot[1] += 16
        tok = Tok(slot[0], slot[1], None)
        self._commit(tok, reads, writes)
        self.dma_toks.append(tok)
        return tok

    def barrier(self):
        cur = [(E.sem, E.cnt) for E in self.engs.values() if E.cnt > 0]
        dm = [(s[0], s[1]) for s in self.dsems if s[1] > 0]
        for E in self.engs.values():
            for sem, val in cur + dm:
                if sem is E.sem:
                    continue
                key = id(sem)
                if E.seen.get(key, 0) < val:
                    E.e.wait_ge(sem, val)
                    E.seen[key] = val

    def finish(self):
        E = self.engs["sp"]
        for s in self.dsems:
            if s[1] > 0 and E.seen.get(id(s[0]), 0) < s[1]:
                E.e.wait_ge(s[0], s[1])
        for F in self.engs.values():
            if F is not E and F.cnt > 0:
                E.e.wait_ge(F.sem, F.cnt)


class Job:
    pass


import os as _os2
ATTACH = _os2.environ.get("KATTACH", "1") == "1"


class _Stop(Exception):
    pass


import os as _os
_STAGE = float(_os.environ.get('KSTAGE', '99'))


def ckpt(n):
    if n > _STAGE:
        raise _Stop()


def build(NPS, T, NSS, TS, HIST):
    nc = bass.Bass("TRN2", target_bir_lowering=False)
    dt = nc.dram_tensor

    def din(name, shape):
        return dt(name, shape, F32, kind="ExternalInput").ap()

    def dout(name, shape):
        return dt(name, shape, F32, kind="ExternalOutput").ap()

    xp = din("xp", [NPS, T, D]); xs = din("xs", [NSS, TS, D])
    ck = din("ck", [NSS, HIST, D]); cv = din("cv", [NSS, HIST, D]); clf = din("clf", [NSS, HIST, NH])
    sconv = din("sconv", [NSS, 3, D]); sh = din("sh", [NSS, D])
    g_mix = din("g_mix", [D]); w_in = din("w_in", [D, DIN]); b_f = din("b_f", [NH])
    conv_w = din("conv_w", [4, D]); conv_b = din("conv_b", [D])
    w_rg_a = din("w_rg_a", [16, 64, 64]); b_rg_a = din("b_rg_a", [D])
    w_rg_x = din("w_rg_x", [16, 64, 64]); b_rg_x = din("b_rg_x", [D])
    rg_lambda = din("rg_lambda", [D])
    w_pa = din("w_pa", [D, D]); w_pr = din("w_pr", [D, D]); w_out = din("w_out", [D, D])
    g_mlp = din("g_mlp", [D]); w_up = din("w_up", [D, DFF]); w_down = din("w_down", [DFF, D]); g_fin = din("g_fin", [D])

    yp = dout("yp", [NPS, T, D]); ys = dout("ys", [NSS, TS, D])
    kp = dout("kp", [NPS, T, D]); vp = dout("vp", [NPS, T, D]); lp = dout("lp", [NPS, T, NH])
    cp = dout("cp", [NPS, 3, D]); hp = dout("hp", [NPS, D])
    ks = dout("ks", [NSS, TS, D]); vs = dout("vs", [NSS, TS, D]); ls = dout("ls", [NSS, TS, NH])
    cs = dout("cs", [NSS, 3, D]); hs_o = dout("hs", [NSS, D])

    GR = {"q": C_Q, "k": C_K, "v": C_V, "xr": C_XR, "yr": C_YR, "ga": C_GA, "gb": C_GB}
    wst = {g: dt("wst_" + g, [NCH, 128, 8, 128], BF16, kind="Internal").ap() for g in GR}
    wst["pa"] = dt("wst_pa", [NCH, 128, 8, 128], BF16, kind="Internal").ap()
    wst["pr"] = dt("wst_pr", [NCH, 128, 8, 128], BF16, kind="Internal").ap()
    wst["up"] = dt("wst_up", [NFF, 128, 8, 128], BF16, kind="Internal").ap()
    wst["out"] = dt("wst_out", [8, 128, 1024], BF16, kind="Internal").ap()
    wst["down"] = dt("wst_down", [NFF, 128, 1024], BF16, kind="Internal").ap()

    jobs = []
    for s in range(NPS):
        j = Job(); j.T = T; j.HIST = 0; j.x = xp[s]; j.y = yp[s]; j.ko = kp[s]; j.vo = vp[s]; j.lo = lp[s]
        j.co = cp[s]; j.ho = hp[s]; jobs.append(j)
    for s in range(NSS):
        j = Job(); j.T = TS; j.HIST = HIST; j.x = xs[s]; j.y = ys[s]; j.ko = ks[s]; j.vo = vs[s]; j.lo = ls[s]
        j.co = cs[s]; j.ho = hs_o[s]; j.ck = ck[s]; j.cv = cv[s]; j.clf = clf[s]; j.sconv = sconv[s]; j.sh = sh[s]
        jobs.append(j)

    TMAX = max(T, TS + HIST)
    NKBMAX = max(T // 128, HIST // 128 + 1)
    NS = 11

    with ExitStack() as es:
        es.enter_context(nc.allow_non_contiguous_dma(reason="small strided parameter / state vectors"))
        sb = lambda name, shape, d=F32: es.enter_context(nc.sbuf_tensor(name, shape, d))
        ident_bf = sb("ident_bf", [128, 128], BF16); ident_f = sb("ident_f", [128, 128])
        mask_bf = sb("mask_bf", [128, 128], BF16); tri_f = sb("tri_f", [128, 128]); ones_f = sb("ones_f", [128, 128])
        gmix = sb("gmix", [128, 8]); gmlp = sb("gmlp", [128, 8]); convw = sb("convw", [128, 4, 8]); convb = sb("convb", [128, 8])
        scr = sb("scr", [128, 8]); hcl2 = sb("hcl2", [128, 8])
        hba = sb("hba", [128, 8]); hbx = sb("hbx", [128, 8]); hcl = sb("hcl", [128, 8]); lam = sb("lam", [128, 8])
        gf_bc = sb("gf_bc", [128, D]); bf_bc = sb("bf_bc", [128, NH])
        bd_a = sb("bd_a", [128, 8, 128], BF16); bd_x = sb("bd_x", [128, 8, 128], BF16); wf = sb("wf", [128, 8, NH], BF16)
        oattnT = sb("oattnT", [128, 8, T], BF16)
        wring = sb("wring", [128, NS, 1024], BF16)
        NSTG = 5
        stg = sb("stg", [128, NSTG, D])
        xn = sb("xn", [128, 2, D], BF16)
        junk = sb("junk", [128, D], BF16)
        ss = sb("ss", [128, 8]); rstd = sb("rstd", [128, 8])
        hstate = sb("hstate", [128, 8]); convhist = sb("convhist", [128, 8, 3])
        lf = sb("lf", [128, NKBMAX, NH]); zf = sb("zf", [128, NKBMAX, NH])
        csb = sb("csb", [128, NKBMAX, NH]); rsb = sb("rsb", [128, NKBMAX + 1, NH])
        bias_ab = [sb("bias_a", [128, 2, NKBMAX, NKBMAX]), sb("bias_b", [128, 2, NKBMAX, NKBMAX])]
        rden = sb("rden", [128, 512]); rden_b = sb("rden_b", [128, 512])
        A_HN = 8 * T; A_VBF = NKBMAX * 8 * 192; A_QK = 2 * T + 2 * TMAX; A_PT = 4 * 512; A_KVF = 2 * 512 * 2
        att_elems = A_HN + A_VBF + A_QK + A_PT + A_KVF
        TT0 = min(512, T)
        USED_T = (0, 1, 2, 3, 5, 6, 8, 10)
        tail_elems = 8 * TT0 + 2 * 4 * D + 2 * len(USED_T) * 520 + 8 * TT0 + 8 * TT0 + 8 * TT0 + 32 * TT0
        arena = sb("arena", [128, max(att_elems, tail_elems)], BF16)
        off = [0]

        def carve(n, d=BF16):
            a = arena[:, off[0]:off[0] + n]
            off[0] += n
            return a if d == BF16 else a.bitcast(F32)

        hnT = carve(A_HN).rearrange("p (c t) -> p c t", c=8)
        vbf = carve(A_VBF).rearrange("p (b j s) -> p b j s", j=8, s=192)
        qT = [carve(T), carve(T)]; kT = [carve(TMAX), carve(TMAX)]
        pT = [carve(512) for _ in range(4)]
        kvf = [carve(1024, F32), carve(1024, F32)]
        off[0] = 0
        hnt = carve(8 * TT0).rearrange("p (c t) -> p c t", c=8)
        x1 = carve(2 * 4 * D, F32).rearrange("p (b n) -> p b n", b=4)
        rt = [carve(2 * 520, F32) if i in USED_T else None for i in range(11)]
        h2T = carve(8 * TT0).rearrange("p (c t) -> p c t", c=8)
        mergedT = carve(8 * TT0).rearrange("p (c t) -> p c t", c=8)
        ornnT = carve(8 * TT0).rearrange("p (c t) -> p c t", c=8)
        aT_off = off[0]
        aT = carve(32 * TT0).rearrange("p (f t) -> p f t", f=32)
        off[0] = aT_off
        rt1 = [carve(2 * 520, F32) if i in USED_T else None for i in range(11)] if 32 * TT0 >= 11 * 1040 else None
        PB = [es.enter_context(nc.psum_tensor("pb%d" % i, [128, 512], F32)) for i in range(8)]
        PBb = [Buf("pb%d" % i, excl=True) for i in range(8)]

        if _os.environ.get('KDEBUG'):
            print('SBUF bytes remaining', nc.sbuf_bytes_remaining, 'att_elems', att_elems, 'tail_elems', tail_elems)
        k = K(nc, es)
        castsems = [es.enter_context(nc.semaphore("castsem%d" % i)) for i in range(5)]
        cast_cnt = [0, 0, 0, 0, 0]
        _GRP = {"xr": 2, "yr": 2, "pa": 2, "pr": 2, "ga": 2, "gb": 2, "out": 3, "up": 3, "down": 4}
        grp_of = lambda key: (0 if key[1] < 2 else 1) if key[0] in ("q", "k", "v") else _GRP[key[0]]
        block = es.enter_context(nc.Block())
        pe, act, dve, pool, sp = nc.tensor, nc.scalar, nc.vector, nc.gpsimd, nc.sync

        ncast = 0
        cast_done = {}

        def cast(key, out, in_):
            g_ = grp_of(key)
            pool.dma_start(out=out, in_=in_).then_inc(castsems[g_], 16)
            cast_cnt[g_] += 16

        def cast_in(g):
            for c in range(NCH):
                cast((g, c), wst[g][c], w_in[:, GR[g] + c * 128:GR[g] + (c + 1) * 128].rearrange("(kc p) m -> p kc m", p=128))

        cB = {n: Buf(n) for n in ("ident", "vec", "bd", "wf")}
        k.op("pool", lambda e: e.memset(ident_bf[:], 1.0), writes=[cB["ident"]])
        k.op("pool", lambda e: e.affine_select(out=ident_bf[:], in_=ident_bf[:], pattern=[[-1, 128]], compare_op=ALU.is_equal, fill=0.0, base=0, channel_multiplier=1), reads=[cB["ident"]], writes=[cB["ident"]])
        k.op("pool", lambda e: e.memset(ident_f[:], 1.0), writes=[cB["ident"]])
        k.op("pool", lambda e: e.affine_select(out=ident_f[:], in_=ident_f[:], pattern=[[-1, 128]], compare_op=ALU.is_equal, fill=0.0, base=0, channel_multiplier=1), reads=[cB["ident"]], writes=[cB["ident"]])
        k.op("pool", lambda e: e.memset(mask_bf[:], 1.0), writes=[cB["ident"]])
        k.op("pool", lambda e: e.affine_select(out=mask_bf[:], in_=mask_bf[:], pattern=[[1, 128]], compare_op=ALU.is_ge, fill=0.0, base=0, channel_multiplier=-1), reads=[cB["ident"]], writes=[cB["ident"]])
        k.op("pool", lambda e: e.memset(tri_f[:], 1.0), writes=[cB["ident"]])
        k.op("pool", lambda e: e.affine_select(out=tri_f[:], in_=tri_f[:], pattern=[[1, 128]], compare_op=ALU.is_ge, fill=0.0, base=0, channel_multiplier=-1), reads=[cB["ident"]], writes=[cB["ident"]])
        k.op("pool", lambda e: e.memset(ones_f[:], 1.0), writes=[cB["ident"]])
        k.op("pool", lambda e: e.memset(bd_a[:], 0.0), writes=[cB["bd"]])
        k.op("pool", lambda e: e.memset(bd_x[:], 0.0), writes=[cB["bd"]])
        for (bd, wsrc) in ((bd_a, w_rg_a), (bd_x, w_rg_x)):
            wv = wsrc.rearrange("(j two) d e -> two d j e", two=2)
            k.dma("pool", bd[0:64, :, 0:64], wv[0], writes=[cB["bd"]])
            k.dma("pool", bd[64:128, :, 64:128], wv[1], writes=[cB["bd"]])
        k.dma("pool", wf[:], w_in[:, C_F:C_F + NH].rearrange("(kc p) n -> p kc n", p=128), writes=[cB["wf"]])
        for c in range(NCH):
            for g in ("q", "k", "v"):
                cast((g, c), wst[g][c], w_in[:, GR[g] + c * 128:GR[g] + (c + 1) * 128].rearrange("(kc p) m -> p kc m", p=128))
        for g in ("xr", "yr", "pa", "pr", "ga", "gb"):
            if g in GR:
                cast_in(g)
            else:
                src = {"pa": w_pa, "pr": w_pr}[g]
                for c in range(NCH):
                    cast((g, c), wst[g][c], src[:, c * 128:(c + 1) * 128].rearrange("(kc p) m -> p kc m", p=128))
        for kc in range(8):
            cast(("out", kc), wst["out"][kc], w_out[kc * 128:(kc + 1) * 128, :])
        for f in range(NFF):
            cast(("up", f), wst["up"][f], w_up[:, f * 128:(f + 1) * 128].rearrange("(kc p) m -> p kc m", p=128))
        for f in range(NFF):
            cast(("down", f), wst["down"][f], w_down[f * 128:(f + 1) * 128, :])

        fm = lambda v: v.rearrange("(c p) -> p c", p=128)
        k.dma("sp", gmix[:], fm(g_mix), writes=[cB["vec"]])
        k.dma("sp", gmlp[:], fm(g_mlp), writes=[cB["vec"]])
        k.dma("sp", convw[:], conv_w.rearrange("w (c p) -> p w c", p=128), writes=[cB["vec"]])
        k.dma("sp", convb[:], fm(conv_b), writes=[cB["vec"]])
        k.dma("sp", hba[:], fm(b_rg_a), writes=[cB["vec"]])
        k.dma("sp", hbx[:], fm(b_rg_x), writes=[cB["vec"]])
        k.dma("sp", lam[:], fm(rg_lambda), writes=[cB["vec"]])
        k.dma("sp", gf_bc[:], g_fin.partition_broadcast(128), writes=[cB["vec"]])
        k.dma("sp", bf_bc[:], b_f.partition_broadcast(128), writes=[cB["vec"]])
        k.op("act", lambda e: e.activation(out=lam[:], in_=lam[:], func=AF.Exp, scale=-1.0), reads=[cB["vec"]], writes=[cB["vec"]])
        k.op("act", lambda e: e.activation(out=lam[:], in_=lam[:], func=AF.Ln, bias=1.0), reads=[cB["vec"]], writes=[cB["vec"]])
        k.op("dve", lambda e: e.tensor_scalar(out=hcl[:], in0=lam[:], scalar1=-4.0, scalar2=None, op0=ALU.mult), reads=[cB["vec"]], writes=[cB["vec"]])
        k.op("dve", lambda e: e.tensor_scalar(out=hba[:], in0=hba[:], scalar1=0.5, scalar2=None, op0=ALU.mult), reads=[cB["vec"]], writes=[cB["vec"]])
        k.op("dve", lambda e: e.tensor_scalar(out=hbx[:], in0=hbx[:], scalar1=0.5, scalar2=None, op0=ALU.mult), reads=[cB["vec"]], writes=[cB["vec"]])
        k.op("dve", lambda e: e.tensor_scalar(out=hcl2[:], in0=hcl[:], scalar1=2.0, scalar2=None, op0=ALU.mult), reads=[cB["vec"]], writes=[cB["vec"]])
        allc = list(cB.values())

        def mlp_events(BPT, has_next):
            steps = [("up", f) for f in range(NFF)]
            for p_ in range(2):
                steps += [("down", p_, f) for f in range(NFF)]
            ins = {}
            if has_next:
                for b in range(BPT):
                    ins.setdefault(3 * b, []).append(("prepA", b))
                    ins.setdefault(3 * b + 2, []).append(("prepB", b))
                for c in range(NCH):
                    ins.setdefault(13 + 10 * c, []).append(("rnnA", c))
                    ins.setdefault(18 + 10 * c, []).append(("rnnG", c))
            ev = []
            for i, st in enumerate(steps):
                ev += ins.get(i, [])
                ev.append(st)
            return ev

        order = []
        for jb in jobs:
            for j in range(NCH):
                order += [("q", j), ("k", j), ("v", j)]
            TTj = min(512, jb.T); NTj = jb.T // TTj; BPTj = TTj // min(128, jb.T)
            for c in range(NCH):
                order += [("xr", c), ("yr", c)]
            for tt in range(NTj):
                for c in range(NCH):
                    order += [("pa", c), ("ga", c), ("pr", c), ("gb", c)]
                order += [("out", kc) for kc in range(8)]
                for ev in mlp_events(BPTj, tt + 1 < NTj):
                    if ev[0] == "up":
                        order.append(("up", ev[1]))
                    elif ev[0] == "down":
                        order.append(("down", ev[2], ev[1]))
                    elif ev[0] == "rnnA":
                        order += [("xr", ev[1]), ("yr", ev[1])]
        ringB = [Buf("ring%d" % i) for i in range(NS)]
        rs = {"issued": 0, "next": 0, "unrel": 0}
        sp_seen_cast = [False] * 5

        def ring_issue(upto):
            while rs["issued"] < min(upto, len(order)):
                i = rs["issued"]
                flush_stores(lambda ent: ent[2] + DEFER <= i)
                key = order[i]
                g_ = grp_of(key)
                if not sp_seen_cast[g_]:
                    sp.wait_ge(castsems[g_], cast_cnt[g_])
                    sp_seen_cast[g_] = True
                slot = i % NS
                if key[0] == "down":
                    k.dma("sp", wring[:, slot, 0:512], wst["down"][key[1]][:, key[2] * 512:(key[2] + 1) * 512], writes=[ringB[slot]])
                elif key[0] == "out":
                    k.dma("sp", wring[:, slot, :], wst["out"][key[1]], writes=[ringB[slot]])
                else:
                    k.dma("sp", wring[:, slot, :], wst[key[0]][key[1]].rearrange("p kc m -> p (kc m)"), writes=[ringB[slot]])
                rs["issued"] += 1

        def getw(key, hold=True):
            i = rs["next"]
            assert order[i] == key, (order[i], key, i)
            rs["next"] += 1
            if not hold:
                rs["unrel"] = rs["next"] - 1
            ring_issue(rs["unrel"] + NS)
            assert rs["issued"] > i
            slot = i % NS
            return wring[:, slot, :], ringB[slot]

        def release_all():
            rs["unrel"] = rs["next"]

        pTB = [Buf("pT%d" % i) for i in range(4)]
        rdenB = Buf("rden")
        rden2 = [rden, rden_b]; rdenB2 = [Buf("rden0"), Buf("rden1")]
        stgB = [Buf("stg%d" % i) for i in range(NSTG)]
        stg_i = [0]
        pend_st = []
        DEFER = 5

        def flush_stores(pred=None):
            keep = []
            for ent in pend_st:
                if pred is None or pred(ent):
                    ent[1]()
                else:
                    keep.append(ent)
            pend_st[:] = keep

        def alloc_stg():
            si = stg_i[0] % NSTG
            stg_i[0] += 1
            flush_stores(lambda ent: ent[0] == si)
            return si

        def defer_store(si, fn):
            pend_st.append((si, fn, rs["issued"]))
        xnB = [Buf("xn0"), Buf("xn1")]
        xn_i = [0]
        junkB = Buf("junk"); ssB = Buf("ss")
        ss_i = [0]

        def norm_rstd(src_ap, srcB, np_):
            col = ss_i[0] % 8
            ss_i[0] += 1
            k.op("act", lambda e: e.activation(out=junk[0:np_, :], in_=src_ap, func=AF.Square, accum_out=ss[0:np_, col:col + 1]), reads=[srcB], writes=[junkB, ssB])
            k.op("act", lambda e: e.activation(out=rstd[0:np_, col:col + 1], in_=ss[0:np_, col:col + 1], func=AF.Ln, scale=1.0 / D, bias=EPS), reads=[ssB], writes=[ssB])
            k.op("act", lambda e: e.activation(out=rstd[0:np_, col:col + 1], in_=rstd[0:np_, col:col + 1], func=AF.Exp, scale=-0.5), reads=[ssB], writes=[ssB])
            return rstd[0:np_, col:col + 1]

        def norm_transpose(src_ap, srcB, np_, gvec, dst, dstB, c0, bank, split=False):
            r = norm_rstd(src_ap, srcB, np_)
            xi = xn_i[0] % 2
            xn_i[0] += 1
            k.op("dve", lambda e: e.tensor_scalar(out=xn[0:np_, xi, :], in0=src_ap, scalar1=r, scalar2=None, op0=ALU.mult), reads=[srcB, ssB], writes=[xnB[xi]])
            pbv = PB[bank][:].bitcast(BF16).rearrange("p (c t) -> p c t", c=8)

            def tr(e):
                ins = None
                for c in range(8):
                    ins = e.transpose(out=pbv[:, c, 0:np_], in_=xn[0:np_, xi, c * 128:(c + 1) * 128], identity=ident_bf[0:np_, 0:np_])
                return ins
            def part2():
                k.op("pe", tr, multi=True, reads=[xnB[xi]] + allc, writes=[PBb[bank]])
                k.op("dve", lambda e: e.tensor_tensor(out=dst[:, :, c0:c0 + np_], in0=pbv[:, :, 0:np_], in1=gvec[:, :].unsqueeze(2).broadcast_to([128, 8, np_]), op=ALU.mult), reads=[PBb[bank]] + allc, writes=[dstB])
            if split:
                return part2
            part2()

        def mm_group(bank_ap, wslot, rhs_fn, n):
            def f(e):
                ins = None
                for kc in range(8):
                    ins = e.matmul(bank_ap, lhsT=wslot[:, kc * 128:(kc + 1) * 128], rhs=rhs_fn(kc), start=(kc == 0), stop=(kc == 7))
                return ins
            return f

        rtB = [Buf("rt%d" % i) for i in range(11)]
        rtf = [r[:, 0:520] if r is not None else None for r in rt]
        if rt1 is None:
            rt1 = rt; rtB1 = rtB
        else:
            rtB1 = [Buf("rtb%d" % i) for i in range(11)]
        rtf1 = [r[:, 0:520] if r is not None else None for r in rt1]
        RT = [(rt, rtf, rtB), (rt1, rtf1, rtB1)]

        def tail(jb, oaB):
            Tn = jb.T; H = jb.HIST
            PBK = min(128, Tn); TT = min(512, Tn); NT = Tn // TT; BPT = TT // PBK
            stB = Buf("state")
            inB = [Buf("stin%d" % i) for i in range(4)]
            if H:
                k.dma("sp", hstate[:], jb.sh.rearrange("(c p) -> p c", p=128), writes=[inB[0]])
                for w_ in range(3):
                    k.dma("sp", convhist[:, :, w_], jb.sconv[w_].rearrange("(c p) -> p c", p=128), writes=[inB[1 + w_]])
            else:
                k.op("pool", lambda e: e.memset(hstate[:], 0.0), writes=[inB[0]])
                k.op("pool", lambda e: e.memset(convhist[:], 0.0), writes=[inB[1]])
            k.op("dve", lambda e: e.memset(scr[0:1, 2:3], 0.0), reads=inB, writes=[stB])
            cvB = [Buf("cv%d" % c) for c in range(NCH)]; hsB = [Buf("hs%d" % c) for c in range(NCH)]
            for b_ in cvB + hsB:
                b_.w = stB.w
            x1B = [Buf("x1_%d" % b) for b in range(BPT)]
            hntB = Buf("hnt"); h2B = Buf("h2T"); orB = Buf("ornnT"); mgB = Buf("mergedT"); aTB = Buf("aT")
            relu_t = [mergedT[:, 2 * i:2 * i + 2, 0:TT0].rearrange("p a t -> p (a t)").bitcast(F32) for i in range(4)]
            reluB = [Buf("relu%d" % i) for i in range(4)]
            rhs_h = lambda kc: hnt[:, kc, 0:TT]

            def prep_hn(tt, b, split=False):
                t0 = (tt * BPT + b) * PBK
                si = alloc_stg()
                k.dma("sp", stg[0:PBK, si, :], jb.x[t0:t0 + PBK, :], writes=[stgB[si]])
                return norm_transpose(stg[0:PBK, si, :], stgB[si], PBK, gmix, hnt, hntB, b * PBK, 4 + b % 2 if split else b % 2, split=split)

            def rnn_A(c, bk):
                wxr, wxrB = getw(("xr", c)); wyr, wyrB = getw(("yr", c))
                k.op("pe", multi=True, fn=mm_group(PB[bk[0]][:, 0:TT], wxr, rhs_h, TT), reads=[wxrB, hntB], writes=[PBb[bk[0]]])
                k.op("pe", multi=True, fn=mm_group(PB[bk[1]][:, 0:TT], wyr, rhs_h, TT), reads=[wyrB, hntB], writes=[PBb[bk[1]]])
                release_all()

            def rnn_S1a(c, bk, ts):
                bx_, by_, ba_, bi_ = bk
                rs_, rf_, rb_ = RT[ts]
                xrp, xc, y2_ = rf_[0], rf_[1], rf_[10]
                xcb = rs_[2].bitcast(BF16)[:, 0:TT]
                k.op("pool", lambda e: e.tensor_copy(out=xrp[:, 0:3], in_=convhist[:, c, :]), reads=[cvB[c]], writes=[rb_[0]])
                k.op("act", lambda e: e.copy(out=xrp[:, 3:3 + TT], in_=PB[bx_][:, 0:TT]), reads=[PBb[bx_]], writes=[rb_[0]])
                k.op("pool", lambda e: e.tensor_copy(out=convhist[:, c, :], in_=xrp[:, TT:TT + 3]), reads=[rb_[0]], writes=[cvB[c]])
                k.op("act", lambda e: e.activation(out=xc[:, 0:TT], in_=PB[bx_][:, 0:TT], func=AF.Identity, bias=convb[:, c:c + 1], scale=convw[:, 3, c:c + 1]), reads=[PBb[bx_]] + allc, writes=[rb_[1]])
                k.op("act", lambda e: e.activation(out=y2_[:, 0:TT], in_=PB[by_][:, 0:TT], func=AF.Square), reads=[PBb[by_]], writes=[rb_[10]])
                for w in (2, 1, 0):
                    k.op("dve", lambda e, w=w: e.scalar_tensor_tensor(out=xc[:, 0:TT], in0=xrp[:, w:w + TT], scalar=convw[:, w, c:c + 1], in1=xc[:, 0:TT], op0=ALU.mult, op1=ALU.add), reads=[rb_[0], rb_[1]], writes=[rb_[1]])
                k.op("pool", lambda e: e.tensor_copy(out=xcb, in_=xc[:, 0:TT]), reads=[rb_[1]], writes=[rb_[2]])

            def rnn_G(c, bk, ts):
                bx_, by_, ba_, bi_ = bk
                rs_, rf_, rb_ = RT[ts]
                y2_ = rf_[10]
                xcb = rs_[2].bitcast(BF16)[:, 0:TT]
                k.op("pe", lambda e: e.matmul(PB[ba_][:, 0:TT], lhsT=bd_a[:, c, :], rhs=xcb, start=True, stop=True), reads=[rb_[2]] + allc, writes=[PBb[ba_]])
                k.op("pe", lambda e: e.matmul(PB[bi_][:, 0:TT], lhsT=bd_x[:, c, :], rhs=xcb, start=True, stop=True), reads=[rb_[2]] + allc, writes=[PBb[bi_]])
                k.op("dve", lambda e: e.tensor_scalar(out=y2_[:, 0:TT], in0=y2_[:, 0:TT], scalar1=0.044715, scalar2=1.0, op0=ALU.mult, op1=ALU.add), reads=[rb_[10]], writes=[rb_[10]])
                k.op("dve", lambda e: e.tensor_tensor(out=y2_[:, 0:TT], in0=y2_[:, 0:TT], in1=PB[by_][:, 0:TT], op=ALU.mult), reads=[rb_[10], PBb[by_]], writes=[rb_[10]])

            def rnn_S2(c, bk, ts):
                bx_, by_, ba_, bi_ = bk
                rs_, rf_, rb_ = RT[ts]
                xc, a_, ti_, u_, hs_, y2_ = rf_[1], rf_[3], rf_[5], rf_[6], rf_[8], rf_[10]
                k.op("act", lambda e: e.activation(out=a_[:, 0:TT], in_=PB[ba_][:, 0:TT], func=AF.Tanh, bias=hba[:, c:c + 1], scale=0.5), reads=[PBb[ba_]] + allc, writes=[rb_[3]])
                k.op("act", lambda e: e.activation(out=ti_[:, 0:TT], in_=PB[bi_][:, 0:TT], func=AF.Tanh, bias=hbx[:, c:c + 1], scale=0.5), reads=[PBb[bi_]] + allc, writes=[rb_[5]])
                k.op("act", lambda e: e.activation(out=y2_[:, 0:TT], in_=y2_[:, 0:TT], func=AF.Tanh, scale=GELU_C), reads=[rb_[10]], writes=[rb_[10]])
                k.op("act", lambda e: e.activation(out=a_[:, 0:TT], in_=a_[:, 0:TT], func=AF.Exp, bias=hcl[:, c:c + 1], scale=hcl[:, c:c + 1]), reads=[rb_[3]] + allc, writes=[rb_[3]])
                k.op("pool", lambda e: e.tensor_tensor(out=u_[:, 0:TT], in0=a_[:, 0:TT], in1=a_[:, 0:TT], op=ALU.mult), reads=[rb_[3]], writes=[rb_[6]])
                k.op("dve", lambda e: e.scalar_tensor_tensor(out=ti_[:, 0:TT], in0=ti_[:, 0:TT], scalar=1.0, in1=xc[:, 0:TT], op0=ALU.add, op1=ALU.mult), reads=[rb_[5], rb_[1]], writes=[rb_[5]])
                k.op("act", lambda e: e.activation(out=u_[:, 0:TT], in_=u_[:, 0:TT], func=AF.Sqrt, bias=1.0, scale=-1.0), reads=[rb_[6]], writes=[rb_[6]])
                k.op("dve", lambda e: e.scalar_tensor_tensor(out=y2_[:, 0:TT], in0=y2_[:, 0:TT], scalar=1.0, in1=PB[by_][:, 0:TT], op0=ALU.add, op1=ALU.mult), reads=[rb_[10], PBb[by_]], writes=[rb_[10]])
                k.op("dve", lambda e: e.scalar_tensor_tensor(out=ti_[:, 0:TT], in0=ti_[:, 0:TT], scalar=0.5, in1=u_[:, 0:TT], op0=ALU.mult, op1=ALU.mult), reads=[rb_[5], rb_[6]], writes=[rb_[5]])
                k.op("dve", lambda e: e.tensor_tensor_scan(out=hs_[:, 0:TT], data0=a_[:, 0:TT], data1=ti_[:, 0:TT], initial=hstate[:, c:c + 1], op0=ALU.mult, op1=ALU.add), reads=[rb_[3], rb_[5], hsB[c]], writes=[rb_[8]])
                k.op("pool", lambda e: e.tensor_copy(out=hstate[:, c:c + 1], in_=hs_[:, TT - 1:TT]), reads=[rb_[8]], writes=[hsB[c]])
                k.op("dve", lambda e: e.scalar_tensor_tensor(out=ornnT[:, c, 0:TT], in0=y2_[:, 0:TT], scalar=0.5, in1=hs_[:, 0:TT], op0=ALU.mult, op1=ALU.mult), reads=[rb_[10], rb_[8]], writes=[orB] + reluB)

            bk2 = lambda c: (c % 2, 2 + c % 2, 4 + c % 2, 6 + c % 2)
            BK1 = (4, 5, 6, 7)

            for b in range(BPT):
                prep_hn(0, b)
            if rtB1 is not rtB:
                k.op("dve", lambda e: e.memset(scr[0:1, 0:1], 0.0), writes=[aTB] + rtB1)
            rnn_A(0, bk2(0)); rnn_A(1, bk2(1))
            rnn_S1a(0, bk2(0), 0); rnn_G(0, bk2(0), 0)
            for c in range(NCH):
                if c + 1 < NCH:
                    rnn_S1a(c + 1, bk2(c + 1), (c + 1) % 2); rnn_G(c + 1, bk2(c + 1), (c + 1) % 2)
                rnn_S2(c, bk2(c), c % 2)
                if c + 2 < NCH:
                    rnn_A(c + 2, bk2(c + 2))

            for tt in range(NT):
                has_next = tt + 1 < NT
                for b in range(BPT):
                    t0 = (tt * BPT + b) * PBK
                    k.dma("sp", x1[0:PBK, b, :], jb.x[t0:t0 + PBK, :], writes=[x1B[b]])
                if rtB1 is not rtB:
                    k.op("dve", lambda e: e.memset(scr[0:1, 3:4], 0.0), writes=[aTB] + rtB1)
                for c in range(NCH):
                    wpa, wpaB = getw(("pa", c)); wga, wgaB = getw(("ga", c)); wpr, wprB = getw(("pr", c)); wgb, wgbB = getw(("gb", c))
                    base = 4 * (c % 2)
                    rhs_a = lambda kc: oattnT[:, kc, tt * TT:(tt + 1) * TT]
                    rhs_r = lambda kc: ornnT[:, kc, 0:TT]
                    k.op("pe", multi=True, fn=mm_group(PB[base][:, 0:TT], wpa, rhs_a, TT), reads=[wpaB] + [oaB[(kc, tt)] for kc in range(8)], writes=[PBb[base]])
                    k.op("pe", multi=True, fn=mm_group(PB[base + 1][:, 0:TT], wga, rhs_h, TT), reads=[wgaB, hntB], writes=[PBb[base + 1]])
                    k.op("pe", multi=True, fn=mm_group(PB[base + 2][:, 0:TT], wpr, rhs_r, TT), reads=[wprB, orB], writes=[PBb[base + 2]])
                    k.op("pe", multi=True, fn=mm_group(PB[base + 3][:, 0:TT], wgb, rhs_h, TT), reads=[wgbB, hntB], writes=[PBb[base + 3]])
                    release_all()
                    _, rf_, rb_ = RT[c % 2]
                    tga, tgb, m1, m2 = rf_[0], rf_[1], rf_[3], rf_[5]
                    bga, bgb, bm1, bm2 = rb_[0], rb_[1], rb_[3], rb_[5]
                    k.op("act", lambda e: e.activation(out=tga[:, 0:TT], in_=PB[base + 1][:, 0:TT], func=AF.Tanh, scale=0.5), reads=[PBb[base + 1]], writes=[bga])
                    k.op("act", lambda e: e.activation(out=tgb[:, 0:TT], in_=PB[base + 3][:, 0:TT], func=AF.Tanh, scale=0.5), reads=[PBb[base + 3]], writes=[bgb])
                    k.op("dve", lambda e: e.scalar_tensor_tensor(out=m1[:, 0:TT], in0=tga[:, 0:TT], scalar=1.0, in1=PB[base][:, 0:TT], op0=ALU.add, op1=ALU.mult), reads=[bga, PBb[base]], writes=[bm1])
                    k.op("dve", lambda e: e.scalar_tensor_tensor(out=m2[:, 0:TT], in0=tgb[:, 0:TT], scalar=1.0, in1=PB[base + 2][:, 0:TT], op0=ALU.add, op1=ALU.mult), reads=[bgb, PBb[base + 2]], writes=[bm2])
                    k.op("pool", lambda e: e.tensor_tensor(out=mergedT[:, c, 0:TT], in0=m1[:, 0:TT], in1=m2[:, 0:TT], op=ALU.add), reads=[bm1, bm2], writes=[mgB] + reluB)
                wo = [getw(("out", kc), hold=True) for kc in range(8)]
                for b in range(BPT):
                    for half in range(2):
                        bank = (b * 2 + half) % 8

                        def f(e, b=b, half=half, bank=bank):
                            ins = None
                            for kc in range(8):
                                ins = e.matmul(PB[bank][0:PBK, :], lhsT=mergedT[:, kc, b * PBK:(b + 1) * PBK], rhs=wo[kc][0][:, half * 512:(half + 1) * 512], start=(kc == 0), stop=(kc == 7))
                            return ins
                        k.op("pe", f, multi=True, reads=[mgB] + [w_[1] for w_ in wo], writes=[PBb[bank]])
                        xs_ = x1[0:PBK, b, half * 512:(half + 1) * 512]
                        k.op("dve", lambda e, bank=bank, xs_=xs_: e.scalar_tensor_tensor(out=xs_, in0=PB[bank][0:PBK, :], scalar=0.5, in1=xs_, op0=ALU.mult, op1=ALU.add), reads=[PBb[bank], x1B[b]], writes=[x1B[b]])
                    if b >= 1:
                        norm_transpose(x1[0:PBK, b - 1, :], x1B[b - 1], PBK, gmlp, h2T, h2B, (b - 1) * PBK, 6 + (b - 1) % 2)
                norm_transpose(x1[0:PBK, BPT - 1, :], x1B[BPT - 1], PBK, gmlp, h2T, h2B, (BPT - 1) * PBK, 6 + (BPT - 1) % 2)
                release_all()
                rhs_2 = lambda kc: h2T[:, kc, 0:TT]
                if rtB1 is not rtB:
                    k.op("dve", lambda e: e.memset(scr[0:1, 1:2], 0.0), writes=[aTB] + rtB1)
                pend = {}
                for ev in mlp_events(BPT, has_next):
                    if ev[0] == "up":
                        f_ = ev[1]
                        wu, wuB = getw(("up", f_))
                        bank = f_ % 4
                        k.op("pe", multi=True, fn=mm_group(PB[bank][:, 0:TT], wu, rhs_2, TT), reads=[wuB, h2B], writes=[PBb[bank]])
                        release_all()
                        rl = relu_t[f_ % 4]; rlB = reluB[f_ % 4]
                        k.op("act", lambda e, bank=bank, rl=rl: e.activation(out=rl[:, 0:TT], in_=PB[bank][:, 0:TT], func=AF.Relu), reads=[PBb[bank]], writes=[rlB, mgB])
                        k.op("pool" if f_ % 2 else "dve", lambda e, rl=rl, f_=f_: e.tensor_tensor(out=aT[:, f_, 0:TT], in0=rl[:, 0:TT], in1=rl[:, 0:TT], op=ALU.mult), reads=[rlB], writes=[aTB])
                    elif ev[0] == "down":
                        h_, f_ = ev[1], ev[2]
                        wd, wdB = getw(("down", f_, h_))

                        def f(e, f_=f_, wd=wd):
                            ins = None
                            for b in range(BPT):
                                ins = e.matmul(PB[b][0:PBK, :], lhsT=aT[:, f_, b * PBK:(b + 1) * PBK], rhs=wd[:, 0:512], start=(f_ == 0), stop=(f_ == NFF - 1))
                            return ins
                        k.op("pe", f, multi=True, reads=[wdB, aTB], writes=[PBb[i] for i in range(BPT)])
                        release_all()
                        if f_ == NFF - 1:
                            for b in range(BPT):
                                xs_ = x1[0:PBK, b, h_ * 512:(h_ + 1) * 512]
                                k.op("dve", lambda e, b=b, xs_=xs_: e.tensor_tensor(out=xs_, in0=PB[b][0:PBK, :], in1=xs_, op=ALU.add), reads=[PBb[b], x1B[b]], writes=[x1B[b]])
                    elif ev[0] == "prepA":
                        pend[ev[1]] = prep_hn(tt + 1, ev[1], split=True)
                    elif ev[0] == "prepB":
                        pend.pop(ev[1])()
                    elif ev[0] == "rnnA":
                        rnn_A(ev[1], BK1); rnn_S1a(ev[1], BK1, 0)
                    elif ev[0] == "rnnG":
                        rnn_G(ev[1], BK1, 0); rnn_S2(ev[1], BK1, 0)
                for b in range(BPT):
                    r = norm_rstd(x1[0:PBK, b, :], x1B[b], PBK)
                    si = alloc_stg()
                    k.op("dve", lambda e, b=b, si=si, r=r: e.scalar_tensor_tensor(out=stg[0:PBK, si, :], in0=x1[0:PBK, b, :], scalar=r, in1=gf_bc[0:PBK, :], op0=ALU.mult, op1=ALU.mult), reads=[x1B[b], ssB] + allc, writes=[stgB[si]])
                    t0 = (tt * BPT + b) * PBK
                    defer_store(si, lambda t0=t0, si=si: k.dma("sp", jb.y[t0:t0 + PBK, :], stg[0:PBK, si, :], reads=[stgB[si]]))
            for w_ in range(3):
                k.dma("sp", jb.co[w_].rearrange("(c p) -> p c", p=128), convhist[:, :, w_], reads=cvB)
            k.dma("sp", jb.ho.rearrange("(c p) -> p c", p=128), hstate[:], reads=hsB)

        def run_job(ji, jb):
            ckpt(1)
            Tn = jb.T; H = jb.HIST
            PBK = min(128, Tn)
            NNB = Tn // PBK; NHB = H // 128; NKB = NHB + NNB
            TT = min(512, Tn); NT = Tn // TT; BPT = TT // PBK
            QS = min(256, Tn); NQS = TT // QS
            k.barrier()
            oaB = {}
            hnB = [Buf("hn%d" % b) for b in range(NNB)]
            vbfB = [Buf("vbf%d" % b) for b in range(NKB)]
            vbfB2 = [Buf("vbfo%d" % b) for b in range(NKB)]
            lfB = Buf("lf"); cB2 = Buf("c")
            k.op("pool", lambda e: e.memset(vbf[:, 0:NKB, :, 64:128], 1.0), writes=vbfB)
            for b in range(NNB):
                si = alloc_stg()
                k.dma("sp", stg[0:PBK, si, :], jb.x[b * PBK:(b + 1) * PBK, :], writes=[stgB[si]])
                norm_transpose(stg[0:PBK, si, :], stgB[si], PBK, gmix, hnT, hnB[b], b * PBK, b % 2)
            ckpt(2)
            if H:
                k.dma("sp", lf[:, 0:NHB, :], jb.clf.rearrange("(b p) h -> p b h", p=128), writes=[lfB])
            zfv = PB[2][:, 0:NKBMAX * NH].rearrange("p (b h) -> p b h", h=NH)
            for b in range(NNB):
                def f(e, b=b):
                    ins = None
                    for kc in range(8):
                        ins = e.matmul(zfv[0:PBK, b, :], lhsT=hnT[:, kc, b * PBK:(b + 1) * PBK], rhs=wf[:, kc, :], start=(kc == 0), stop=(kc == 7))
                    return ins
                k.op("pe", f, multi=True, reads=[hnB[b]] + allc, writes=[PBb[2]])
            ckpt(2.2)
            lfn = lf[0:PBK, NHB:NKB, :]
            k.op("dve", lambda e: e.tensor_tensor(out=zf[0:PBK, 0:NNB, :], in0=zfv[0:PBK, 0:NNB, :], in1=bf_bc[0:PBK, :].unsqueeze(1).broadcast_to([PBK, NNB, NH]), op=ALU.add), reads=[PBb[2]] + allc, writes=[cB2])
            k.op("act", lambda e: e.activation(out=zf[0:PBK, 0:NNB, :], in_=zf[0:PBK, 0:NNB, :], func=AF.Exp, scale=-1.0), reads=[cB2], writes=[cB2])
            k.op("act", lambda e: e.activation(out=zf[0:PBK, 0:NNB, :], in_=zf[0:PBK, 0:NNB, :], func=AF.Ln, bias=1.0), reads=[cB2], writes=[cB2])
            k.op("dve", lambda e: e.tensor_scalar(out=lfn, in0=zf[0:PBK, 0:NNB, :], scalar1=-1.0, scalar2=None, op0=ALU.mult), reads=[cB2], writes=[lfB])
            ckpt(2.4)
            k.dma("sp", jb.lo.rearrange("(b p) h -> p b h", p=PBK), lfn, reads=[lfB])
            ckpt(2.5)
            kn_of = lambda i: 128 if i < NHB else PBK
            cv_ = PB[3][:, 0:NKBMAX * NH].rearrange("p (b h) -> p b h", h=NH)
            rv_ = PB[5][:, 0:(NKBMAX + 1) * NH].rearrange("p (b h) -> p b h", h=NH)

            def fc(e):
                ins = None
                for i in range(NKB):
                    kn = kn_of(i)
                    ins = e.matmul(cv_[0:kn, i, :], lhsT=tri_f[0:kn, 0:kn], rhs=lf[0:kn, i, :], start=True, stop=(i == 0))
                    for i2 in range(i):
                        k2 = kn_of(i2)
                        ins = e.matmul(cv_[0:kn, i, :], lhsT=ones_f[0:k2, 0:kn], rhs=lf[0:k2, i2, :], start=False, stop=(i2 == i - 1))
                for m in range(1, NKB + 1):
                    for i2 in range(m):
                        k2 = kn_of(i2)
                        ins = e.matmul(rv_[:, m, :], lhsT=ones_f[0:k2, :], rhs=lf[0:k2, i2, :], start=(i2 == 0), stop=(i2 == m - 1))
                ins = e.matmul(PB[3][0:8, 504:512], lhsT=ident_bf[:, 0:8], rhs=ident_bf[:, 0:8], start=True, stop=True)
                return ins
            k.op("pe", fc, multi=True, reads=[lfB] + allc, writes=[PBb[3], PBb[5]])
            ckpt(2.6)
            k.op("dve", lambda e: e.tensor_scalar(out=csb[:, 0:NKB, :], in0=cv_[:, 0:NKB, :], scalar1=-1.0, scalar2=None, op0=ALU.mult), reads=[PBb[3]], writes=[cB2])
            ckpt(2.7)
            k.op("dve", lambda e: e.memset(rsb[:, 0:1, :], 0.0), writes=[cB2])
            ckpt(2.8)
            k.op("act", lambda e: e.copy(out=rsb[:, 1:NKB + 1, :], in_=rv_[:, 1:NKB + 1, :]), reads=[PBb[5]], writes=[cB2])
            def ref_index(qb0):
                return NHB + qb0 + 1 if QS == 256 else NHB + qb0
            NQ = Tn // QS

            ckpt(3)
            pst = {}

            def produce(j):
                bi = j % 2
                qTj, kTj = qT[bi], kT[bi]
                qB = Buf("qT"); kBs = [Buf("kT%d" % t) for t in range((H + Tn + 511) // 512 + 1)]
                kb_of = lambda col: kBs[col // 512]
                bias = bias_ab[bi]; biasB = Buf("bias")
                pst[j] = (qTj, kTj, qB, kb_of, bias, biasB)
                def fb(e):
                    ins = None
                    for hh in range(2):
                        h = 2 * j + hh
                        for q in range(NQ):
                            m = ref_index(q * (QS // PBK))
                            ins = e.tensor_scalar(out=bias[:, hh, 0:NKB, q:q + 1], in0=csb[:, 0:NKB, h:h + 1], scalar1=rsb[:, m, h:h + 1], scalar2=None, op0=ALU.add)
                    return ins
                k.op("dve", fb, multi=True, reads=[cB2], writes=[biasB])
                ckpt(3.05)
                if H:
                    xi = xn_i[0] % 2; xn_i[0] += 1
                    kst = xn[:, xi, :].rearrange("p (b m) -> p b m", m=128)
                    k.dma("pool", kst[:, 0:NHB, :], jb.ck[:, j * 128:(j + 1) * 128].rearrange("(b p) m -> p b m", p=128), writes=[xnB[xi]])
                    bank = 4 + (j % 2)
                    pbh = PB[bank][:].bitcast(BF16)
                    pbv3 = pbh.rearrange("p (b t) -> p b t", t=128)

                    def trh(e):
                        ins = None
                        for hb in range(NHB):
                            ins = e.transpose(out=pbv3[:, hb, :], in_=kst[:, hb, :], identity=ident_bf[:])
                        return ins
                    k.op("pe", trh, multi=True, reads=[xnB[xi]] + allc, writes=[PBb[bank]])
                    k.op("act", lambda e: e.copy(out=kTj[:, 0:H], in_=pbh[:, 0:H]), reads=[PBb[bank]], writes=list({id(kb_of(c_)): kb_of(c_) for c_ in range(0, H, 512)}.values()))
                    cvv = jb.cv[:, j * 128:(j + 1) * 128].rearrange("(b p) m -> p b m", p=128)
                    k.dma("pool", vbf[:, 0:NHB, j, 0:64], cvv[:, :, 0:64], writes=vbfB[0:NHB])
                    k.dma("pool", vbf[:, 0:NHB, j, 128:192], cvv[:, :, 64:128], writes=vbfB2[0:NHB])
                wq, wqB = getw(("q", j)); wk, wkB = getw(("k", j)); wv, wvB = getw(("v", j))
                ckpt(3.1)
                for tt in range(NT):
                    cols = slice(tt * TT, (tt + 1) * TT)
                    hb_ = hnB[tt * BPT:(tt + 1) * BPT]
                    rhs = lambda kc, cols=cols: hnT[:, kc, cols]
                    k.op("pe", multi=True, fn=mm_group(PB[0][:, 0:TT], wq, rhs, TT), reads=[wqB] + hb_, writes=[PBb[0]])
                    k.op("dve", lambda e, cols=cols: e.tensor_scalar(out=qTj[:, cols], in0=PB[0][:, 0:TT], scalar1=0.125, scalar2=None, op0=ALU.mult), reads=[PBb[0]], writes=[qB])
                    ckpt(3.15)
                    k.op("pe", multi=True, fn=mm_group(PB[1][:, 0:TT], wk, rhs, TT), reads=[wkB] + hb_, writes=[PBb[1]])
                    kvB = [Buf("kvf0"), Buf("kvf1")]
                    k.op("dve", lambda e: e.tensor_copy(out=kTj[:, H + tt * TT:H + (tt + 1) * TT], in_=PB[1][:, 0:TT]), reads=[PBb[1]], writes=[kb_of(H + tt * TT)])
                    k.op("dve", lambda e: e.tensor_copy(out=kvf[0][:, 0:TT], in_=PB[1][:, 0:TT]), reads=[PBb[1]], writes=[kvB[0]])
                    ckpt(3.2)
                    k.op("pe", multi=True, fn=mm_group(PB[2][:, 0:TT], wv, rhs, TT), reads=[wvB] + hb_, writes=[PBb[2]])
                    k.op("dve", lambda e: e.tensor_copy(out=kvf[1][:, 0:TT], in_=PB[2][:, 0:TT]), reads=[PBb[2]], writes=[kvB[1]])
                    ckpt(3.3)
                    ktv = PB[3][:, :].rearrange("p (b m) -> p b m", m=128)
                    vtv = PB[4][:, :].rearrange("p (b m) -> p b m", m=128)

                    def ftr(e, src, dstv):
                        ins = None
                        for b in range(BPT):
                            ins = e.transpose(out=dstv[0:PBK, b, :], in_=src[:, b * PBK:(b + 1) * PBK], identity=ident_f[:])
                        return ins
                    k.op("pe", lambda e: ftr(e, kvf[0], ktv), multi=True, reads=[kvB[0]] + allc, writes=[PBb[3]])
                    k.op("pe", lambda e: ftr(e, kvf[1], vtv), multi=True, reads=[kvB[1]] + allc, writes=[PBb[4]])
                    ckpt(3.4)
                    sk = alloc_stg()
                    sv = alloc_stg()
                    skv = stg[:, sk, 0:512].rearrange("p (b m) -> p b m", m=128)
                    svv = stg[:, sv, 0:512].rearrange("p (b m) -> p b m", m=128)
                    k.op("dve", lambda e: e.tensor_copy(out=skv[0:PBK, 0:BPT, :], in_=ktv[0:PBK, 0:BPT, :]), reads=[PBb[3]], writes=[stgB[sk]])
                    k.op("dve", lambda e: e.tensor_copy(out=svv[0:PBK, 0:BPT, :], in_=vtv[0:PBK, 0:BPT, :]), reads=[PBb[4]], writes=[stgB[sv]])
                    nb0 = tt * BPT
                    vdst = vbf[0:PBK, NHB + nb0:NHB + nb0 + BPT, j, :]
                    k.op("dve", lambda e: e.tensor_copy(out=vdst[:, :, 0:64], in_=vtv[0:PBK, 0:BPT, 0:64]), reads=[PBb[4]], writes=vbfB[NHB + nb0:NHB + nb0 + BPT])
                    k.op("dve", lambda e: e.tensor_copy(out=vdst[:, :, 128:192], in_=vtv[0:PBK, 0:BPT, 64:128]), reads=[PBb[4]], writes=vbfB[NHB + nb0:NHB + nb0 + BPT])
                    ckpt(3.5)
                    kdst = jb.ko[tt * TT:(tt + 1) * TT, j * 128:(j + 1) * 128].rearrange("(b p) m -> p b m", p=PBK)
                    vdsto = jb.vo[tt * TT:(tt + 1) * TT, j * 128:(j + 1) * 128].rearrange("(b p) m -> p b m", p=PBK)
                    defer_store(sk, lambda kdst=kdst, skv=skv, sk=sk: k.dma("sp", kdst, skv[0:PBK, 0:BPT, :], reads=[stgB[sk]]))
                    defer_store(sv, lambda vdsto=vdsto, svv=svv, sv=sv: k.dma("sp", vdsto, svv[0:PBK, 0:BPT, :], reads=[stgB[sv]]))
                release_all()

            def attend(j):
                qTj, kTj, qB, kb_of, bias, biasB = pst[j]
                items = []
                for tt in range(NT):
                    oaB[(j, tt)] = Buf("oa")
                    for hh in range(2):
                        last_i = NHB + (tt + 1) * BPT - 1
                        for i in range(last_i + 1):
                            items.append((tt, hh, i, last_i))
                LA = 2

                def geom(n):
                    tt, hh, i, last_i = items[n]
                    kn = kn_of(i)
                    kcol = i * 128 if i < NHB else H + (i - NHB) * PBK
                    nbi = i - NHB
                    dq = max(0, nbi - tt * BPT)
                    return tt, hh, i, last_i, kn, kcol, nbi, dq * PBK, slice(hh * 64, (hh + 1) * 64), tt * TT

                def emit_qk(n):
                    tt, hh, i, last_i, kn, kcol, nbi, qlo, prow, q0 = geom(n)
                    sbank = 3 + (n % 3)
                    k.op("pe", lambda e: e.matmul(PB[sbank][0:kn, qlo:TT], lhsT=kTj[prow, kcol:kcol + kn], rhs=qTj[prow, q0 + qlo:q0 + TT], start=True, stop=True), reads=[kb_of(kcol), qB], writes=[PBb[sbank]])

                def emit_rest(n):
                    tt, hh, i, last_i, kn, kcol, nbi, qlo, prow, q0 = geom(n)
                    sbank = 3 + (n % 3)
                    obank = 6 + hh
                    pt = pT[n % 4]; ptb = pTB[n % 4]
                    for sq in range(NQS):
                        c0 = max(qlo, sq * QS); c1 = (sq + 1) * QS
                        if c0 >= c1:
                            continue
                        qidx = (q0 + sq * QS) // QS
                        k.op("act", lambda e, c0=c0, c1=c1, qidx=qidx: e.activation(out=pt[0:kn, c0:c1], in_=PB[sbank][0:kn, c0:c1], func=AF.Exp, bias=bias[0:kn, hh, i, qidx:qidx + 1]), reads=[PBb[sbank], biasB], writes=[ptb])
                    if nbi >= tt * BPT:
                        k.op("pool", lambda e: e.tensor_tensor(out=pt[0:kn, qlo:qlo + PBK], in0=pt[0:kn, qlo:qlo + PBK], in1=mask_bf[0:kn, 0:PBK], op=ALU.mult), reads=[ptb] + allc, writes=[ptb])
                    k.op("pe", lambda e: e.matmul(PB[obank][:, qlo:TT], lhsT=vbf[0:kn, i, j, hh * 64:hh * 64 + 128], rhs=pt[0:kn, qlo:TT], start=(i == 0), stop=(i == last_i)), reads=[ptb, vbfB[i], vbfB2[i]], writes=[PBb[obank]])
                    if i == last_i:
                        drow = slice((1 - hh) * 64, (2 - hh) * 64)
                        rd = rden2[hh]; rdB = rdenB2[hh]
                        k.op("dve", lambda e: e.reciprocal(out=rd[drow, 0:TT], in_=PB[obank][drow, 0:TT]), reads=[PBb[obank]], writes=[rdB])
                        k.op("dve", lambda e: e.tensor_tensor(out=oattnT[prow, j, q0:q0 + TT], in0=PB[obank][prow, 0:TT], in1=rd[drow, 0:TT], op=ALU.mult), reads=[PBb[obank], rdB], writes=[oaB[(j, tt)]])

                for n in range(len(items) + LA):
                    if n < len(items):
                        emit_qk(n)
                    if n >= LA:
                        emit_rest(n - LA)
            produce(0)
            for j in range(NCH):
                if j + 1 < NCH:
                    produce(j + 1)
                attend(j)
            flush_stores()
            k.barrier()
            ckpt(5)
            tail(jb, oaB)
            flush_stores()

        try:
            ckpt(0)
            for ji, jb in enumerate(jobs):
                run_job(ji, jb)
        except _Stop:
            pass
        for g_ in range(5):
            if cast_cnt[g_] > 0:
                sp.wait_ge(castsems[g_], cast_cnt[g_])
        k.finish()
    return nc


_NC_CACHE = {}


def _run(inputs, n_cores, NPS, T, NSS, TS, HIST, trace=False):
    key = (NPS, T, NSS, TS, HIST)
    if key not in _NC_CACHE:
        _NC_CACHE[key] = build(NPS, T, NSS, TS, HIST)
    nc = _NC_CACHE[key]
    f = lambda a: np.ascontiguousarray(np.asarray(a, dtype=np.float32))
    i = inputs
    shared = {
        "g_mix": f(i["norm_mix_g"][0]), "w_in": f(i["w_in"][0]), "b_f": f(i["b_f"][0]), "conv_w": f(i["conv_w"][0]),
        "conv_b": f(i["conv_b"][0]), "w_rg_a": f(i["w_rg_a"][0]), "b_rg_a": f(i["b_rg_a"][0]), "w_rg_x": f(i["w_rg_x"][0]),
        "b_rg_x": f(i["b_rg_x"][0]), "rg_lambda": f(i["rg_lambda"][0]), "w_pa": f(i["w_proj_attn"][0]),
        "w_pr": f(i["w_proj_rnn"][0]), "w_out": f(i["w_out"][0]), "g_mlp": f(i["norm_mlp_g"][0]), "w_up": f(i["w_up"][0]),
        "w_down": f(i["w_down"][0]), "g_fin": f(i["norm_final_g"]),
    }
    in_maps = []
    for c in range(n_cores):
        m = dict(shared)
        ps = slice(c * NPS, (c + 1) * NPS); ss = slice(c * NSS, (c + 1) * NSS)
        m["xp"] = f(i["x_prompt"][ps]); m["xs"] = f(i["x_sample"][ss])
        m["ck"] = f(i["cache_k"][0, ss]).reshape(NSS, HIST, D); m["cv"] = f(i["cache_v"][0, ss]).reshape(NSS, HIST, D)
        m["clf"] = f(i["cache_logf"][0, ss]); m["sconv"] = f(i["state_conv"][0, ss]); m["sh"] = f(i["state_rglru"][0, ss])
        in_maps.append(m)
    res = run_bass_kernel_spmd(nc, in_maps, core_ids=list(range(n_cores)), trace=trace)
    R = res.results
    cat = lambda name: np.concatenate([np.asarray(r[name]) for r in R], axis=0)
    BP = n_cores * NPS; BS = n_cores * NSS
    out = (
        cat("yp"), cat("ys"),
        cat("kp").reshape(1, BP, T, NH, DH), cat("vp").reshape(1, BP, T, NH, DH), cat("lp").reshape(1, BP, T, NH),
        cat("cp").reshape(1, BP, 3, D), cat("hp").reshape(1, BP, D),
        cat("ks").reshape(1, BS, TS, NH, DH), cat("vs").reshape(1, BS, TS, NH, DH), cat("ls").reshape(1, BS, TS, NH),
        cat("cs").reshape(1, BS, 3, D), cat("hs").reshape(1, BS, D),
    )
    return tuple(np.ascontiguousarray(o, dtype=np.float32) for o in out), res


def kernel(**inputs):
    out, _ = _run(inputs, 8, 4, 2048, 4, 64, 1024)
    return out
```

```python
import numpy as np
from contextlib import ExitStack
import concourse.bass as bass
import concourse.mybir as mybir
from concourse.bass_utils import run_bass_kernel_spmd

F32 = mybir.dt.float32
BF16 = mybir.dt.bfloat16
ALU = mybir.AluOpType
AF = mybir.ActivationFunctionType

D = 1024
NH = 16
DH = 64
NCH = 8
DFF = 4096
NFF = 32
DIN = 7184
C_Q, C_K, C_V, C_F, C_XR, C_YR, C_GA, C_GB = 0, 1024, 2048, 3072, 3088, 4112, 5136, 6160
EPS = 1e-6
GELU_C = 0.7978845608028654


class Tok:
    __slots__ = ("sem", "val", "eng")

    def __init__(self, sem, val, eng):
        self.sem, self.val, self.eng = sem, val, eng


class Buf:
    __slots__ = ("name", "w", "r", "excl")

    def __init__(self, name, excl=False):
        self.name, self.w, self.r, self.excl = name, None, [], excl


class Eng:
    def __init__(self, name, e, sem):
        self.name, self.e, self.sem, self.cnt, self.seen = name, e, sem, 0, {}


class K:
    def __init__(self, nc, es, ndma=40):
        self.nc = nc
        self.engs = {}
        for name, e in (("pe", nc.tensor), ("act", nc.scalar), ("dve", nc.vector), ("pool", nc.gpsimd), ("sp", nc.sync)):
            self.engs[name] = Eng(name, e, es.enter_context(nc.semaphore("sem_" + name)))
        self.dpools = {"sp": [[es.enter_context(nc.semaphore("dsp%d" % i)), 0] for i in range(28)],
                       "pool": [[es.enter_context(nc.semaphore("dpl%d" % i)), 0] for i in range(12)]}
        self.dsems = self.dpools["sp"] + self.dpools["pool"]
        self.dnext = {"sp": 0, "pool": 0}
        self.dma_toks = []

    def _wait(self, E, t, raw, is_dma=False):
        if t is None:
            return
        if t.eng is E and not is_dma:
            if E.name in ("pe", "sp"):
                return
            if not raw or t.val < E.cnt - 2:
                return
        key = id(t.sem)
        if E.seen.get(key, 0) >= t.val:
            return
        E.seen[key] = t.val
        self._pw.append((t.sem, t.val))

    def _deps(self, E, reads, writes, is_dma=False):
        self._pw = []
        for b in reads:
            self._wait(E, b.w, True, is_dma)
            if b.excl:
                for t in b.r:
                    self._wait(E, t, False, is_dma)
        for b in writes:
            self._wait(E, b.w, False, is_dma)
            for t in b.r:
                self._wait(E, t, False, is_dma)

    def _commit(self, tok, reads, writes):
        for b in reads:
            b.r = [t for t in b.r if t.eng is not tok.eng or tok.eng is None] + [tok]
        for b in writes:
            b.w = tok
            b.r = []

    def op(self, en, fn, reads=(), writes=(), multi=False):
        E = self.engs[en]
        self._deps(E, reads, writes)
        pw = {}
        for sem_, val_ in self._pw:
            pw[id(sem_)] = (sem_, max(val_, pw.get(id(sem_), (None, 0))[1]))
        pw = list(pw.values())
        att = pw.pop() if (pw and multi is not True and ATTACH) else None
        for sem_, val_ in pw:
            E.e.wait_ge(sem_, val_)
        ins = fn(E.e)
        first = ins
        if isinstance(ins, tuple):
            first, ins = ins
        if att is not None:
            first._wait_ge(att[0], att[1])
        E.cnt += 1
        ins.then_inc(E.sem, 1)
        tok = Tok(E.sem, E.cnt, E)
        self._commit(tok, reads, writes)
        return tok

    def dma(self, en, out, in_, reads=(), writes=()):
        E = self.engs[en]
        self._deps(E, reads, writes, True)
        for sem_, val_ in self._pw:
            E.e.wait_ge(sem_, val_)
        pl = self.dpools[en]
        slot = pl[self.dnext[en]]
        self.dnext[en] = (self.dnext[en] + 1) % len(pl)
        if slot[1] > 0:
            key = id(slot[0])
            if E.seen.get(key, 0) < slot[1]:
                E.e.wait_ge(slot[0], slot[1])
                E.seen[key] = slot[1]
        E.e.dma_start(out=out, in_=in_).then_inc(slot[0], 16)
        slot[1] += 16
        tok = Tok(slot[0], slot[1], None)
        self._commit(tok, reads, writes)
        self.dma_toks.append(tok)
        return tok

    def barrier(self):
        cur = [(E.sem, E.cnt) for E in self.engs.values() if E.cnt > 0]
        dm = [(s[0], s[1]) for s in self.dsems if s[1] > 0]
        for E in self.engs.values():
            for sem, val in cur + dm:
                if sem is E.sem:
                    continue
                key = id(sem)
                if E.seen.get(key, 0) < val:
                    E.e.wait_ge(sem, val)
                    E.seen[key] = val

    def finish(self):
        E = self.engs["sp"]
        for s in self.dsems:
            if s[1] > 0 and E.seen.get(id(s[0]), 0) < s[1]:
                E.e.wait_ge(s[0], s[1])
        for F in self.engs.values():
            if F is not E and F.cnt > 0:
                E.e.wait_ge(F.sem, F.cnt)


class Job:
    pass


import os as _os2
ATTACH = _os2.environ.get("KATTACH", "1") == "1"


class _Stop(Exception):
    pass


import os as _os
_STAGE = float(_os.environ.get('KSTAGE', '99'))


def ckpt(n):
    if n > _STAGE:
        raise _Stop()


def build(NPS, T, NSS, TS, HIST):
    nc = bass.Bass("TRN2", target_bir_lowering=False)
    dt = nc.dram_tensor

    def din(name, shape):
        return dt(name, shape, F32, kind="ExternalInput").ap()

    def dout(name, shape):
        return dt(name, shape, F32, kind="ExternalOutput").ap()

    xp = din("xp", [NPS, T, D]); xs = din("xs", [NSS, TS, D])
    ck = din("ck", [NSS, HIST, D]); cv = din("cv", [NSS, HIST, D]); clf = din("clf", [NSS, HIST, NH])
    sconv = din("sconv", [NSS, 3, D]); sh = din("sh", [NSS, D])
    g_mix = din("g_mix", [D]); w_in = din("w_in", [D, DIN]); b_f = din("b_f", [NH])
    conv_w = din("conv_w", [4, D]); conv_b = din("conv_b", [D])
    w_rg_a = din("w_rg_a", [16, 64, 64]); b_rg_a = din("b_rg_a", [D])
    w_rg_x = din("w_rg_x", [16, 64, 64]); b_rg_x = din("b_rg_x", [D])
    rg_lambda = din("rg_lambda", [D])
    w_pa = din("w_pa", [D, D]); w_pr = din("w_pr", [D, D]); w_out = din("w_out", [D, D])
    g_mlp = din("g_mlp", [D]); w_up = din("w_up", [D, DFF]); w_down = din("w_down", [DFF, D]); g_fin = din("g_fin", [D])

    yp = dout("yp", [NPS, T, D]); ys = dout("ys", [NSS, TS, D])
    kp = dout("kp", [NPS, T, D]); vp = dout("vp", [NPS, T, D]); lp = dout("lp", [NPS, T, NH])
    cp = dout("cp", [NPS, 3, D]); hp = dout("hp", [NPS, D])
    ks = dout("ks", [NSS, TS, D]); vs = dout("vs", [NSS, TS, D]); ls = dout("ls", [NSS, TS, NH])
    cs = dout("cs", [NSS, 3, D]); hs_o = dout("hs", [NSS, D])

    GR = {"q": C_Q, "k": C_K, "v": C_V, "xr": C_XR, "yr": C_YR, "ga": C_GA, "gb": C_GB}
    wst = {g: dt("wst_" + g, [NCH, 128, 8, 128], BF16, kind="Internal").ap() for g in GR}
    wst["pa"] = dt("wst_pa", [NCH, 128, 8, 128], BF16, kind="Internal").ap()
    wst["pr"] = dt("wst_pr", [NCH, 128, 8, 128], BF16, kind="Internal").ap()
    wst["up"] = dt("wst_up", [NFF, 128, 8, 128], BF16, kind="Internal").ap()
    wst["out"] = dt("wst_out", [8, 128, 1024], BF16, kind="Internal").ap()
    wst["down"] = dt("wst_down", [NFF, 128, 1024], BF16, kind="Internal").ap()

    jobs = []
    for s in range(NPS):
        j = Job(); j.T = T; j.HIST = 0; j.x = xp[s]; j.y = yp[s]; j.ko = kp[s]; j.vo = vp[s]; j.lo = lp[s]
        j.co = cp[s]; j.ho = hp[s]; jobs.append(j)
    for s in range(NSS):
        j = Job(); j.T = TS; j.HIST = HIST; j.x = xs[s]; j.y = ys[s]; j.ko = ks[s]; j.vo = vs[s]; j.lo = ls[s]
        j.co = cs[s]; j.ho = hs_o[s]; j.ck = ck[s]; j.cv = cv[s]; j.clf = clf[s]; j.sconv = sconv[s]; j.sh = sh[s]
        jobs.append(j)

    TMAX = max(T, TS + HIST)
    NKBMAX = max(T // 128, HIST // 128 + 1)
    NS = 11

    with ExitStack() as es:
        es.enter_context(nc.allow_non_contiguous_dma(reason="small strided parameter / state vectors"))
        sb = lambda name, shape, d=F32: es.enter_context(nc.sbuf_tensor(name, shape, d))
        ident_bf = sb("ident_bf", [128, 128], BF16); ident_f = sb("ident_f", [128, 128])
        mask_bf = sb("mask_bf", [128, 128], BF16); tri_f = sb("tri_f", [128, 128]); ones_f = sb("ones_f", [128, 128])
        gmix = sb("gmix", [128, 8]); gmlp = sb("gmlp", [128, 8]); convw = sb("convw", [128, 4, 8]); convb = sb("convb", [128, 8])
        scr = sb("scr", [128, 8]); hcl2 = sb("hcl2", [128, 8])
        hba = sb("hba", [128, 8]); hbx = sb("hbx", [128, 8]); hcl = sb("hcl", [128, 8]); lam = sb("lam", [128, 8])
        gf_bc = sb("gf_bc", [128, D]); bf_bc = sb("bf_bc", [128, NH])
        bd_a = sb("bd_a", [128, 8, 128], BF16); bd_x = sb("bd_x", [128, 8, 128], BF16); wf = sb("wf", [128, 8, NH], BF16)
        oattnT = sb("oattnT", [128, 8, T], BF16)
        wring = sb("wring", [128, NS, 1024], BF16)
        NSTG = 5
        stg = sb("stg", [128, NSTG, D])
        xn = sb("xn", [128, 2, D], BF16)
        junk = sb("junk", [128, D], BF16)
        ss = sb("ss", [128, 8]); rstd = sb("rstd", [128, 8])
        hstate = sb("hstate", [128, 8]); convhist = sb("convhist", [128, 8, 3])
        lf = sb("lf", [128, NKBMAX, NH]); zf = sb("zf", [128, NKBMAX, NH])
        csb = sb("csb", [128, NKBMAX, NH]); rsb = sb("rsb", [128, NKBMAX + 1, NH])
        bias_ab = [sb("bias_a", [128, 2, NKBMAX, NKBMAX]), sb("bias_b", [128, 2, NKBMAX, NKBMAX])]
        rden = sb("rden", [128, 512]); rden_b = sb("rden_b", [128, 512])
        A_HN = 8 * T; A_VBF = NKBMAX * 8 * 192; A_QK = 2 * T + 2 * TMAX; A_PT = 4 * 512; A_KVF = 2 * 512 * 2
        att_elems = A_HN + A_VBF + A_QK + A_PT + A_KVF
        TT0 = min(512, T)
        USED_T = (0, 1, 2, 3, 5, 6, 8, 10)
        tail_elems = 8 * TT0 + 2 * 4 * D + 2 * len(USED_T) * 520 + 8 * TT0 + 8 * TT0 + 8 * TT0 + 32 * TT0
        arena = sb("arena", [128, max(att_elems, tail_elems)], BF16)
        off = [0]

        def carve(n, d=BF16):
            a = arena[:, off[0]:off[0] + n]
            off[0] += n
            return a if d == BF16 else a.bitcast(F32)

        hnT = carve(A_HN).rearrange("p (c t) -> p c t", c=8)
        vbf = carve(A_VBF).rearrange("p (b j s) -> p b j s", j=8, s=192)
        qT = [carve(T), carve(T)]; kT = [carve(TMAX), carve(TMAX)]
        pT = [carve(512) for _ in range(4)]
        kvf = [carve(1024, F32), carve(1024, F32)]
        off[0] = 0
        hnt = carve(8 * TT0).rearrange("p (c t) -> p c t", c=8)
        x1 = carve(2 * 4 * D, F32).rearrange("p (b n) -> p b n", b=4)
        rt = [carve(2 * 520, F32) if i in USED_T else None for i in range(11)]
        h2T = carve(8 * TT0).rearrange("p (c t) -> p c t", c=8)
        mergedT = carve(8 * TT0).rearrange("p (c t) -> p c t", c=8)
        ornnT = carve(8 * TT0).rearrange("p (c t) -> p c t", c=8)
        aT_off = off[0]
        aT = carve(32 * TT0).rearrange("p (f t) -> p f t", f=32)
        off[0] = aT_off
        rt1 = [carve(2 * 520, F32) if i in USED_T else None for i in range(11)] if 32 * TT0 >= 11 * 1040 else None
        PB = [es.enter_context(nc.psum_tensor("pb%d" % i, [128, 512], F32)) for i in range(8)]
        PBb = [Buf("pb%d" % i, excl=True) for i in range(8)]

        if _os.environ.get('KDEBUG'):
            print('SBUF bytes remaining', nc.sbuf_bytes_remaining, 'att_elems', att_elems, 'tail_elems', tail_elems)
        k = K(nc, es)
        castsems = [es.enter_context(nc.semaphore("castsem%d" % i)) for i in range(5)]
        cast_cnt = [0, 0, 0, 0, 0]
        _GRP = {"xr": 2, "yr": 2, "pa": 2, "pr": 2, "ga": 2, "gb": 2, "out": 3, "up": 3, "down": 4}
        grp_of = lambda key: (0 if key[1] < 2 else 1) if key[0] in ("q", "k", "v") else _GRP[key[0]]
        block = es.enter_context(nc.Block())
        pe, act, dve, pool, sp = nc.tensor, nc.scalar, nc.vector, nc.gpsimd, nc.sync

        ncast = 0
        cast_done = {}

        def cast(key, out, in_):
            g_ = grp_of(key)
            pool.dma_start(out=out, in_=in_).then_inc(castsems[g_], 16)
            cast_cnt[g_] += 16

        def cast_in(g):
            for c in range(NCH):
                cast((g, c), wst[g][c], w_in[:, GR[g] + c * 128:GR[g] + (c + 1) * 128].rearrange("(kc p) m -> p kc m", p=128))

        cB = {n: Buf(n) for n in ("ident", "vec", "bd", "wf")}
        k.op("pool", lambda e: e.memset(ident_bf[:], 1.0), writes=[cB["ident"]])
        k.op("pool", lambda e: e.affine_select(out=ident_bf[:], in_=ident_bf[:], pattern=[[-1, 128]], compare_op=ALU.is_equal, fill=0.0, base=0, channel_multiplier=1), reads=[cB["ident"]], writes=[cB["ident"]])
        k.op("pool", lambda e: e.memset(ident_f[:], 1.0), writes=[cB["ident"]])
        k.op("pool", lambda e: e.affine_select(out=ident_f[:], in_=ident_f[:], pattern=[[-1, 128]], compare_op=ALU.is_equal, fill=0.0, base=0, channel_multiplier=1), reads=[cB["ident"]], writes=[cB["ident"]])
        k.op("pool", lambda e: e.memset(mask_bf[:], 1.0), writes=[cB["ident"]])
        k.op("pool", lambda e: e.affine_select(out=mask_bf[:], in_=mask_bf[:], pattern=[[1, 128]], compare_op=ALU.is_ge, fill=0.0, base=0, channel_multiplier=-1), reads=[cB["ident"]], writes=[cB["ident"]])
        k.op("pool", lambda e: e.memset(tri_f[:], 1.0), writes=[cB["ident"]])
        k.op("pool", lambda e: e.affine_select(out=tri_f[:], in_=tri_f[:], pattern=[[1, 128]], compare_op=ALU.is_ge, fill=0.0, base=0, channel_multiplier=-1), reads=[cB["ident"]], writes=[cB["ident"]])
        k.op("pool", lambda e: e.memset(ones_f[:], 1.0), writes=[cB["ident"]])
        k.op("pool", lambda e: e.memset(bd_a[:], 0.0), writes=[cB["bd"]])
        k.op("pool", lambda e: e.memset(bd_x[:], 0.0), writes=[cB["bd"]])
        for (bd, wsrc) in ((bd_a, w_rg_a), (bd_x, w_rg_x)):
            wv = wsrc.rearrange("(j two) d e -> two d j e", two=2)
            k.dma("pool", bd[0:64, :, 0:64], wv[0], writes=[cB["bd"]])
            k.dma("pool", bd[64:128, :, 64:128], wv[1], writes=[cB["bd"]])
        k.dma("pool", wf[:], w_in[:, C_F:C_F + NH].rearrange("(kc p) n -> p kc n", p=128), writes=[cB["wf"]])
        for c in range(NCH):
            for g in ("q", "k", "v"):
                cast((g, c), wst[g][c], w_in[:, GR[g] + c * 128:GR[g] + (c + 1) * 128].rearrange("(kc p) m -> p kc m", p=128))
        for g in ("xr", "yr", "pa", "pr", "ga", "gb"):
            if g in GR:
                cast_in(g)
            else:
                src = {"pa": w_pa, "pr": w_pr}[g]
                for c in range(NCH):
                    cast((g, c), wst[g][c], src[:, c * 128:(c + 1) * 128].rearrange("(kc p) m -> p kc m", p=128))
        for kc in range(8):
            cast(("out", kc), wst["out"][kc], w_out[kc * 128:(kc + 1) * 128, :])
        for f in range(NFF):
            cast(("up", f), wst["up"][f], w_up[:, f * 128:(f + 1) * 128].rearrange("(kc p) m -> p kc m", p=128))
        for f in range(NFF):
            cast(("down", f), wst["down"][f], w_down[f * 128:(f + 1) * 128, :])

        fm = lambda v: v.rearrange("(c p) -> p c", p=128)
        k.dma("sp", gmix[:], fm(g_mix), writes=[cB["vec"]])
        k.dma("sp", gmlp[:], fm(g_mlp), writes=[cB["vec"]])
        k.dma("sp", convw[:], conv_w.rearrange("w (c p) -> p w c", p=128), writes=[cB["vec"]])
        k.dma("sp", convb[:], fm(conv_b), writes=[cB["vec"]])
        k.dma("sp", hba[:], fm(b_rg_a), writes=[cB["vec"]])
        k.dma("sp", hbx[:], fm(b_rg_x), writes=[cB["vec"]])
        k.dma("sp", lam[:], fm(rg_lambda), writes=[cB["vec"]])
        k.dma("sp", gf_bc[:], g_fin.partition_broadcast(128), writes=[cB["vec"]])
        k.dma("sp", bf_bc[:], b_f.partition_broadcast(128), writes=[cB["vec"]])
        k.op("act", lambda e: e.activation(out=lam[:], in_=lam[:], func=AF.Exp, scale=-1.0), reads=[cB["vec"]], writes=[cB["vec"]])
        k.op("act", lambda e: e.activation(out=lam[:], in_=lam[:], func=AF.Ln, bias=1.0), reads=[cB["vec"]], writes=[cB["vec"]])
        k.op("dve", lambda e: e.tensor_scalar(out=hcl[:], in0=lam[:], scalar1=-4.0, scalar2=None, op0=ALU.mult), reads=[cB["vec"]], writes=[cB["vec"]])
        k.op("dve", lambda e: e.tensor_scalar(out=hba[:], in0=hba[:], scalar1=0.5, scalar2=None, op0=ALU.mult), reads=[cB["vec"]], writes=[cB["vec"]])
        k.op("dve", lambda e: e.tensor_scalar(out=hbx[:], in0=hbx[:], scalar1=0.5, scalar2=None, op0=ALU.mult), reads=[cB["vec"]], writes=[cB["vec"]])
        k.op("dve", lambda e: e.tensor_scalar(out=hcl2[:], in0=hcl[:], scalar1=2.0, scalar2=None, op0=ALU.mult), reads=[cB["vec"]], writes=[cB["vec"]])
        allc = list(cB.values())

        def mlp_events(BPT, has_next):
            steps = [("up", f) for f in range(NFF)]
            for p_ in range(2):
                steps += [("down", p_, f) for f in range(NFF)]
            ins = {}
            if has_next:
                for b in range(BPT):
                    ins.setdefault(3 * b, []).append(("prepA", b))
                    ins.setdefault(3 * b + 2, []).append(("prepB", b))
                for c in range(NCH):
                    ins.setdefault(13 + 10 * c, []).append(("rnnA", c))
                    ins.setdefault(18 + 10 * c, []).append(("rnnG", c))
            ev = []
            for i, st in enumerate(steps):
                ev += ins.get(i, [])
                ev.append(st)
            return ev

        order = []
        for jb in jobs:
            for j in range(NCH):
                order += [("q", j), ("k", j), ("v", j)]
            TTj = min(512, jb.T); NTj = jb.T // TTj; BPTj = TTj // min(128, jb.T)
            for c in range(NCH):
                order += [("xr", c), ("yr", c)]
            for tt in range(NTj):
                for c in range(NCH):
                    order += [("pa", c), ("ga", c), ("pr", c), ("gb", c)]
                order += [("out", kc) for kc in range(8)]
                for ev in mlp_events(BPTj, tt + 1 < NTj):
                    if ev[0] == "up":
                        order.append(("up", ev[1]))
                    elif ev[0] == "down":
                        order.append(("down", ev[2], ev[1]))
                    elif ev[0] == "rnnA":
                        order += [("xr", ev[1]), ("yr", ev[1])]
        ringB = [Buf("ring%d" % i) for i in range(NS)]
        rs = {"issued": 0, "next": 0, "unrel": 0}
        sp_seen_cast = [False] * 5

        def ring_issue(upto):
            while rs["issued"] < min(upto, len(order)):
                i = rs["issued"]
                flush_stores(lambda ent: ent[2] + DEFER <= i)
                key = order[i]
                g_ = grp_of(key)
                if not sp_seen_cast[g_]:
                    sp.wait_ge(castsems[g_], cast_cnt[g_])
                    sp_seen_cast[g_] = True
                slot = i % NS
                if key[0] == "down":
                    k.dma("sp", wring[:, slot, 0:512], wst["down"][key[1]][:, key[2] * 512:(key[2] + 1) * 512], writes=[ringB[slot]])
                elif key[0] == "out":
                    k.dma("sp", wring[:, slot, :], wst["out"][key[1]], writes=[ringB[slot]])
                else:
                    k.dma("sp", wring[:, slot, :], wst[key[0]][key[1]].rearrange("p kc m -> p (kc m)"), writes=[ringB[slot]])
                rs["issued"] += 1

        def getw(key, hold=True):
            i = rs["next"]
            assert order[i] == key, (order[i], key, i)
            rs["next"] += 1
            if not hold:
                rs["unrel"] = rs["next"] - 1
            ring_issue(rs["unrel"] + NS)
            assert rs["issued"] > i
            slot = i % NS
            return wring[:, slot, :], ringB[slot]

        def release_all():
            rs["unrel"] = rs["next"]

        pTB = [Buf("pT%d" % i) for i in range(4)]
        rdenB = Buf("rden")
        rden2 = [rden, rden_b]; rdenB2 = [Buf("rden0"), Buf("rden1")]
        stgB = [Buf("stg%d" % i) for i in range(NSTG)]
        stg_i = [0]
        pend_st = []
        DEFER = 5

        def flush_stores(pred=None):
            keep = []
            for ent in pend_st:
                if pred is None or pred(ent):
                    ent[1]()
                else:
                    keep.append(ent)
            pend_st[:] = keep

        def alloc_stg():
            si = stg_i[0] % NSTG
            stg_i[0] += 1
            flush_stores(lambda ent: ent[0] == si)
            return si

        def defer_store(si, fn):
            pend_st.append((si, fn, rs["issued"]))
        xnB = [Buf("xn0"), Buf("xn1")]
        xn_i = [0]
        junkB = Buf("junk"); ssB = Buf("ss")
        ss_i = [0]

        def norm_rstd(src_ap, srcB, np_):
            col = ss_i[0] % 8
            ss_i[0] += 1
            k.op("act", lambda e: e.activation(out=junk[0:np_, :], in_=src_ap, func=AF.Square, accum_out=ss[0:np_, col:col + 1]), reads=[srcB], writes=[junkB, ssB])
            k.op("act", lambda e: e.activation(out=rstd[0:np_, col:col + 1], in_=ss[0:np_, col:col + 1], func=AF.Ln, scale=1.0 / D, bias=EPS), reads=[ssB], writes=[ssB])
            k.op("act", lambda e: e.activation(out=rstd[0:np_, col:col + 1], in_=rstd[0:np_, col:col + 1], func=AF.Exp, scale=-0.5), reads=[ssB], writes=[ssB])
            return rstd[0:np_, col:col + 1]

        def norm_transpose(src_ap, srcB, np_, gvec, dst, dstB, c0, bank, split=False):
            r = norm_rstd(src_ap, srcB, np_)
            xi = xn_i[0] % 2
            xn_i[0] += 1
            k.op("dve", lambda e: e.tensor_scalar(out=xn[0:np_, xi, :], in0=src_ap, scalar1=r, scalar2=None, op0=ALU.mult), reads=[srcB, ssB], writes=[xnB[xi]])
            pbv = PB[bank][:].bitcast(BF16).rearrange("p (c t) -> p c t", c=8)

            def tr(e):
                ins = None
                for c in range(8):
                    ins = e.transpose(out=pbv[:, c, 0:np_], in_=xn[0:np_, xi, c * 128:(c + 1) * 128], identity=ident_bf[0:np_, 0:np_])
                return ins
            def part2():
                k.op("pe", tr, multi=True, reads=[xnB[xi]] + allc, writes=[PBb[bank]])
                k.op("dve", lambda e: e.tensor_tensor(out=dst[:, :, c0:c0 + np_], in0=pbv[:, :, 0:np_], in1=gvec[:, :].unsqueeze(2).broadcast_to([128, 8, np_]), op=ALU.mult), reads=[PBb[bank]] + allc, writes=[dstB])
            if split:
                return part2
            part2()

        def mm_group(bank_ap, wslot, rhs_fn, n):
            def f(e):
                ins = None; first = None
                for kc in range(8):
                    ins = e.matmul(bank_ap, lhsT=wslot[:, kc * 128:(kc + 1) * 128], rhs=rhs_fn(kc), start=(kc == 0), stop=(kc == 7))
                    first = first or ins
                return first, ins
            return f

        rtB = [Buf("rt%d" % i) for i in range(11)]
        rtf = [r[:, 0:520] if r is not None else None for r in rt]
        if rt1 is None:
            rt1 = rt; rtB1 = rtB
        else:
            rtB1 = [Buf("rtb%d" % i) for i in range(11)]
        rtf1 = [r[:, 0:520] if r is not None else None for r in rt1]
        RT = [(rt, rtf, rtB), (rt1, rtf1, rtB1)]

        def tail(jb, oaB):
            Tn = jb.T; H = jb.HIST
            PBK = min(128, Tn); TT = min(512, Tn); NT = Tn // TT; BPT = TT // PBK
            stB = Buf("state")
            inB = [Buf("stin%d" % i) for i in range(4)]
            if H:
                k.dma("sp", hstate[:], jb.sh.rearrange("(c p) -> p c", p=128), writes=[inB[0]])
                for w_ in range(3):
                    k.dma("sp", convhist[:, :, w_], jb.sconv[w_].rearrange("(c p) -> p c", p=128), writes=[inB[1 + w_]])
            else:
                k.op("pool", lambda e: e.memset(hstate[:], 0.0), writes=[inB[0]])
                k.op("pool", lambda e: e.memset(convhist[:], 0.0), writes=[inB[1]])
            k.op("dve", lambda e: e.memset(scr[0:1, 2:3], 0.0), reads=inB, writes=[stB])
            cvB = [Buf("cv%d" % c) for c in range(NCH)]; hsB = [Buf("hs%d" % c) for c in range(NCH)]
            for b_ in cvB + hsB:
                b_.w = stB.w
            x1B = [Buf("x1_%d" % b) for b in range(BPT)]
            hntB = Buf("hnt"); h2B = Buf("h2T"); orB = Buf("ornnT"); mgB = Buf("mergedT"); aTB = Buf("aT")
            relu_t = [mergedT[:, 2 * i:2 * i + 2, 0:TT0].rearrange("p a t -> p (a t)").bitcast(F32) for i in range(4)]
            reluB = [Buf("relu%d" % i) for i in range(4)]
            rhs_h = lambda kc: hnt[:, kc, 0:TT]

            def prep_hn(tt, b, split=False):
                t0 = (tt * BPT + b) * PBK
                si = alloc_stg()
                k.dma("sp", stg[0:PBK, si, :], jb.x[t0:t0 + PBK, :], writes=[stgB[si]])
                return norm_transpose(stg[0:PBK, si, :], stgB[si], PBK, gmix, hnt, hntB, b * PBK, 4 + b % 2 if split else b % 2, split=split)

            def rnn_A(c, bk):
                wxr, wxrB = getw(("xr", c)); wyr, wyrB = getw(("yr", c))
                k.op("pe", multi="fl", fn=mm_group(PB[bk[0]][:, 0:TT], wxr, rhs_h, TT), reads=[wxrB, hntB], writes=[PBb[bk[0]]])
                k.op("pe", multi="fl", fn=mm_group(PB[bk[1]][:, 0:TT], wyr, rhs_h, TT), reads=[wyrB, hntB], writes=[PBb[bk[1]]])
                release_all()

            def rnn_S1a(c, bk, ts):
                bx_, by_, ba_, bi_ = bk
                rs_, rf_, rb_ = RT[ts]
                xrp, xc, y2_ = rf_[0], rf_[1], rf_[10]
                xcb = rs_[2].bitcast(BF16)[:, 0:TT]
                k.op("pool", lambda e: e.tensor_copy(out=xrp[:, 0:3], in_=convhist[:, c, :]), reads=[cvB[c]], writes=[rb_[0]])
                k.op("act", lambda e: e.copy(out=xrp[:, 3:3 + TT], in_=PB[bx_][:, 0:TT]), reads=[PBb[bx_]], writes=[rb_[0]])
                k.op("pool", lambda e: e.tensor_copy(out=convhist[:, c, :], in_=xrp[:, TT:TT + 3]), reads=[rb_[0]], writes=[cvB[c]])
                k.op("act", lambda e: e.activation(out=xc[:, 0:TT], in_=PB[bx_][:, 0:TT], func=AF.Identity, bias=convb[:, c:c + 1], scale=convw[:, 3, c:c + 1]), reads=[PBb[bx_]] + allc, writes=[rb_[1]])
                k.op("act", lambda e: e.activation(out=y2_[:, 0:TT], in_=PB[by_][:, 0:TT], func=AF.Square), reads=[PBb[by_]], writes=[rb_[10]])
                for w in (2, 1, 0):
                    k.op("dve", lambda e, w=w: e.scalar_tensor_tensor(out=xc[:, 0:TT], in0=xrp[:, w:w + TT], scalar=convw[:, w, c:c + 1], in1=xc[:, 0:TT], op0=ALU.mult, op1=ALU.add), reads=[rb_[0], rb_[1]], writes=[rb_[1]])
                k.op("pool", lambda e: e.tensor_copy(out=xcb, in_=xc[:, 0:TT]), reads=[rb_[1]], writes=[rb_[2]])

            def rnn_G(c, bk, ts):
                bx_, by_, ba_, bi_ = bk
                rs_, rf_, rb_ = RT[ts]
                y2_ = rf_[10]
                xcb = rs_[2].bitcast(BF16)[:, 0:TT]
                k.op("pe", lambda e: e.matmul(PB[ba_][:, 0:TT], lhsT=bd_a[:, c, :], rhs=xcb, start=True, stop=True), reads=[rb_[2]] + allc, writes=[PBb[ba_]])
                k.op("pe", lambda e: e.matmul(PB[bi_][:, 0:TT], lhsT=bd_x[:, c, :], rhs=xcb, start=True, stop=True), reads=[rb_[2]] + allc, writes=[PBb[bi_]])
                k.op("dve", lambda e: e.tensor_scalar(out=y2_[:, 0:TT], in0=y2_[:, 0:TT], scalar1=0.044715, scalar2=1.0, op0=ALU.mult, op1=ALU.add), reads=[rb_[10]], writes=[rb_[10]])
                k.op("dve", lambda e: e.tensor_tensor(out=y2_[:, 0:TT], in0=y2_[:, 0:TT], in1=PB[by_][:, 0:TT], op=ALU.mult), reads=[rb_[10], PBb[by_]], writes=[rb_[10]])

            def rnn_S2(c, bk, ts):
                bx_, by_, ba_, bi_ = bk
                rs_, rf_, rb_ = RT[ts]
                xc, a_, ti_, u_, hs_, y2_ = rf_[1], rf_[3], rf_[5], rf_[6], rf_[8], rf_[10]
                k.op("act", lambda e: e.activation(out=a_[:, 0:TT], in_=PB[ba_][:, 0:TT], func=AF.Tanh, bias=hba[:, c:c + 1], scale=0.5), reads=[PBb[ba_]] + allc, writes=[rb_[3]])
                k.op("act", lambda e: e.activation(out=ti_[:, 0:TT], in_=PB[bi_][:, 0:TT], func=AF.Tanh, bias=hbx[:, c:c + 1], scale=0.5), reads=[PBb[bi_]] + allc, writes=[rb_[5]])
                k.op("act", lambda e: e.activation(out=y2_[:, 0:TT], in_=y2_[:, 0:TT], func=AF.Tanh, scale=GELU_C), reads=[rb_[10]], writes=[rb_[10]])
                k.op("act", lambda e: e.activation(out=a_[:, 0:TT], in_=a_[:, 0:TT], func=AF.Exp, bias=hcl[:, c:c + 1], scale=hcl[:, c:c + 1]), reads=[rb_[3]] + allc, writes=[rb_[3]])
                k.op("pool", lambda e: e.tensor_tensor(out=u_[:, 0:TT], in0=a_[:, 0:TT], in1=a_[:, 0:TT], op=ALU.mult), reads=[rb_[3]], writes=[rb_[6]])
                k.op("dve", lambda e: e.scalar_tensor_tensor(out=ti_[:, 0:TT], in0=ti_[:, 0:TT], scalar=1.0, in1=xc[:, 0:TT], op0=ALU.add, op1=ALU.mult), reads=[rb_[5], rb_[1]], writes=[rb_[5]])
                k.op("act", lambda e: e.activation(out=u_[:, 0:TT], in_=u_[:, 0:TT], func=AF.Sqrt, bias=1.0, scale=-1.0), reads=[rb_[6]], writes=[rb_[6]])
                k.op("dve", lambda e: e.scalar_tensor_tensor(out=y2_[:, 0:TT], in0=y2_[:, 0:TT], scalar=1.0, in1=PB[by_][:, 0:TT], op0=ALU.add, op1=ALU.mult), reads=[rb_[10], PBb[by_]], writes=[rb_[10]])
                k.op("dve", lambda e: e.scalar_tensor_tensor(out=ti_[:, 0:TT], in0=ti_[:, 0:TT], scalar=0.5, in1=u_[:, 0:TT], op0=ALU.mult, op1=ALU.mult), reads=[rb_[5], rb_[6]], writes=[rb_[5]])
                k.op("dve", lambda e: e.tensor_tensor_scan(out=hs_[:, 0:TT], data0=a_[:, 0:TT], data1=ti_[:, 0:TT], initial=hstate[:, c:c + 1], op0=ALU.mult, op1=ALU.add), reads=[rb_[3], rb_[5], hsB[c]], writes=[rb_[8]])
                k.op("pool", lambda e: e.tensor_copy(out=hstate[:, c:c + 1], in_=hs_[:, TT - 1:TT]), reads=[rb_[8]], writes=[hsB[c]])
                k.op("dve", lambda e: e.scalar_tensor_tensor(out=ornnT[:, c, 0:TT], in0=y2_[:, 0:TT], scalar=0.5, in1=hs_[:, 0:TT], op0=ALU.mult, op1=ALU.mult), reads=[rb_[10], rb_[8]], writes=[orB] + reluB)

            bk2 = lambda c: (c % 2, 2 + c % 2, 4 + c % 2, 6 + c % 2)
            BK1 = (4, 5, 6, 7)

            for b in range(BPT):
                prep_hn(0, b)
            if rtB1 is not rtB:
                k.op("dve", lambda e: e.memset(scr[0:1, 0:1], 0.0), writes=[aTB] + rtB1)
            rnn_A(0, bk2(0)); rnn_A(1, bk2(1))
            rnn_S1a(0, bk2(0), 0); rnn_G(0, bk2(0), 0)
            for c in range(NCH):
                if c + 1 < NCH:
                    rnn_S1a(c + 1, bk2(c + 1), (c + 1) % 2); rnn_G(c + 1, bk2(c + 1), (c + 1) % 2)
                rnn_S2(c, bk2(c), c % 2)
                if c + 2 < NCH:
                    rnn_A(c + 2, bk2(c + 2))

            for tt in range(NT):
                has_next = tt + 1 < NT
                for b in range(BPT):
                    t0 = (tt * BPT + b) * PBK
                    k.dma("sp", x1[0:PBK, b, :], jb.x[t0:t0 + PBK, :], writes=[x1B[b]])
                if rtB1 is not rtB:
                    k.op("dve", lambda e: e.memset(scr[0:1, 3:4], 0.0), writes=[aTB] + rtB1)
                for c in range(NCH):
                    wpa, wpaB = getw(("pa", c)); wga, wgaB = getw(("ga", c)); wpr, wprB = getw(("pr", c)); wgb, wgbB = getw(("gb", c))
                    base = 4 * (c % 2)
                    rhs_a = lambda kc: oattnT[:, kc, tt * TT:(tt + 1) * TT]
                    rhs_r = lambda kc: ornnT[:, kc, 0:TT]
                    k.op("pe", multi="fl", fn=mm_group(PB[base][:, 0:TT], wpa, rhs_a, TT), reads=[wpaB] + [oaB[(kc, tt)] for kc in range(8)], writes=[PBb[base]])
                    k.op("pe", multi="fl", fn=mm_group(PB[base + 1][:, 0:TT], wga, rhs_h, TT), reads=[wgaB, hntB], writes=[PBb[base + 1]])
                    k.op("pe", multi="fl", fn=mm_group(PB[base + 2][:, 0:TT], wpr, rhs_r, TT), reads=[wprB, orB], writes=[PBb[base + 2]])
                    k.op("pe", multi="fl", fn=mm_group(PB[base + 3][:, 0:TT], wgb, rhs_h, TT), reads=[wgbB, hntB], writes=[PBb[base + 3]])
                    release_all()
                    _, rf_, rb_ = RT[c % 2]
                    tga, tgb, m1, m2 = rf_[0], rf_[1], rf_[3], rf_[5]
                    bga, bgb, bm1, bm2 = rb_[0], rb_[1], rb_[3], rb_[5]
                    k.op("act", lambda e: e.activation(out=tga[:, 0:TT], in_=PB[base + 1][:, 0:TT], func=AF.Tanh, scale=0.5), reads=[PBb[base + 1]], writes=[bga])
                    k.op("act", lambda e: e.activation(out=tgb[:, 0:TT], in_=PB[base + 3][:, 0:TT], func=AF.Tanh, scale=0.5), reads=[PBb[base + 3]], writes=[bgb])
                    k.op("dve", lambda e: e.scalar_tensor_tensor(out=m1[:, 0:TT], in0=tga[:, 0:TT], scalar=1.0, in1=PB[base][:, 0:TT], op0=ALU.add, op1=ALU.mult), reads=[bga, PBb[base]], writes=[bm1])
                    k.op("dve", lambda e: e.scalar_tensor_tensor(out=m2[:, 0:TT], in0=tgb[:, 0:TT], scalar=1.0, in1=PB[base + 2][:, 0:TT], op0=ALU.add, op1=ALU.mult), reads=[bgb, PBb[base + 2]], writes=[bm2])
                    k.op("pool", lambda e: e.tensor_tensor(out=mergedT[:, c, 0:TT], in0=m1[:, 0:TT], in1=m2[:, 0:TT], op=ALU.add), reads=[bm1, bm2], writes=[mgB] + reluB)
                wo = [getw(("out", kc), hold=True) for kc in range(8)]
                for b in range(BPT):
                    for half in range(2):
                        bank = (b * 2 + half) % 8

                        def f(e, b=b, half=half, bank=bank):
                            ins = None
                            for kc in range(8):
                                ins = e.matmul(PB[bank][0:PBK, :], lhsT=mergedT[:, kc, b * PBK:(b + 1) * PBK], rhs=wo[kc][0][:, half * 512:(half + 1) * 512], start=(kc == 0), stop=(kc == 7))
                            return ins
                        k.op("pe", f, multi=True, reads=[mgB] + [w_[1] for w_ in wo], writes=[PBb[bank]])
                        xs_ = x1[0:PBK, b, half * 512:(half + 1) * 512]
                        k.op("dve", lambda e, bank=bank, xs_=xs_: e.scalar_tensor_tensor(out=xs_, in0=PB[bank][0:PBK, :], scalar=0.5, in1=xs_, op0=ALU.mult, op1=ALU.add), reads=[PBb[bank], x1B[b]], writes=[x1B[b]])
                    if b >= 1:
                        norm_transpose(x1[0:PBK, b - 1, :], x1B[b - 1], PBK, gmlp, h2T, h2B, (b - 1) * PBK, 6 + (b - 1) % 2)
                norm_transpose(x1[0:PBK, BPT - 1, :], x1B[BPT - 1], PBK, gmlp, h2T, h2B, (BPT - 1) * PBK, 6 + (BPT - 1) % 2)
                release_all()
                rhs_2 = lambda kc: h2T[:, kc, 0:TT]
                if rtB1 is not rtB:
                    k.op("dve", lambda e: e.memset(scr[0:1, 1:2], 0.0), writes=[aTB] + rtB1)
                pend = {}
                for ev in mlp_events(BPT, has_next):
                    if ev[0] == "up":
                        f_ = ev[1]
                        wu, wuB = getw(("up", f_))
                        bank = f_ % 4
                        k.op("pe", multi="fl", fn=mm_group(PB[bank][:, 0:TT], wu, rhs_2, TT), reads=[wuB, h2B], writes=[PBb[bank]])
                        release_all()
                        rl = relu_t[f_ % 4]; rlB = reluB[f_ % 4]
                        k.op("act", lambda e, bank=bank, rl=rl: e.activation(out=rl[:, 0:TT], in_=PB[bank][:, 0:TT], func=AF.Relu), reads=[PBb[bank]], writes=[rlB, mgB])
                        k.op("pool" if f_ % 2 else "dve", lambda e, rl=rl, f_=f_: e.tensor_tensor(out=aT[:, f_, 0:TT], in0=rl[:, 0:TT], in1=rl[:, 0:TT], op=ALU.mult), reads=[rlB], writes=[aTB])
                    elif ev[0] == "down":
                        h_, f_ = ev[1], ev[2]
                        wd, wdB = getw(("down", f_, h_))

                        def f(e, f_=f_, wd=wd):
                            ins = None; first = None
                            for b in range(BPT):
                                ins = e.matmul(PB[b][0:PBK, :], lhsT=aT[:, f_, b * PBK:(b + 1) * PBK], rhs=wd[:, 0:512], start=(f_ == 0), stop=(f_ == NFF - 1))
                                first = first or ins
                            return first, ins
                        k.op("pe", f, multi="fl", reads=[wdB, aTB], writes=[PBb[i] for i in range(BPT)])
                        release_all()
                        if f_ == NFF - 1:
                            for b in range(BPT):
                                xs_ = x1[0:PBK, b, h_ * 512:(h_ + 1) * 512]
                                k.op("dve", lambda e, b=b, xs_=xs_: e.tensor_tensor(out=xs_, in0=PB[b][0:PBK, :], in1=xs_, op=ALU.add), reads=[PBb[b], x1B[b]], writes=[x1B[b]])
                    elif ev[0] == "prepA":
                        pend[ev[1]] = prep_hn(tt + 1, ev[1], split=True)
                    elif ev[0] == "prepB":
                        pend.pop(ev[1])()
                    elif ev[0] == "rnnA":
                        rnn_A(ev[1], BK1); rnn_S1a(ev[1], BK1, 0)
                    elif ev[0] == "rnnG":
                        rnn_G(ev[1], BK1, 0); rnn_S2(ev[1], BK1, 0)
                for b in range(BPT):
                    r = norm_rstd(x1[0:PBK, b, :], x1B[b], PBK)
                    si = alloc_stg()
                    k.op("dve", lambda e, b=b, si=si, r=r: e.scalar_tensor_tensor(out=stg[0:PBK, si, :], in0=x1[0:PBK, b, :], scalar=r, in1=gf_bc[0:PBK, :], op0=ALU.mult, op1=ALU.mult), reads=[x1B[b], ssB] + allc, writes=[stgB[si]])
                    t0 = (tt * BPT + b) * PBK
                    defer_store(si, lambda t0=t0, si=si: k.dma("sp", jb.y[t0:t0 + PBK, :], stg[0:PBK, si, :], reads=[stgB[si]]))
            for w_ in range(3):
                k.dma("sp", jb.co[w_].rearrange("(c p) -> p c", p=128), convhist[:, :, w_], reads=cvB)
            k.dma("sp", jb.ho.rearrange("(c p) -> p c", p=128), hstate[:], reads=hsB)

        def run_job(ji, jb):
            ckpt(1)
            Tn = jb.T; H = jb.HIST
            PBK = min(128, Tn)
            NNB = Tn // PBK; NHB = H // 128; NKB = NHB + NNB
            TT = min(512, Tn); NT = Tn // TT; BPT = TT // PBK
            QS = min(256, Tn); NQS = TT // QS
            k.barrier()
            oaB = {}
            hnB = [Buf("hn%d" % b) for b in range(NNB)]
            vbfB = [Buf("vbf%d" % b) for b in range(NKB)]
            vbfB2 = [Buf("vbfo%d" % b) for b in range(NKB)]
            lfB = Buf("lf"); cB2 = Buf("c")
            k.op("pool", lambda e: e.memset(vbf[:, 0:NKB, :, 64:128], 1.0), writes=vbfB)
            for b in range(NNB):
                si = alloc_stg()
                k.dma("sp", stg[0:PBK, si, :], jb.x[b * PBK:(b + 1) * PBK, :], writes=[stgB[si]])
                norm_transpose(stg[0:PBK, si, :], stgB[si], PBK, gmix, hnT, hnB[b], b * PBK, b % 2)
            ckpt(2)
            if H:
                k.dma("sp", lf[:, 0:NHB, :], jb.clf.rearrange("(b p) h -> p b h", p=128), writes=[lfB])
            zfv = PB[2][:, 0:NKBMAX * NH].rearrange("p (b h) -> p b h", h=NH)
            for b in range(NNB):
                def f(e, b=b):
                    ins = None
                    for kc in range(8):
                        ins = e.matmul(zfv[0:PBK, b, :], lhsT=hnT[:, kc, b * PBK:(b + 1) * PBK], rhs=wf[:, kc, :], start=(kc == 0), stop=(kc == 7))
                    return ins
                k.op("pe", f, multi=True, reads=[hnB[b]] + allc, writes=[PBb[2]])
            ckpt(2.2)
            lfn = lf[0:PBK, NHB:NKB, :]
            k.op("dve", lambda e: e.tensor_tensor(out=zf[0:PBK, 0:NNB, :], in0=zfv[0:PBK, 0:NNB, :], in1=bf_bc[0:PBK, :].unsqueeze(1).broadcast_to([PBK, NNB, NH]), op=ALU.add), reads=[PBb[2]] + allc, writes=[cB2])
            k.op("act", lambda e: e.activation(out=zf[0:PBK, 0:NNB, :], in_=zf[0:PBK, 0:NNB, :], func=AF.Exp, scale=-1.0), reads=[cB2], writes=[cB2])
            k.op("act", lambda e: e.activation(out=zf[0:PBK, 0:NNB, :], in_=zf[0:PBK, 0:NNB, :], func=AF.Ln, bias=1.0), reads=[cB2], writes=[cB2])
            k.op("dve", lambda e: e.tensor_scalar(out=lfn, in0=zf[0:PBK, 0:NNB, :], scalar1=-1.0, scalar2=None, op0=ALU.mult), reads=[cB2], writes=[lfB])
            ckpt(2.4)
            k.dma("sp", jb.lo.rearrange("(b p) h -> p b h", p=PBK), lfn, reads=[lfB])
            ckpt(2.5)
            kn_of = lambda i: 128 if i < NHB else PBK
            cv_ = PB[3][:, 0:NKBMAX * NH].rearrange("p (b h) -> p b h", h=NH)
            rv_ = PB[5][:, 0:(NKBMAX + 1) * NH].rearrange("p (b h) -> p b h", h=NH)

            def fc(e):
                ins = None
                for i in range(NKB):
                    kn = kn_of(i)
                    ins = e.matmul(cv_[0:kn, i, :], lhsT=tri_f[0:kn, 0:kn], rhs=lf[0:kn, i, :], start=True, stop=(i == 0))
                    for i2 in range(i):
                        k2 = kn_of(i2)
                        ins = e.matmul(cv_[0:kn, i, :], lhsT=ones_f[0:k2, 0:kn], rhs=lf[0:k2, i2, :], start=False, stop=(i2 == i - 1))
                for m in range(1, NKB + 1):
                    for i2 in range(m):
                        k2 = kn_of(i2)
                        ins = e.matmul(rv_[:, m, :], lhsT=ones_f[0:k2, :], rhs=lf[0:k2, i2, :], start=(i2 == 0), stop=(i2 == m - 1))
                ins = e.matmul(PB[3][0:8, 504:512], lhsT=ident_bf[:, 0:8], rhs=ident_bf[:, 0:8], start=True, stop=True)
                return ins
            k.op("pe", fc, multi=True, reads=[lfB] + allc, writes=[PBb[3], PBb[5]])
            ckpt(2.6)
            k.op("dve", lambda e: e.tensor_scalar(out=csb[:, 0:NKB, :], in0=cv_[:, 0:NKB, :], scalar1=-1.0, scalar2=None, op0=ALU.mult), reads=[PBb[3]], writes=[cB2])
            ckpt(2.7)
            k.op("dve", lambda e: e.memset(rsb[:, 0:1, :], 0.0), writes=[cB2])
            ckpt(2.8)
            k.op("act", lambda e: e.copy(out=rsb[:, 1:NKB + 1, :], in_=rv_[:, 1:NKB + 1, :]), reads=[PBb[5]], writes=[cB2])
            def ref_index(qb0):
                return NHB + qb0 + 1 if QS == 256 else NHB + qb0
            NQ = Tn // QS

            ckpt(3)
            pst = {}

            def produce(j):
                bi = j % 2
                qTj, kTj = qT[bi], kT[bi]
                qB = Buf("qT"); kBs = [Buf("kT%d" % t) for t in range((H + Tn + 511) // 512 + 1)]
                kb_of = lambda col: kBs[col // 512]
                bias = bias_ab[bi]; biasB = Buf("bias")
                pst[j] = (qTj, kTj, qB, kb_of, bias, biasB)
                def fb(e):
                    ins = None
                    for hh in range(2):
                        h = 2 * j + hh
                        for q in range(NQ):
                            m = ref_index(q * (QS // PBK))
                            ins = e.tensor_scalar(out=bias[:, hh, 0:NKB, q:q + 1], in0=csb[:, 0:NKB, h:h + 1], scalar1=rsb[:, m, h:h + 1], scalar2=None, op0=ALU.add)
                    return ins
                k.op("dve", fb, multi=True, reads=[cB2], writes=[biasB])
                ckpt(3.05)
                if H:
                    xi = xn_i[0] % 2; xn_i[0] += 1
                    kst = xn[:, xi, :].rearrange("p (b m) -> p b m", m=128)
                    k.dma("pool", kst[:, 0:NHB, :], jb.ck[:, j * 128:(j + 1) * 128].rearrange("(b p) m -> p b m", p=128), writes=[xnB[xi]])
                    bank = 4 + (j % 2)
                    pbh = PB[bank][:].bitcast(BF16)
                    pbv3 = pbh.rearrange("p (b t) -> p b t", t=128)

                    def trh(e):
                        ins = None
                        for hb in range(NHB):
                            ins = e.transpose(out=pbv3[:, hb, :], in_=kst[:, hb, :], identity=ident_bf[:])
                        return ins
                    k.op("pe", trh, multi=True, reads=[xnB[xi]] + allc, writes=[PBb[bank]])
                    k.op("act", lambda e: e.copy(out=kTj[:, 0:H], in_=pbh[:, 0:H]), reads=[PBb[bank]], writes=list({id(kb_of(c_)): kb_of(c_) for c_ in range(0, H, 512)}.values()))
                    cvv = jb.cv[:, j * 128:(j + 1) * 128].rearrange("(b p) m -> p b m", p=128)
                    k.dma("pool", vbf[:, 0:NHB, j, 0:64], cvv[:, :, 0:64], writes=vbfB[0:NHB])
                    k.dma("pool", vbf[:, 0:NHB, j, 128:192], cvv[:, :, 64:128], writes=vbfB2[0:NHB])
                wq, wqB = getw(("q", j)); wk, wkB = getw(("k", j)); wv, wvB = getw(("v", j))
                ckpt(3.1)
                for tt in range(NT):
                    cols = slice(tt * TT, (tt + 1) * TT)
                    hb_ = hnB[tt * BPT:(tt + 1) * BPT]
                    rhs = lambda kc, cols=cols: hnT[:, kc, cols]
                    k.op("pe", multi="fl", fn=mm_group(PB[0][:, 0:TT], wq, rhs, TT), reads=[wqB] + hb_, writes=[PBb[0]])
                    k.op("dve", lambda e, cols=cols: e.tensor_scalar(out=qTj[:, cols], in0=PB[0][:, 0:TT], scalar1=0.125, scalar2=None, op0=ALU.mult), reads=[PBb[0]], writes=[qB])
                    ckpt(3.15)
                    k.op("pe", multi="fl", fn=mm_group(PB[1][:, 0:TT], wk, rhs, TT), reads=[wkB] + hb_, writes=[PBb[1]])
                    kvB = [Buf("kvf0"), Buf("kvf1")]
                    k.op("dve", lambda e: e.tensor_copy(out=kTj[:, H + tt * TT:H + (tt + 1) * TT], in_=PB[1][:, 0:TT]), reads=[PBb[1]], writes=[kb_of(H + tt * TT)])
                    k.op("dve", lambda e: e.tensor_copy(out=kvf[0][:, 0:TT], in_=PB[1][:, 0:TT]), reads=[PBb[1]], writes=[kvB[0]])
                    ckpt(3.2)
                    k.op("pe", multi="fl", fn=mm_group(PB[2][:, 0:TT], wv, rhs, TT), reads=[wvB] + hb_, writes=[PBb[2]])
                    k.op("dve", lambda e: e.tensor_copy(out=kvf[1][:, 0:TT], in_=PB[2][:, 0:TT]), reads=[PBb[2]], writes=[kvB[1]])
                    ckpt(3.3)
                    ktv = PB[3][:, :].rearrange("p (b m) -> p b m", m=128)
                    vtv = PB[4][:, :].rearrange("p (b m) -> p b m", m=128)

                    def ftr(e, src, dstv):
                        ins = None
                        for b in range(BPT):
                            ins = e.transpose(out=dstv[0:PBK, b, :], in_=src[:, b * PBK:(b + 1) * PBK], identity=ident_f[:])
                        return ins
                    k.op("pe", lambda e: ftr(e, kvf[0], ktv), multi=True, reads=[kvB[0]] + allc, writes=[PBb[3]])
                    k.op("pe", lambda e: ftr(e, kvf[1], vtv), multi=True, reads=[kvB[1]] + allc, writes=[PBb[4]])
                    ckpt(3.4)
                    sk = alloc_stg()
                    sv = alloc_stg()
                    skv = stg[:, sk, 0:512].rearrange("p (b m) -> p b m", m=128)
                    svv = stg[:, sv, 0:512].rearrange("p (b m) -> p b m", m=128)
                    k.op("dve", lambda e: e.tensor_copy(out=skv[0:PBK, 0:BPT, :], in_=ktv[0:PBK, 0:BPT, :]), reads=[PBb[3]], writes=[stgB[sk]])
                    k.op("dve", lambda e: e.tensor_copy(out=svv[0:PBK, 0:BPT, :], in_=vtv[0:PBK, 0:BPT, :]), reads=[PBb[4]], writes=[stgB[sv]])
                    nb0 = tt * BPT
                    vdst = vbf[0:PBK, NHB + nb0:NHB + nb0 + BPT, j, :]
                    k.op("dve", lambda e: e.tensor_copy(out=vdst[:, :, 0:64], in_=vtv[0:PBK, 0:BPT, 0:64]), reads=[PBb[4]], writes=vbfB[NHB + nb0:NHB + nb0 + BPT])
                    k.op("dve", lambda e: e.tensor_copy(out=vdst[:, :, 128:192], in_=vtv[0:PBK, 0:BPT, 64:128]), reads=[PBb[4]], writes=vbfB[NHB + nb0:NHB + nb0 + BPT])
                    ckpt(3.5)
                    kdst = jb.ko[tt * TT:(tt + 1) * TT, j * 128:(j + 1) * 128].rearrange("(b p) m -> p b m", p=PBK)
                    vdsto = jb.vo[tt * TT:(tt + 1) * TT, j * 128:(j + 1) * 128].rearrange("(b p) m -> p b m", p=PBK)
                    defer_store(sk, lambda kdst=kdst, skv=skv, sk=sk: k.dma("sp", kdst, skv[0:PBK, 0:BPT, :], reads=[stgB[sk]]))
                    defer_store(sv, lambda vdsto=vdsto, svv=svv, sv=sv: k.dma("sp", vdsto, svv[0:PBK, 0:BPT, :], reads=[stgB[sv]]))
                release_all()

            def attend(j):
                qTj, kTj, qB, kb_of, bias, biasB = pst[j]
                items = []
                for tt in range(NT):
                    oaB[(j, tt)] = Buf("oa")
                    for hh in range(2):
                        last_i = NHB + (tt + 1) * BPT - 1
                        for i in range(last_i + 1):
                            items.append((tt, hh, i, last_i))
                LA = 2

                def geom(n):
                    tt, hh, i, last_i = items[n]
                    kn = kn_of(i)
                    kcol = i * 128 if i < NHB else H + (i - NHB) * PBK
                    nbi = i - NHB
                    dq = max(0, nbi - tt * BPT)
                    return tt, hh, i, last_i, kn, kcol, nbi, dq * PBK, slice(hh * 64, (hh + 1) * 64), tt * TT

                def emit_qk(n):
                    tt, hh, i, last_i, kn, kcol, nbi, qlo, prow, q0 = geom(n)
                    sbank = 3 + (n % 3)
                    k.op("pe", lambda e: e.matmul(PB[sbank][0:kn, qlo:TT], lhsT=kTj[prow, kcol:kcol + kn], rhs=qTj[prow, q0 + qlo:q0 + TT], start=True, stop=True), reads=[kb_of(kcol), qB], writes=[PBb[sbank]])

                def emit_rest(n):
                    tt, hh, i, last_i, kn, kcol, nbi, qlo, prow, q0 = geom(n)
                    sbank = 3 + (n % 3)
                    obank = 6 + hh
                    pt = pT[n % 4]; ptb = pTB[n % 4]
                    for sq in range(NQS):
                        c0 = max(qlo, sq * QS); c1 = (sq + 1) * QS
                        if c0 >= c1:
                            continue
                        qidx = (q0 + sq * QS) // QS
                        k.op("act", lambda e, c0=c0, c1=c1, qidx=qidx: e.activation(out=pt[0:kn, c0:c1], in_=PB[sbank][0:kn, c0:c1], func=AF.Exp, bias=bias[0:kn, hh, i, qidx:qidx + 1]), reads=[PBb[sbank], biasB], writes=[ptb])
                    if nbi >= tt * BPT:
                        k.op("pool", lambda e: e.tensor_tensor(out=pt[0:kn, qlo:qlo + PBK], in0=pt[0:kn, qlo:qlo + PBK], in1=mask_bf[0:kn, 0:PBK], op=ALU.mult), reads=[ptb] + allc, writes=[ptb])
                    k.op("pe", lambda e: e.matmul(PB[obank][:, qlo:TT], lhsT=vbf[0:kn, i, j, hh * 64:hh * 64 + 128], rhs=pt[0:kn, qlo:TT], start=(i == 0), stop=(i == last_i)), reads=[ptb, vbfB[i], vbfB2[i]], writes=[PBb[obank]])
                    if i == last_i:
                        drow = slice((1 - hh) * 64, (2 - hh) * 64)
                        rd = rden2[hh]; rdB = rdenB2[hh]
                        k.op("dve", lambda e: e.reciprocal(out=rd[drow, 0:TT], in_=PB[obank][drow, 0:TT]), reads=[PBb[obank]], writes=[rdB])
                        k.op("dve", lambda e: e.tensor_tensor(out=oattnT[prow, j, q0:q0 + TT], in0=PB[obank][prow, 0:TT], in1=rd[drow, 0:TT], op=ALU.mult), reads=[PBb[obank], rdB], writes=[oaB[(j, tt)]])

                for n in range(len(items) + LA):
                    if n < len(items):
                        emit_qk(n)
                    if n >= LA:
                        emit_rest(n - LA)
            produce(0)
            for j in range(NCH):
                if j + 1 < NCH:
                    produce(j + 1)
                attend(j)
            flush_stores()
            k.barrier()
            ckpt(5)
            tail(jb, oaB)
            flush_stores()

        try:
            ckpt(0)
            for ji, jb in enumerate(jobs):
                run_job(ji, jb)
        except _Stop:
            pass
        for g_ in range(5):
            if cast_cnt[g_] > 0:
                sp.wait_ge(castsems[g_], cast_cnt[g_])
        k.finish()
    return nc


_NC_CACHE = {}


def _run(inputs, n_cores, NPS, T, NSS, TS, HIST, trace=False):
    key = (NPS, T, NSS, TS, HIST)
    if key not in _NC_CACHE:
        _NC_CACHE[key] = build(NPS, T, NSS, TS, HIST)
    nc = _NC_CACHE[key]
    f = lambda a: np.ascontiguousarray(np.asarray(a, dtype=np.float32))
    i = inputs
    shared = {
        "g_mix": f(i["norm_mix_g"][0]), "w_in": f(i["w_in"][0]), "b_f": f(i["b_f"][0]), "conv_w": f(i["conv_w"][0]),
        "conv_b": f(i["conv_b"][0]), "w_rg_a": f(i["w_rg_a"][0]), "b_rg_a": f(i["b_rg_a"][0]), "w_rg_x": f(i["w_rg_x"][0]),
        "b_rg_x": f(i["b_rg_x"][0]), "rg_lambda": f(i["rg_lambda"][0]), "w_pa": f(i["w_proj_attn"][0]),
        "w_pr": f(i["w_proj_rnn"][0]), "w_out": f(i["w_out"][0]), "g_mlp": f(i["norm_mlp_g"][0]), "w_up": f(i["w_up"][0]),
        "w_down": f(i["w_down"][0]), "g_fin": f(i["norm_final_g"]),
    }
    in_maps = []
    for c in range(n_cores):
        m = dict(shared)
        ps = slice(c * NPS, (c + 1) * NPS); ss = slice(c * NSS, (c + 1) * NSS)
        m["xp"] = f(i["x_prompt"][ps]); m["xs"] = f(i["x_sample"][ss])
        m["ck"] = f(i["cache_k"][0, ss]).reshape(NSS, HIST, D); m["cv"] = f(i["cache_v"][0, ss]).reshape(NSS, HIST, D)
        m["clf"] = f(i["cache_logf"][0, ss]); m["sconv"] = f(i["state_conv"][0, ss]); m["sh"] = f(i["state_rglru"][0, ss])
        in_maps.append(m)
    res = run_bass_kernel_spmd(nc, in_maps, core_ids=list(range(n_cores)), trace=trace)
    R = res.results
    cat = lambda name: np.concatenate([np.asarray(r[name]) for r in R], axis=0)
    BP = n_cores * NPS; BS = n_cores * NSS
    out = (
        cat("yp"), cat("ys"),
        cat("kp").reshape(1, BP, T, NH, DH), cat("vp").reshape(1, BP, T, NH, DH), cat("lp").reshape(1, BP, T, NH),
        cat("cp").reshape(1, BP, 3, D), cat("hp").reshape(1, BP, D),
        cat("ks").reshape(1, BS, TS, NH, DH), cat("vs").reshape(1, BS, TS, NH, DH), cat("ls").reshape(1, BS, TS, NH),
        cat("cs").reshape(1, BS, 3, D), cat("hs").reshape(1, BS, D),
    )
    return tuple(np.ascontiguousarray(o, dtype=np.float32) for o in out), res


def kernel(**inputs):
    out, _ = _run(inputs, 8, 4, 2048, 4, 64, 1024)
    return out
```
